# Optimizing a Trainium2 kernel written in Bass

```python
import math
import jax, jax.numpy as jnp
from jax import lax
import numpy as np

D_MODEL = 2048
BATCH = 8
SEQ = 2048
DEPTH = 2

HEAD_DIM = 64
BLOCK_Q = 128
EPS = 1e-6
NEG = -1e30
BIG = 1e4
N_BUCKETS = 32
MAX_DISTANCE = 128
A_HEADS = 16
A_KV_HEADS = 2
A_WINDOW = 128
B_HEADS = 16
B_KV_GROUPS = 2
CMP_BLOCK = 32
CMP_STRIDE = 16
CMP_HIDDEN = 256
SLC_BLOCK = 64
SLC_TOPK = 8
B_WINDOW = 512
N_GATES = 3
BIAS_HEADS = A_HEADS + B_HEADS
A_Q = A_HEADS * HEAD_DIM
A_KV = A_KV_HEADS * HEAD_DIM
B_Q = B_HEADS * HEAD_DIM
B_KV = B_KV_GROUPS * HEAD_DIM
EVEN_SPLITS = [A_Q, A_KV, A_KV, B_Q, B_KV, B_KV, B_KV, B_KV, B_KV, B_KV, B_HEADS * N_GATES]
EVEN_IN = sum(EVEN_SPLITS)
EVEN_OUT = A_Q + B_Q
C_HEADS = 16
Q_LORA = 768
KV_LORA = 512
NOPE_DIM = 128
ROPE_DIM = 64
V_DIM = 128
ROPE_THETA = 10000.0
ODD_IN = Q_LORA + KV_LORA + ROPE_DIM
ODD_OUT = C_HEADS * V_DIM
D_FF = 4 * D_MODEL

kernel_name = "hybrid_swa_nsa_mla_block"


def rmsnorm(x, g):
    xf = x.astype(jnp.float32)
    y = xf * lax.rsqrt(jnp.mean(xf * xf, axis=-1, keepdims=True) + EPS)
    return (y * g.astype(jnp.float32)).astype(x.dtype)


def t5_bucket(dist):
    dist = jnp.maximum(dist, 0)
    max_exact = N_BUCKETS // 2
    d = jnp.maximum(dist, 1).astype(jnp.float32)
    large = max_exact + (jnp.log(d / max_exact) / math.log(MAX_DISTANCE / max_exact)
                         * (N_BUCKETS - max_exact)).astype(jnp.int32)
    large = jnp.minimum(large, N_BUCKETS - 1)
    return jnp.where(dist < max_exact, dist, large)


def softmax_f32(logits):
    return jax.nn.softmax(logits.astype(jnp.float32), axis=-1)


def band_keys(k, n_prev):
    b, s, g, d = k.shape
    nb = s // BLOCK_Q
    kb = k.reshape(b, nb, BLOCK_Q, g, d)
    kp = jnp.pad(kb, ((0, 0), (n_prev, 0), (0, 0), (0, 0), (0, 0)))
    return jnp.concatenate([kp[:, j:j + nb] for j in range(n_prev + 1)], axis=2)


def banded_gqa(q, k, v, window, bias_table, sinks=None):
    b, s, h, d = q.shape
    g = k.shape[2]
    hpg = h // g
    nb = s // BLOCK_Q
    n_prev = (window - 1 + BLOCK_Q - 1) // BLOCK_Q
    kl = (n_prev + 1) * BLOCK_Q
    kb = band_keys(k, n_prev)
    vb = band_keys(v, n_prev)
    qb = q.reshape(b, nb, BLOCK_Q, g, hpg, d)
    logits = jnp.einsum('bnqgjd,bnkgd->bngjqk', qb, kb).astype(jnp.float32) * (d ** -0.5)
    qpos = jnp.arange(nb)[:, None] * BLOCK_Q + jnp.arange(BLOCK_Q)[None]
    kpos = (jnp.arange(nb)[:, None] - n_prev) * BLOCK_Q + jnp.arange(kl)[None]
    dist = qpos[:, :, None] - kpos[:, None, :]
    valid = (dist >= 0) & (dist < window) & (kpos[:, None, :] >= 0)
    bias = bias_table[t5_bucket(dist)].astype(jnp.float32)
    bias = bias.reshape(nb, BLOCK_Q, kl, g, hpg).transpose(0, 3, 4, 1, 2)
    logits = jnp.where(valid[:, None, None], logits + bias, NEG)
    if sinks is not None:
        sink = sinks.astype(jnp.float32).reshape(g, hpg)[None, None, :, :, None, None]
        sink = jnp.broadcast_to(sink, logits.shape[:-1] + (1,))
        p = softmax_f32(jnp.concatenate([logits, sink], axis=-1))[..., :-1]
    else:
        p = softmax_f32(logits)
    out = jnp.einsum('bngjqk,bnkgd->bnqgjd', p.astype(v.dtype), vb)
    return out.reshape(b, s, h, d)


def compress_blocks(k, pos_emb, w1, w2):
    b, s, g, d = k.shape
    nc = (s - CMP_BLOCK) // CMP_STRIDE + 1
    idx = jnp.arange(nc)[:, None] * CMP_STRIDE + jnp.arange(CMP_BLOCK)[None]
    blocks = k[:, idx] + pos_emb[:, None, :]
    blocks = blocks.transpose(0, 1, 3, 2, 4).reshape(b, nc, g, CMP_BLOCK * d)
    return jax.nn.gelu(blocks @ w1) @ w2


def nsa_attention(q, kc_raw, vc_raw, ks, vs, kw, vw, gates, bias_table,
                  cmp_pos_k, cmp_pos_v, cmp_k_w1, cmp_k_w2, cmp_v_w1, cmp_v_w2):
    b, s, h, d = q.shape
    g = kc_raw.shape[2]
    hpg = h // g
    scale = d ** -0.5
    tpos = jnp.arange(s)
    kc = compress_blocks(kc_raw, cmp_pos_k, cmp_k_w1, cmp_k_w2)
    vc = compress_blocks(vc_raw, cmp_pos_v, cmp_v_w1, cmp_v_w2)
    nc = kc.shape[1]
    qg = q.reshape(b, s, g, hpg, d)
    logits = jnp.einsum('bsgjd,bcgd->bgjsc', qg, kc).astype(jnp.float32) * scale
    cend = jnp.arange(nc) * CMP_STRIDE + CMP_BLOCK - 1
    cdist = tpos[:, None] - cend[None, :]
    cbias = bias_table[t5_bucket(cdist)].astype(jnp.float32)
    cbias = cbias.reshape(s, nc, g, hpg).transpose(2, 3, 0, 1)
    logits = jnp.where(cdist >= 0, logits + cbias, NEG)
    any_visible = (tpos >= CMP_BLOCK - 1).astype(jnp.float32)[:, None]
    p_cmp = softmax_f32(logits) * any_visible
    o_cmp = jnp.einsum('bgjsc,bcgd->bsgjd', p_cmp.astype(vc.dtype), vc).reshape(b, s, h, d)
    ns = s // SLC_BLOCK
    c_start = np.arange(nc) * CMP_STRIDE
    s_start = np.arange(ns) * SLC_BLOCK
    overlap = ((c_start[:, None] <= s_start[None] + SLC_BLOCK - 1) &
               (c_start[:, None] + CMP_BLOCK - 1 >= s_start[None])).astype(np.float32)
    p_slc = jnp.einsum('bgsc,cn->bgsn', p_cmp.sum(axis=2), jnp.asarray(overlap))
    blk = jnp.arange(ns)[None, :]
    cur = tpos[:, None] // SLC_BLOCK
    forced = (blk == 0) | (blk == cur) | (blk == cur - 1)
    future = blk * SLC_BLOCK > tpos[:, None]
    score = jnp.where(future, NEG, jnp.where(forced, BIG, p_slc))
    k_eff = min(SLC_TOPK, ns)
    _, sel = lax.top_k(score, k_eff)
    kblk = ks.reshape(b, ns, SLC_BLOCK, g, d).transpose(0, 3, 1, 2, 4)
    vblk = vs.reshape(b, ns, SLC_BLOCK, g, d).transpose(0, 3, 1, 2, 4)
    bi = jnp.arange(b)[:, None, None, None]
    gi = jnp.arange(g)[None, :, None, None]
    table_g = bias_table.reshape(N_BUCKETS, g, hpg)
    nb = s // BLOCK_Q

    def slc_chunk(args):
        qc, selc, qpos = args
        kg = kblk[bi, gi, selc]
        vg = vblk[bi, gi, selc]
        lg = jnp.einsum('bqgjd,bgqkld->bgjqkl', qc, kg).astype(jnp.float32) * scale
        kpos = selc[..., None] * SLC_BLOCK + jnp.arange(SLC_BLOCK)
        dist = qpos[None, None, :, None, None] - kpos
        bias = table_g[t5_bucket(dist), gi[..., None]].astype(jnp.float32)
        bias = bias.transpose(0, 1, 5, 2, 3, 4)
        lg = jnp.where((dist >= 0)[:, :, None], lg + bias, NEG)
        shp = lg.shape
        p = softmax_f32(lg.reshape(shp[:4] + (shp[4] * shp[5],))).reshape(shp)
        return jnp.einsum('bgjqkl,bgqkld->bqgjd', p.astype(vg.dtype), vg)

    q_chunks = qg.reshape(b, nb, BLOCK_Q, g, hpg, d).transpose(1, 0, 2, 3, 4, 5)
    sel_chunks = sel.reshape(b, g, nb, BLOCK_Q, k_eff).transpose(2, 0, 1, 3, 4)
    pos_chunks = tpos.reshape(nb, BLOCK_Q)
    o_slc = lax.map(slc_chunk, (q_chunks, sel_chunks, pos_chunks))
    o_slc = o_slc.transpose(1, 0, 2, 3, 4, 5).reshape(b, s, h, d)
    o_win = banded_gqa(q, kw, vw, B_WINDOW, bias_table)
    gt = jax.nn.sigmoid(gates.astype(jnp.float32)).astype(q.dtype)
    return gt[..., 0:1] * o_cmp + gt[..., 1:2] * o_slc + gt[..., 2:3] * o_win


def even_mixer(h, w_in, sinks, rel_bias, cmp_pos_k, cmp_pos_v, cmp_k_w1, cmp_k_w2,
               cmp_v_w1, cmp_v_w2, w_out):
    b, s, _ = h.shape
    proj = h @ w_in
    cuts = [int(c) for c in np.cumsum(EVEN_SPLITS)[:-1]]
    qa, ka, va, qb, kcb, vcb, ksb, vsb, kwb, vwb, gate = jnp.split(proj, cuts, axis=-1)
    heads = lambda t, n: t.reshape(b, s, n, HEAD_DIM)
    o_a = banded_gqa(heads(qa, A_HEADS), heads(ka, A_KV_HEADS), heads(va, A_KV_HEADS),
                     A_WINDOW, rel_bias[:, :A_HEADS], sinks)
    o_b = nsa_attention(heads(qb, B_HEADS), heads(kcb, B_KV_GROUPS), heads(vcb, B_KV_GROUPS),
                        heads(ksb, B_KV_GROUPS), heads(vsb, B_KV_GROUPS),
                        heads(kwb, B_KV_GROUPS), heads(vwb, B_KV_GROUPS),
                        gate.reshape(b, s, B_HEADS, N_GATES), rel_bias[:, A_HEADS:],
                        cmp_pos_k, cmp_pos_v, cmp_k_w1, cmp_k_w2, cmp_v_w1, cmp_v_w2)
    o = jnp.concatenate([o_a.reshape(b, s, A_Q), o_b.reshape(b, s, B_Q)], axis=-1)
    return o @ w_out


def apply_rope(x, cos, sin):
    half = x.shape[-1] // 2
    x1, x2 = x[..., :half], x[..., half:]
    return jnp.concatenate([x1 * cos - x2 * sin, x2 * cos + x1 * sin], axis=-1).astype(x.dtype)


def odd_mixer(h, w_in, q_norm, w_q_up, kv_norm, w_kv_up, w_out):
    b, s, _ = h.shape
    proj = h @ w_in
    cq, ckv, k_rope = jnp.split(proj, [Q_LORA, Q_LORA + KV_LORA], axis=-1)
    q = (rmsnorm(cq, q_norm) @ w_q_up).reshape(b, s, C_HEADS, NOPE_DIM + ROPE_DIM)
    q_nope, q_rope = q[..., :NOPE_DIM], q[..., NOPE_DIM:]
    kv = (rmsnorm(ckv, kv_norm) @ w_kv_up).reshape(b, s, C_HEADS, NOPE_DIM + V_DIM)
    k_nope, v = kv[..., :NOPE_DIM], kv[..., NOPE_DIM:]
    inv = 1.0 / (ROPE_THETA ** (jnp.arange(0, ROPE_DIM, 2, dtype=jnp.float32) / ROPE_DIM))
    ang = jnp.arange(s, dtype=jnp.float32)[:, None] * inv[None]
    cos, sin = jnp.cos(ang), jnp.sin(ang)
    q_rope = apply_rope(q_rope, cos[None, :, None, :], sin[None, :, None, :])
    k_rope = apply_rope(k_rope, cos[None], sin[None])
    scale = (NOPE_DIM + ROPE_DIM) ** -0.5
    kpos = jnp.arange(s)
    nb = s // BLOCK_Q

    def att_chunk(args):
        qn, qr, qpos = args
        lg = (jnp.einsum('bqhd,bkhd->bhqk', qn, k_nope) +
              jnp.einsum('bqhr,bkr->bhqk', qr, k_rope)).astype(jnp.float32) * scale
        lg = jnp.where(kpos[None, :] <= qpos[:, None], lg, NEG)
        p = softmax_f32(lg)
        return jnp.einsum('bhqk,bkhd->bqhd', p.astype(v.dtype), v)

    qn_c = q_nope.reshape(b, nb, BLOCK_Q, C_HEADS, NOPE_DIM).transpose(1, 0, 2, 3, 4)
    qr_c = q_rope.reshape(b, nb, BLOCK_Q, C_HEADS, ROPE_DIM).transpose(1, 0, 2, 3, 4)
    o = lax.map(att_chunk, (qn_c, qr_c, kpos.reshape(nb, BLOCK_Q)))
    o = o.transpose(1, 0, 2, 3, 4).reshape(b, s, ODD_OUT)
    return o @ w_out


def sqrelu_mlp(h, w_up, w_down):
    return jnp.square(jax.nn.relu(h @ w_up)) @ w_down


def setup_inputs(seed: int = 0) -> dict:
    key = jax.random.key(seed)
    ks = iter(jax.random.split(key, 40))
    ne = (DEPTH + 1) // 2
    no = DEPTH // 2
    f32 = jnp.float32

    def nrm(shape, fan_in):
        return jax.random.normal(next(ks), shape, f32) * (fan_in ** -0.5)

    def gain(shape):
        return 1.0 + 0.02 * jax.random.normal(next(ks), shape, f32)

    def small(shape, s):
        return s * jax.random.normal(next(ks), shape, f32)

    return {
        "x": jax.random.normal(next(ks), (BATCH, SEQ, D_MODEL), f32),
        "rel_bias": small((N_BUCKETS, BIAS_HEADS), 0.5),
        "norm_mix_e": gain((ne, D_MODEL)),
        "w_in_e": nrm((ne, D_MODEL, EVEN_IN), D_MODEL),
        "sinks": small((ne, A_HEADS), 0.5),
        "cmp_pos_k": small((ne, CMP_BLOCK, HEAD_DIM), 0.1),
        "cmp_pos_v": small((ne, CMP_BLOCK, HEAD_DIM), 0.1),
        "cmp_k_w1": nrm((ne, CMP_BLOCK * HEAD_DIM, CMP_HIDDEN), CMP_BLOCK * HEAD_DIM),
        "cmp_k_w2": nrm((ne, CMP_HIDDEN, HEAD_DIM), CMP_HIDDEN),
        "cmp_v_w1": nrm((ne, CMP_BLOCK * HEAD_DIM, CMP_HIDDEN), CMP_BLOCK * HEAD_DIM),
        "cmp_v_w2": nrm((ne, CMP_HIDDEN, HEAD_DIM), CMP_HIDDEN),
        "w_out_e": nrm((ne, EVEN_OUT, D_MODEL), EVEN_OUT),
        "norm_mix_o": gain((no, D_MODEL)),
        "w_in_o": nrm((no, D_MODEL, ODD_IN), D_MODEL),
        "q_norm": gain((no, Q_LORA)),
        "w_q_up": nrm((no, Q_LORA, C_HEADS * (NOPE_DIM + ROPE_DIM)), Q_LORA),
        "kv_norm": gain((no, KV_LORA)),
        "w_kv_up": nrm((no, KV_LORA, C_HEADS * (NOPE_DIM + V_DIM)), KV_LORA),
        "w_out_o": nrm((no, ODD_OUT, D_MODEL), ODD_OUT),
        "norm_mlp": gain((DEPTH, D_MODEL)),
        "w_up": nrm((DEPTH, D_MODEL, D_FF), D_MODEL),
        "w_down": nrm((DEPTH, D_FF, D_MODEL), D_FF),
        "norm_final": gain((D_MODEL,)),
    }


def reference(x, rel_bias, norm_mix_e, w_in_e, sinks, cmp_pos_k, cmp_pos_v, cmp_k_w1,
              cmp_k_w2, cmp_v_w1, cmp_v_w2, w_out_e, norm_mix_o, w_in_o, q_norm, w_q_up,
              kv_norm, w_kv_up, w_out_o, norm_mlp, w_up, w_down, norm_final):
    for layer in range(DEPTH):
        i = layer // 2
        if layer % 2 == 0:
            h = rmsnorm(x, norm_mix_e[i])
            x = x + even_mixer(h, w_in_e[i], sinks[i], rel_bias, cmp_pos_k[i], cmp_pos_v[i],
                               cmp_k_w1[i], cmp_k_w2[i], cmp_v_w1[i], cmp_v_w2[i], w_out_e[i])
        else:
            h = rmsnorm(x, norm_mix_o[i])
            x = x + odd_mixer(h, w_in_o[i], q_norm[i], w_q_up[i], kv_norm[i], w_kv_up[i],
                              w_out_o[i])
        x = x + sqrelu_mlp(rmsnorm(x, norm_mlp[layer]), w_up[layer], w_down[layer])
    return rmsnorm(x, norm_final)
```

```python
import math
import numpy as np
import concourse.bass as bass
import concourse.mybir as mybir
from concourse.bass_utils import run_bass_kernel_spmd

F32 = mybir.dt.float32
BF16 = mybir.dt.bfloat16
AF = mybir.ActivationFunctionType
ALU = mybir.AluOpType
AX = mybir.AxisListType

S = 2048
D = 2048
NT = 16
DFF = 8192
NEGM = -30000.0
EPS = 1e-6
NCORES = 8


class Buf:
    __slots__ = ("name", "t", "last_w", "readers", "off", "size", "psum")

    def __init__(self, name, t=None):
        self.psum = False
        self.name = name
        self.t = t
        self.last_w = None
        self.readers = []
        self.off = None
        self.size = 0

    def __getitem__(self, k):
        return self.t[k]


class Prog:
    ENGS = ("pe", "act", "dve", "pool", "sp")
    SB_LO = 16512
    SB_HI = 229344

    def __init__(self, nc):
        self.nc = nc
        self.eng = {"pe": nc.tensor, "act": nc.scalar, "dve": nc.vector,
                    "pool": nc.gpsimd, "sp": nc.sync}
        self.ops = []
        self.top = self.SB_LO
        self.allocs = []
        self.uid = 0

    def sb(self, name, shape, dtype):
        esz = 2 if dtype == BF16 else 4
        n = 1
        for s_ in shape[1:]:
            n *= s_
        size = (n * esz + 63) // 64 * 64
        off = self.top
        assert off + size <= self.SB_HI, f"SBUF overflow allocating {name}: {off}+{size}"
        self.top = off + size
        self.uid += 1
        t = self.nc.alloc_sbuf_tensor_at(f"{name}_{self.uid}", list(shape), dtype, offset=off)
        b = Buf(f"{name}_{self.uid}", t)
        b.off, b.size = off, size
        inh = set()
        for (o2, s2, b2) in self.allocs:
            if o2 < off + size and off < o2 + s2:
                if b2.last_w is not None:
                    inh.add(b2.last_w)
                inh.update(b2.readers)
        b.readers = list(inh)
        self.allocs.append((off, size, b))
        return b

    def mark(self):
        return self.top

    def release(self, m):
        self.top = m

    def ps(self, name, shape, dtype=F32):
        b = Buf(name, self.nc.alloc_psum_tensor(name, list(shape), dtype))
        b.psum = True
        return b

    def tok(self, name):
        return Buf(name, None)

    def op(self, engine, fn, reads=(), writes=(), dma=None, extra=()):
        oid = len(self.ops)
        deps = set(extra)
        for b in reads:
            if b.last_w is not None:
                deps.add(b.last_w)
            if b.psum:
                deps.update(r for r in b.readers if self.ops[r][0] != engine)
        for b in writes:
            if b.last_w is not None:
                deps.add(b.last_w)
            deps.update(b.readers)
        for b in writes:
            b.last_w = oid
            b.readers = []
        for b in reads:
            if b not in writes:
                if dma is None:
                    b.readers = [r for r in b.readers if not (self.ops[r][0] == engine and self.ops[r][3] is None)]
                b.readers.append(oid)
        self.ops.append((engine, fn, deps, dma))
        return oid

    def dma(self, engine, out_ap, in_ap, reads, writes, chan, **kw):
        return self.op(engine, lambda e: e.dma_start(out=out_ap, in_=in_ap, **kw), reads, writes, dma=chan)

    def mm(self, W, out, R, lhsT, rhs, start, stop, skip=False):
        return self.op("pe", lambda e: e.matmul(out, lhsT=lhsT, rhs=rhs, start=start, stop=stop, skip_group_check=skip), R, W)

    def tr(self, W, out, R, in_, ident):
        return self.op("pe", lambda e: e.transpose(out=out, in_=in_, identity=ident), R, W)

    def act(self, W, out, R, in_, func, **kw):
        return self.op("act", lambda e: e.activation(out=out, in_=in_, func=func, **kw), R, W)

    def tt(self, eng, W, out, R, in0, in1, op):
        return self.op(eng, lambda e: e.tensor_tensor(out=out, in0=in0, in1=in1, op=op), R, W)

    def ts(self, eng, W, out, R, in0, s1, s2, op0, op1=None):
        if op1 is None:
            return self.op(eng, lambda e: e.tensor_scalar(out=out, in0=in0, scalar1=s1, scalar2=None, op0=op0), R, W)
        return self.op(eng, lambda e: e.tensor_scalar(out=out, in0=in0, scalar1=s1, scalar2=s2, op0=op0, op1=op1), R, W)

    def stt(self, W, out, R, in0, scalar, in1, op0, op1):
        return self.op("dve", lambda e: e.scalar_tensor_tensor(out=out, in0=in0, scalar=scalar, in1=in1, op0=op0, op1=op1), R, W)

    def copy(self, eng, W, out, R, in_):
        if eng == "act":
            return self.op("act", lambda e: e.activation(out=out, in_=in_, func=AF.Copy), R, W)
        return self.op(eng, lambda e: e.tensor_copy(out=out, in_=in_), R, W)

    def recip(self, W, out, R, in_):
        return self.op("dve", lambda e: e.reciprocal(out=out, in_=in_), R, W)

    def memset(self, eng, W, out, val):
        return self.op(eng, lambda e: e.memset(out, val), (), W)

    def emit(self):
        nc = self.nc
        ops = self.ops
        n = len(ops)

        def skip(e, dma, d):
            return e == "pe" and dma is None and ops[d][0] == "pe" and ops[d][3] is None

        needed = [False] * n
        for (e, fn, deps, dma) in ops:
            for d in deps:
                if not skip(e, dma, d):
                    needed[d] = True
        sems = {"e_" + e: nc.alloc_semaphore(name=f"sem_{e}") for e in self.ENGS}
        ecount = {e: 0 for e in self.ENGS}
        chan_count = {}
        event = [None] * n
        waited = {e: {} for e in self.ENGS}
        nwaits = 0
        for i, (e, fn, deps, dma) in enumerate(ops):
            eng = self.eng[e]
            req = {}
            for d in deps:
                if skip(e, dma, d):
                    continue
                k, v = event[d]
                if k in chan_count:
                    v = chan_count[k]
                if req.get(k, 0) < v:
                    req[k] = v
            for k, v in req.items():
                if waited[e].get(k, 0) >= v:
                    continue
                eng.wait_ge(sems[k], v)
                waited[e][k] = v
                nwaits += 1
            ins = fn(eng)
            if dma is not None:
                key = "c_" + dma.name + "_" + e
                if key not in sems:
                    sems[key] = nc.alloc_semaphore(name="sem_" + key)
                    chan_count[key] = 0
                chan_count[key] += 16
                ins.then_inc(sems[key], 16)
                event[i] = (key, chan_count[key])
            elif needed[i]:
                ecount[e] += 1
                ins.then_inc(sems["e_" + e], 1)
                event[i] = ("e_" + e, ecount[e])
            else:
                event[i] = ("e_" + e, ecount[e] + 1)
        self.stats = dict(n_ops=n, n_waits=nwaits, counts=dict(ecount), n_sems=len(sems))


def _t5_bucket(dist):
    dist = np.maximum(dist, 0)
    d = np.maximum(dist, 1).astype(np.float32)
    large = 16 + (np.log(d / np.float32(16)) / np.float32(math.log(128 / 16)) * np.float32(16)).astype(np.int32)
    large = np.minimum(large, 31)
    return np.where(dist < 16, dist, large).astype(np.int64)


def _host_tables(rel_bias):
    rb = np.concatenate([rel_bias.astype(np.float32), np.full((1, 32), NEGM, np.float32)], axis=0)
    k = np.arange(128)[:, None]
    q = np.arange(128)[None, :]

    def tile(dist, valid, heads):
        idx = np.where(valid, _t5_bucket(dist), 32)
        return rb[idx][:, :, heads].transpose(0, 2, 1)

    hA = np.arange(0, 16)
    hB = np.arange(16, 32)
    d0 = q - k
    d1 = q - k + 128
    d4 = q - k + 512
    ones = np.ones((128, 128), bool)
    biasA = np.stack([tile(d0, (d0 >= 0) & (d0 < 128), hA), tile(d1, (d1 >= 0) & (d1 < 128), hA)])
    biasB = np.stack([tile(d0, d0 >= 0, hB), tile(d1, ones, hB), tile(np.full((128, 128), 1000), ones, hB),
                      tile(d4, d4 < 512, hB)])
    cc = (np.arange(248) - 120)[:, None]
    tq = np.arange(128)[None, :]
    dc = tq - 16 * cc - 31
    idx = np.where(dc >= 0, _t5_bucket(dc), 32)
    cbiasU = rb[idx][:, :, hB].transpose(0, 2, 1)
    cfar = np.zeros((16, 32, 16), np.float32)
    for i in range(16):
        for n in range(32):
            if n < 2 * i - 2:
                cfar[i, n, :] = rel_bias[31, 16:32]
    t = np.arange(S)[:, None]
    n = np.arange(32)[None, :]
    cur = t // 64
    future = n * 64 > t
    forced = (n == 0) | (n == cur) | (n == cur - 1)
    keep = np.where(future | forced, 0.0, 1.0).astype(np.float32)
    add = np.where(future, -1e30, np.where(forced, 1e4, 0.0)).astype(np.float32)
    keepadd = np.stack([keep, add], axis=1).reshape(16, 128, 2, 32).transpose(1, 0, 2, 3)
    c_start = np.arange(127) * 16
    s_start = np.arange(32) * 64
    overlap = ((c_start[:, None] <= s_start[None] + 63) & (c_start[:, None] + 31 >= s_start[None])).astype(np.float32)
    bind = (np.arange(S)[None, :] // 64 == np.arange(32)[:, None]).astype(np.float32)
    inv = 1.0 / (10000.0 ** (np.arange(0, 64, 2, dtype=np.float32) / 64))
    ang = np.arange(S, dtype=np.float32)[:, None] * inv[None].astype(np.float32)
    cos = np.cos(ang.astype(np.float32)).astype(np.float32).T
    sin = np.sin(ang.astype(np.float32)).astype(np.float32).T
    cs = np.stack([np.concatenate([cos, cos], 0), np.concatenate([sin, sin], 0)], axis=1)
    mlamask = np.where(k <= q, 0.0, NEGM).astype(np.float32)
    return dict(biasA=np.ascontiguousarray(biasA.transpose(1, 0, 2, 3)),
                biasB=np.ascontiguousarray(biasB.transpose(1, 0, 2, 3)),
                cbiasU=np.ascontiguousarray(cbiasU), cfar=np.ascontiguousarray(cfar.transpose(1, 0, 2)),
                keepadd=np.ascontiguousarray(keepadd), overlap=overlap, bind=bind,
                cs=np.ascontiguousarray(cs), mlamask=mlamask, ident=np.eye(128, dtype=np.float32))


INPUT_SHAPES = dict(
    x=[S, D], norm_mix_e=[1, D], w_in_e=[D, 3120], sinks=[1, 16], cmp_pos_k=[32, 64], cmp_pos_v=[32, 64],
    cmp_k_w1=[2048, 256], cmp_k_w2=[256, 64], cmp_v_w1=[2048, 256], cmp_v_w2=[256, 64], w_out_e=[D, D],
    norm_mix_o=[1, D], w_in_o=[D, 1344], q_norm=[1, 768], w_q_up=[768, 3072], kv_norm=[1, 512],
    w_kv_up=[512, 4096], w_out_o=[D, D], norm_mlp=[2, D], w_up0=[D, DFF], w_up1=[D, DFF],
    w_down0=[DFF, D], w_down1=[DFF, D], norm_final=[1, D],
    biasA=[128, 2, 16, 128], biasB=[128, 4, 16, 128], cbiasU=[248, 16, 128], cfar=[32, 16, 16],
    keepadd=[128, 16, 2, 32], overlap=[127, 32], bind=[32, S], cs=[64, 2, S], mlamask=[128, 128], ident=[128, 128],
)


class Ctx:
    pass


class LazyInputs:
    def __init__(self, nc):
        self.nc = nc
        self.d = {}

    def __getitem__(self, k):
        if k not in self.d:
            self.d[k] = self.nc.dram_tensor(k, INPUT_SHAPES[k], F32, kind="ExternalInput").ap()
        return self.d[k]


def build_program(stages=("l0mix", "l0mlp", "l1mix", "l1mlp"), dbg=(), cut=None):
    nc = bass.Bass("TRN2", target_bir_lowering=False)
    P = Prog(nc)
    C = Ctx()
    C.nc, C.P = nc, P
    I = LazyInputs(nc)
    C.I = I
    C.out = nc.dram_tensor("out", [S, D], F32, kind="ExternalOutput").ap()
    C.xs = nc.dram_tensor("xs", [S, D], F32, kind="Internal").ap()
    C.xsB = [P.tok(f"xs{t}") for t in range(NT)]
    C.out_ops = []
    C.dbg = {}
    C.dbg_want = dbg
    C.cut = cut
    C.B = [P.ps(f"pb{i}", [128, 512], F32) for i in range(2)]
    C.O = [P.ps(f"po{i}", [128, 8, 128], F32) for i in range(2)]
    C.Of = [o.t[:, :, :].rearrange("p h c -> p (h c)") for o in C.O]
    C.TR = [P.ps(f"ptr{i}", [128, 1024], BF16) for i in range(2)]
    C.TRap = [t.t[:, 0:512] for t in C.TR]
    C.ident = P.sb("ident", [128, 128], BF16)
    P.dma("pool", C.ident[:], I["ident"], [], [C.ident], C.ident)
    C.ones = P.sb("ones", [128, 1], BF16)
    P.memset("dve", [C.ones], C.ones[:], 1.0)
    C.neghalf = P.sb("neghalf", [128, 2], F32)
    P.memset("pool", [C.neghalf], C.neghalf[:], -0.5)
    C.x_src = I["x"]
    C.x_srcB = None

    if "l0mix" in stages:
        layer0_mixer(C)
    if "l0mlp" in stages:
        mlp(C, 0, final=False)
    if "l1mix" in stages:
        layer1_mixer(C)
    if "l1mlp" in stages:
        mlp(C, 1, final=True)
    if "xs" in dbg:
        d = nc.dram_tensor("dbg_xs", [S, D], F32, kind="ExternalOutput").ap()
        db = P.tok("dbgxs")
        for t in range(NT):
            C.out_ops.append(P.dma("sp", d[t * 128:(t + 1) * 128, :], C.xs[t * 128:(t + 1) * 128, :], [C.xsB[t]], [], db))
    P.op("sp", lambda e: e.nop(), extra=C.out_ops)
    P.emit()
    P.used_inputs = list(I.d.keys())
    return nc, P


def bcast_row(ap_row, n=128):
    return ap_row.rearrange("o n -> (o n)").partition_broadcast(n)


def load_x_tile(C, dst, t):
    P = C.P
    reads = [] if C.x_srcB is None else [C.x_srcB[t]]
    P.dma("sp", dst[:], C.x_src[t * 128:(t + 1) * 128, :], reads, [dst], dst)


def norm_rows(C, src_ap, srcB, n, gbc, out_ap, outB, tmp):
    P = C.P
    junk, ssq, sd, rstd = tmp
    P.act([junk, ssq], junk[:, 0:n], [srcB], src_ap, AF.Square, accum_out=ssq[:, 0:1])
    P.ts("dve", [sd], sd[:, 0:1], [ssq], ssq[:, 0:1], 1.0 / n, EPS, ALU.mult, ALU.add)
    P.tt("pool", [rstd], rstd[:, 0:1], [sd, C.neghalf], sd[:, 0:1], C.neghalf[:, 0:1], ALU.pow)
    P.stt([outB], out_ap, [srcB, rstd, gbc], src_ap, rstd[:, 0:1], gbc[:, 0:n], ALU.mult, ALU.mult)


def norm_transpose(C, tiles, gbc, hT, col0, xst, hb, tmp):
    P = C.P
    for j, t in enumerate(tiles):
        xa = xst[j % 2]
        load_x_tile(C, xa, t)
        norm_rows(C, xa[:], xa, D, gbc, hb[:], hb, tmp)
        for q4 in range(4):
            tb = C.TR[q4 % 2]
            tap = C.TRap[q4 % 2]
            for k in range(4):
                kc = q4 * 4 + k
                P.tr([tb], tap[:, k * 128:(k + 1) * 128], [hb, C.ident], hb[:, kc * 128:(kc + 1) * 128], C.ident[:])
            dst = hT[:, q4 * 4:q4 * 4 + 4, col0 + j * 128:col0 + (j + 1) * 128]
            P.copy("act" if q4 % 2 == 0 else "dve", [hT], dst, [tb], tap.rearrange("p (a b) -> p a b", a=4))


def load_w_cast(C, dst, dst_ap, src_ap):
    C.P.dma("pool", dst_ap, src_ap, [], [dst], dst)


def layer0_mixer(C):
    P, I = C.P, C.I
    B = C.B
    m_persist = P.mark()
    kaT = P.sb("kaT", [64, 2, S], BF16)
    kwT = P.sb("kwT", [64, 2, S], BF16)
    ksA = P.sb("ksA", [96, 2, S], BF16)
    va = P.sb("va", [128, NT, 2, 65], BF16)
    vs = P.sb("vs", [128, NT, 2, 65], BF16)
    vw = P.sb("vw", [128, NT, 2, 65], BF16)
    kcT = P.sb("kcT", [64, 2, 128], BF16)
    vcA = P.sb("vcA", [128, 2, 97], BF16)
    gates = P.sb("gates", [128, NT, 48], F32)
    sinkexp = P.sb("sinkexp", [128, 16], F32)
    qsa = C.nc.dram_tensor("qsa", [1024, S], BF16, kind="Internal").ap()
    qsb = C.nc.dram_tensor("qsb", [1024, S], BF16, kind="Internal").ap()
    qsB = {(w, c, tb): P.tok(f"qs{w}{c}_{tb}") for w in "ab" for c in range(8) for tb in range(4)}
    for vt in (va, vs, vw):
        P.memset("pool", [vt], vt[:, :, :, 64:65], 1.0)
    P.memset("pool", [vcA], vcA[:], 0.0)
    P.memset("pool", [kcT], kcT[:], 0.0)
    P.memset("pool", [vcA], vcA[:, :, 64:65], 1.0)
    for g in range(2):
        P.dma("pool", ksA[64:96, g, :], I["bind"], [], [ksA], ksA)
        P.dma("pool", vcA[0:127, g, 65:97], I["overlap"], [], [vcA], vcA)
    P.dma("sp", sinkexp[:], bcast_row(I["sinks"]), [], [sinkexp], sinkexp)
    P.act([sinkexp], sinkexp[:], [sinkexp], sinkexp[:], AF.Exp)

    m_raw = P.mark()
    rawk = P.sb("rawk", [64, 2, S], BF16)
    rawv = P.sb("rawv", [64, 2, S], BF16)
    m0 = P.mark()
    hT = P.sb("hT", [128, 16, S], BF16)
    gbc = P.sb("gbc", [128, D], F32)
    P.dma("sp", gbc[:], bcast_row(I["norm_mix_e"]), [], [gbc], gbc)
    xst = [P.sb(f"xst{i}", [128, D], F32) for i in range(2)]
    hb = P.sb("hb", [128, D], BF16)
    tmp = (P.sb("junk", [128, D], BF16), P.sb("ssq", [128, 1], F32), P.sb("sd", [128, 2], F32), P.sb("rstd", [128, 1], F32))
    WKV = P.sb("WKV", [128, 16, 1072], BF16)
    wv = I["w_in_e"].rearrange("(kc p) c -> p kc c", p=128)
    load_w_cast(C, WKV, WKV[:, :, 0:256], wv[:, :, 1024:1280])
    load_w_cast(C, WKV, WKV[:, :, 256:1072], wv[:, :, 2304:3120])
    WQ = [P.sb(f"WQ{i}", [128, 16, 256], BF16) for i in range(2)]
    qblocks = [(which, col0, blk) for (which, col0) in (("a", 0), ("b", 1280)) for blk in range(4)]

    def load_q(n):
        which, col0, blk = qblocks[n]
        load_w_cast(C, WQ[n % 2], WQ[n % 2][:], wv[:, :, col0 + blk * 256:col0 + (blk + 1) * 256])
    load_q(0)
    load_q(1)
    qstage = [P.sb(f"qst{i}", [128, 512], BF16) for i in range(3)]
    norm_transpose(C, range(NT), gbc, hT, 0, xst, hb, tmp)
    if C.cut == "a":
        return
    kdst = [(kaT, 0), (None, 256), (None, 384), (ksA, 512), (kwT, 768)]
    kdst[1] = (rawk, 256)
    kdst[2] = (rawv, 384)
    nb = 0
    for (dst, c0) in kdst:
        for g in range(2):
            for tb in range(4):
                pb = B[nb % 2]
                for kc in range(16):
                    P.mm([pb], pb[0:64, :], [WKV, hT], WKV[:, kc, c0 + g * 64:c0 + (g + 1) * 64],
                         hT[:, kc, tb * 512:(tb + 1) * 512], kc == 0, kc == 15)
                P.copy("act" if nb % 2 == 0 else "dve", [dst], dst[0:64, g, tb * 512:(tb + 1) * 512], [pb], pb[0:64, :])
                nb += 1
    if C.cut == "b":
        return
    for t in range(NT):
        pbB, pb = C.O[t % 2], C.Of[t % 2]
        for (o0, c0, w) in ((0, 128, 128), (128, 640, 128), (256, 896, 176)):
            for kc in range(16):
                P.mm([pbB], pb[:, o0:o0 + w], [WKV, hT], hT[:, kc, t * 128:(t + 1) * 128], WKV[:, kc, c0:c0 + w], kc == 0, kc == 15)
        import os
        VV = int(os.environ.get("VV", "9"))
        for vi, vt in enumerate((va, vs, vw)):
            if VV < 1 or (VV < 2 and vi == 1):
                continue
            P.copy("dve" if vi != 1 else "act", [vt], vt[:, t, :, 0:64], [pbB],
                   pb[:, vi * 128:(vi + 1) * 128].rearrange("p (g d) -> p g d", g=2))
        if VV >= 3:
            P.act([gates], gates[:, t, :], [pbB], pb[:, 384:432], AF.Tanh, scale=0.5)
            P.ts("pool", [gates], gates[:, t, :], [gates], gates[:, t, :], 0.5, 0.5, ALU.mult, ALU.add)
    if C.cut == "c":
        return
    nb = 0
    for n, (which, col0, blk) in enumerate(qblocks):
        qs = qsa if which == "a" else qsb
        W = WQ[n % 2]
        for c4 in range(2):
            c = blk * 2 + c4
            for tb in range(4):
                pb = B[nb % 2]
                for kc in range(16):
                    P.mm([pb], pb[:], [W, hT], W[:, kc, c4 * 128:(c4 + 1) * 128], hT[:, kc, tb * 512:(tb + 1) * 512], kc == 0, kc == 15)
                st = qstage[nb % 3]
                if nb % 2 == 0:
                    P.op("act", lambda e, o=st[:], a=pb[:]: e.mul(o, a, 0.125), [pb], [st])
                else:
                    P.ts("dve", [st], st[:], [pb], pb[:], 0.125, None, ALU.mult)
                P.dma("sp", qs[c * 128:(c + 1) * 128, tb * 512:(tb + 1) * 512], st[:], [st], [qsB[(which, c, tb)]], st)
                nb += 1
        if n + 2 < len(qblocks):
            load_q(n + 2)
    if C.cut == "d":
        return
    P.release(m0)
    hid = P.sb("hid", [128, 2, 128], BF16)
    zt = [P.sb(f"cz{i}", [128, 128], F32) for i in range(4)]
    pbias = P.sb("pbias", [128, 1], F32)
    for kv, (w1n, w2n, posn, raw) in enumerate((("cmp_k_w1", "cmp_k_w2", "cmp_pos_k", rawk), ("cmp_v_w1", "cmp_v_w2", "cmp_pos_v", rawv))):
        w1 = P.sb(f"cw1{kv}", [64, 32, 256], BF16)
        w2 = P.sb(f"cw2{kv}", [128, 2, 64], BF16)
        posT = P.sb(f"cpos{kv}", [64, 32], BF16)
        P.dma("pool", w1[:], I[w1n].rearrange("(l d) h -> d l h", d=64), [], [w1], w1)
        P.dma("pool", w2[:], I[w2n].rearrange("(c p) d -> p c d", p=128), [], [w2], w2)
        P.dma("pool", posT[:], I[posn].rearrange("l d -> d l"), [], [posT], posT, allow_slow_non_contiguous=True)
        for g in range(2):
            for hc in range(2):
                pm, pp = B[0], B[1]
                for l in range(32):
                    P.mm([pm], pm[:, 0:127], [w1, raw], w1[:, l, hc * 128:(hc + 1) * 128], raw[:, g, l:l + 16 * 126 + 1:16], l == 0, l == 31)
                for l in range(32):
                    P.mm([pp], pp[:, 0:1], [w1, posT], w1[:, l, hc * 128:(hc + 1) * 128], posT[:, l:l + 1], l == 0, l == 31)
                z, z2, u, sg = zt
                P.copy("dve", [pbias], pbias[:], [pp], pp[:, 0:1])
                P.ts("dve", [z], z[:, 0:127], [pm, pbias], pm[:, 0:127], pbias[:, 0:1], None, ALU.add)
                P.tt("dve", [z2], z2[:, 0:127], [z], z[:, 0:127], z[:, 0:127], ALU.mult)
                P.ts("dve", [z2], z2[:, 0:127], [z2], z2[:, 0:127], 0.044715, 1.0, ALU.mult, ALU.add)
                P.tt("dve", [u], u[:, 0:127], [z2, z], z2[:, 0:127], z[:, 0:127], ALU.mult)
                P.act([sg], sg[:, 0:127], [u], u[:, 0:127], AF.Tanh, scale=0.7978845608028654)
                P.stt([sg], sg[:, 0:127], [sg, z], sg[:, 0:127], 1.0, z[:, 0:127], ALU.add, ALU.mult)
                P.ts("dve", [hid], hid[:, hc, 0:127], [sg], sg[:, 0:127], 0.5, None, ALU.mult)
            if kv == 0:
                poB, po = C.O[0], C.Of[0]
                for hc in range(2):
                    P.mm([poB], po[0:64, 0:127], [w2, hid], w2[:, hc, :], hid[:, hc, 0:127], hc == 0, hc == 1)
                P.copy("dve", [kcT], kcT[:, g, 0:127], [poB], po[0:64, 0:127])
            else:
                poB, po = C.O[1], C.Of[1]
                for hc in range(2):
                    P.mm([poB], po[0:127, 0:64], [hid, w2], hid[:, hc, 0:127], w2[:, hc, :], hc == 0, hc == 1)
                P.copy("dve", [vcA], vcA[0:127, g, 0:64], [poB], po[0:127, 0:64])
    if "l0proj" in C.dbg_want:
        dbg_dump(C, "kaT", kaT, kaT[:], [64, 2, S], BF16)
        dbg_dump(C, "ksA", ksA, ksA[:], [96, 2, S], BF16)
        dbg_dump(C, "va", va, va[:], [128, NT, 2, 65], BF16)
        dbg_dump(C, "kcT", kcT, kcT[:], [64, 2, 128], BF16)
        dbg_dump(C, "vcA", vcA, vcA[:], [128, 2, 97], BF16)
        dbg_dump(C, "gates", gates, gates[:], [128, NT, 48], F32)
    P.release(m_raw)

    if C.cut == "e":
        return
    WO = P.sb("WO", [128, 16, D], BF16)
    wov = I["w_out_e"].rearrange("(kc p) c -> p kc c", p=128)
    for q4 in range(4):
        load_w_cast(C, WO, WO[:, q4 * 4:(q4 + 1) * 4, :], wov[:, q4 * 4:(q4 + 1) * 4, :])
    biasA = P.sb("biasA", [128, 2, 16, 128], BF16)
    biasB = P.sb("biasB", [128, 4, 16, 128], BF16)
    P.dma("pool", biasA[:], I["biasA"], [], [biasA], biasA)
    P.dma("pool", biasB[:], I["biasB"], [], [biasB], biasB)
    keepadd = P.sb("keepadd", [128, NT, 2, 32], F32)
    P.dma("sp", keepadd[:], I["keepadd"], [], [keepadd], keepadd)
    cfar = P.sb("cfar", [128, NT, 16], F32)
    P.dma("sp", cfar[64:96, :, :], I["cfar"], [], [cfar], cfar)
    cb = [P.sb(f"cb{i}", [128, 16, 128], BF16) for i in range(2)]
    qa = [P.sb(f"qa{i}", [64, 16, 128], BF16) for i in range(2)]
    qb = [P.sb(f"qb{i}", [96, 16, 128], BF16) for i in range(2)]
    PT = [P.sb(f"PT{i}", [128, 512], BF16) for i in range(3)]
    TM = P.sb("TM", [128, 128], BF16)
    P.memset("dve", [TM], TM[:], 0.0)
    ocat = P.sb("ocat", [128, D], BF16)
    oT = P.sb("oT", [128, 16, 128], BF16)
    xst = [P.sb(f"xat{i}", [128, D], F32) for i in range(2)]
    acc = P.sb("oacc", [128, 16, 64], F32)
    tmpo = P.sb("otmp", [128, 8, 64], F32)
    sm = {k: P.sb(f"sm_{k}", shp, F32) for k, shp in dict(z=[128, 8], rz=[128, 8], w=[128, 8], ps3=[128, 8, 32],
                                                          pslc=[128, 32], sc=[128, 32], m8=[128, 8], sel=[128, 32]).items()}
    ST = [B[0], B[1]]
    Ot = C.O
    osl = [0]

    def attn_group(items, PVrhs, nheads_cols, g, Ob):
        n = len(items)
        pend = None
        for idx, it in enumerate(items):
            stb = ST[st_ctr[0] % 2]
            st_ctr[0] += 1
            mml, (half, kt, nrows, first, last) = it
            for mi, (l, r, rb) in enumerate(mml):
                P.mm([stb], stb[0:nrows, :], rb, l, r, mi == 0, mi == len(mml) - 1)
            if pend is not None:
                finish(*pend)
            pend = (stb, half, kt, nrows, first, last, PVrhs, nheads_cols, g, Ob)
        if pend is not None:
            finish(*pend)

    def finish(stb, half, kt, nrows, first, last, PVrhs, ncols, g, Ob):
        pt = PT[pt_ctr[0] % 3]
        pt_ctr[0] += 1
        P.act([pt], pt[0:nrows, :], [stb], stb[0:nrows, :], AF.Exp)
        vt = PVrhs
        for hl in range(4):
            h8 = half * 4 + hl
            if vt is vcA:
                rhs = vcA[0:nrows, g, 0:ncols]
            else:
                rhs = vt[:, kt, g, 0:ncols]
            P.mm([Ob], Ob[:, h8, 0:ncols], [pt, vt], pt[0:nrows, hl * 128:(hl + 1) * 128], rhs, first and hl == 0, last and hl == 3, skip=True)

    st_ctr = [0]
    pt_ctr = [0]
    def load_tile_inputs(i):
        qat, qbt, cbt = qa[i % 2], qb[i % 2], cb[i % 2]
        tb = i // 4
        P.dma("sp", qat[:], qsa.rearrange("(h d) t -> d h t", d=64)[:, :, i * 128:(i + 1) * 128],
              [qsB[("a", c, tb)] for c in range(8)], [qat], qat)
        P.dma("sp", qbt[0:64, :, :], qsb.rearrange("(h d) t -> d h t", d=64)[:, :, i * 128:(i + 1) * 128],
              [qsB[("b", c, tb)] for c in range(8)], [qbt], qbt)
        P.dma("pool", cbt[:, :, :], I["cbiasU"][120 - 8 * i:248 - 8 * i, :, :], [], [cbt], cbt)

    def out_proj(i):
        xa = xst[i % 2]
        oc = ocats[i % 2]
        for q4 in range(4):
            trb, trap = C.TR[q4 % 2], C.TRap[q4 % 2]
            for k in range(4):
                kc = q4 * 4 + k
                P.tr([trb], trap[:, k * 128:(k + 1) * 128], [oc, C.ident], oc[:, kc * 128:(kc + 1) * 128], C.ident[:])
            P.copy("act" if q4 % 2 == 0 else "dve", [oT], oT[:, q4 * 4:q4 * 4 + 4, :], [trb], trap.rearrange("p (a b) -> p a b", a=4))
        for db in range(4):
            pb = ST[st_ctr[0] % 2]
            st_ctr[0] += 1
            for kc in range(16):
                P.mm([pb], pb[:], [oT, WO], oT[:, kc, :], WO[:, kc, db * 512:(db + 1) * 512], kc == 0, kc == 15)
            P.tt("dve", [xa], xa[:, db * 512:(db + 1) * 512], [pb, xa], pb[:], xa[:, db * 512:(db + 1) * 512], ALU.add)
        P.dma("sp", C.xs[i * 128:(i + 1) * 128, :], xa[:], [xa], [C.xsB[i]], xa)

    ocats = [ocat, P.sb("ocat2", [128, D], BF16)]
    load_tile_inputs(0)
    for i in range(NT):
        if C.cut is not None and C.cut.startswith("f") and i >= int(C.cut[1:]):
            return
        qat, qbt, cbt = qa[i % 2], qb[i % 2], cb[i % 2]
        ocat = ocats[i % 2]
        if i + 1 < NT:
            load_tile_inputs(i + 1)
        load_x_tile(C, xst[i % 2], i)
        xa = xst[i % 2]
        bo = 0
        for g in range(2):
            Ob = Ot[bo % 2]
            bo += 1
            items = []
            for half in range(2):
                hs = slice(g * 8 + half * 4, g * 8 + half * 4 + 4)
                mml = [(kcT[0:64, g, 0:127], qbt[0:64, hs, :], [kcT, qbt]),
                       (C.ident[0:127, 0:127], cbt[0:127, hs, :], [C.ident, cbt])]
                items.append((mml, (half, 0, 127, True, True)))
            attn_group(items, vcA, 97, g, Ob)
            z, rz, w = sm["z"], sm["rz"], sm["w"]
            P.ts("dve", [z], z[:], [Ob], Ob[:, :, 64], 1e-30, None, ALU.max)
            P.recip([rz], rz[:], [z], z[:])
            P.tt("dve", [w], w[:], [rz, gates], rz[:], gates[:, i, g * 24:(g + 1) * 24].rearrange("p (h k) -> p h k", k=3)[:, :, 0], ALU.mult)
            P.tt("dve", [acc], acc[:, g * 8:(g + 1) * 8, :], [Ob, w], Ob[:, :, 0:64], w[:].unsqueeze(2).to_broadcast([128, 8, 64]), ALU.mult)
            ps3 = sm["ps3"]
            P.tt("dve", [ps3], ps3[:], [Ob, rz], Ob[:, :, 65:97], rz[:].unsqueeze(2).to_broadcast([128, 8, 32]), ALU.mult)
            pslc, sc, m8, sel = sm["pslc"], sm["sc"], sm["m8"], sm["sel"]
            P.op("dve", lambda e, o=pslc[:], a=ps3[:].rearrange("p h n -> p n h"): e.tensor_reduce(out=o, in_=a, axis=AX.X, op=ALU.add), [ps3], [pslc])
            P.tt("dve", [sc], sc[:], [pslc, keepadd], pslc[:], keepadd[:, i, 0, :], ALU.mult)
            P.tt("dve", [sc], sc[:], [sc, keepadd], sc[:], keepadd[:, i, 1, :], ALU.add)
            P.op("dve", lambda e, o=m8[:], a=sc[:]: e.max(out=o, in_=a), [sc], [m8])
            P.ts("dve", [sel], sel[:], [sc, m8], sc[:], m8[:, 7:8], None, ALU.is_ge)
            P.ts("dve", [TM], TM[:, 64:96], [sel], sel[:], 1.0, -NEGM, ALU.subtract, ALU.mult)
            trb, trap = C.TR[g], C.TRap[g]
            P.tr([trb], trap[:, 0:128], [TM, C.ident], TM[:], C.ident[:])
            P.tt("dve", [qbt], qbt[64:96, g * 8:(g + 1) * 8, :], [trb, cfar],
                 trap[64:96, 0:128].unsqueeze(1).to_broadcast([32, 8, 128]),
                 cfar[64:96, i, g * 8:(g + 1) * 8].unsqueeze(2).to_broadcast([32, 8, 128]), ALU.add)
        for g in range(2):
            Ob = Ot[bo % 2]
            bo += 1
            kts = [kt for kt in (i - 1, i) if kt >= 0]
            items = []
            for half in range(2):
                hs = slice(g * 8 + half * 4, g * 8 + half * 4 + 4)
                for kt in kts:
                    kind = 0 if kt == i else 1
                    mml = [(kaT[0:64, g, kt * 128:(kt + 1) * 128], qat[0:64, hs, :], [kaT, qat]),
                           (C.ident[:], biasA[:, kind, hs, :], [C.ident, biasA])]
                    items.append((mml, (half, kt, 128, kt == kts[0], kt == kts[-1])))
            attn_group(items, va, 65, g, Ob)
            z, rz = sm["z"], sm["rz"]
            P.tt("dve", [z], z[:], [Ob, sinkexp], Ob[:, :, 64], sinkexp[:, g * 8:(g + 1) * 8], ALU.add)
            P.recip([rz], rz[:], [z], z[:])
            P.tt("dve", [ocat], ocat[:, g * 512:(g + 1) * 512].rearrange("p (h d) -> p h d", d=64), [Ob, rz], Ob[:, :, 0:64],
                 rz[:].unsqueeze(2).to_broadcast([128, 8, 64]), ALU.mult)
        if i > 0:
            out_proj(i - 1)
        for br in ("win", "slc"):
            for g in range(2):
                Ob = Ot[bo % 2]
                bo += 1
                items = []
                kts = list(range(max(0, i - 4), i + 1)) if br == "win" else list(range(0, i + 1))
                for half in range(2):
                    hs = slice(g * 8 + half * 4, g * 8 + half * 4 + 4)
                    for kt in kts:
                        dk = i - kt
                        if br == "win":
                            kind = {0: 0, 1: 1, 2: 2, 3: 2, 4: 3}[dk]
                            mml = [(kwT[0:64, g, kt * 128:(kt + 1) * 128], qbt[0:64, hs, :], [kwT, qbt]),
                                   (C.ident[:], biasB[:, kind, hs, :], [C.ident, biasB])]
                        else:
                            mml = [(ksA[0:96, g, kt * 128:(kt + 1) * 128], qbt[0:96, hs, :], [ksA, qbt])]
                            if dk <= 1:
                                mml.append((C.ident[:], biasB[:, dk, hs, :], [C.ident, biasB]))
                        items.append((mml, (half, kt, 128, kt == kts[0], kt == kts[-1])))
                attn_group(items, vw if br == "win" else vs, 65, g, Ob)
                rz, w = sm["rz"], sm["w"]
                P.recip([rz], rz[:], [Ob], Ob[:, :, 64])
                gi = 2 if br == "win" else 1
                P.tt("dve", [w], w[:], [rz, gates], rz[:], gates[:, i, g * 24:(g + 1) * 24].rearrange("p (h k) -> p h k", k=3)[:, :, gi], ALU.mult)
                P.tt("dve", [tmpo], tmpo[:], [Ob, w], Ob[:, :, 0:64], w[:].unsqueeze(2).to_broadcast([128, 8, 64]), ALU.mult)
                if br == "win":
                    P.tt("pool", [acc], acc[:, g * 8:(g + 1) * 8, :], [acc, tmpo], acc[:, g * 8:(g + 1) * 8, :], tmpo[:], ALU.add)
                else:
                    P.tt("pool", [ocat], ocat[:, 1024 + g * 512:1024 + (g + 1) * 512].rearrange("p (h d) -> p h d", d=64), [acc, tmpo],
                         acc[:, g * 8:(g + 1) * 8, :], tmpo[:], ALU.add)
        if "ocat" in C.dbg_want:
            dbg_dump(C, f"ocat{i}", ocat, ocat[:], [128, D], BF16)
    out_proj(NT - 1)
    C.x_src = C.xs
    C.x_srcB = C.xsB
    P.release(m_persist)


def dbg_dump(C, name, buf, ap, shape, dtype):
    P = C.P
    d = C.nc.dram_tensor("dbg_" + name, list(shape), dtype, kind="ExternalOutput").ap()
    C.dbg[name] = (shape, dtype)
    oid = P.dma("sp", d, ap, [buf], [], buf)
    C.out_ops.append(oid)


def mlp(C, layer, final):
    P, I, B = C.P, C.I, C.B
    m = P.mark()
    gbc = P.sb("gbcm", [128, D], F32)
    gfin = gbc
    TB = 8
    hT = P.sb("hTm", [128, 16, TB * 128], BF16)
    yacc = P.sb("yacc", [128, TB, D], F32)
    xst = [P.sb(f"xsm{i}", [128, D], F32) for i in range(2)]
    hb = P.sb("hbm", [128, D], BF16)
    tmp = (hb, P.sb("ssqm", [128, 1], F32), P.sb("sdm", [128, 2], F32), P.sb("rstdm", [128, 1], F32))
    wu = [P.sb(f"wu{i}", [128, 16, 512], BF16) for i in range(2)]
    wd = [P.sb(f"wd{i}", [128, 4, D], BF16) for i in range(2)]
    uT = [P.sb(f"uT{i}", [128, 4, 512], BF16) for i in range(2)]
    rl = [P.sb(f"rl{i}", [128, 512], F32) for i in range(2)]
    wuv = I[f"w_up{layer}"].rearrange("(kc p) f -> p kc f", p=128)
    wdv = I[f"w_down{layer}"].rearrange("(fc p) d -> p fc d", p=128)
    nu = 0
    no = 0
    NG = 16

    def load_group(gi):
        wub, wdb = wu[gi % 2], wd[gi % 2]
        load_w_cast(C, wub, wub[:], wuv[:, :, gi * 512:(gi + 1) * 512])
        load_w_cast(C, wdb, wdb[:], wdv[:, gi * 4:(gi + 1) * 4, :])

    def up(gi, tb):
        nonlocal nu
        wub = wu[gi % 2]
        u = uT[(gi * 2 + tb) % 2]
        for fc in range(4):
            pb = B[nu % 2]
            r = rl[nu % 2]
            nu += 1
            for kc in range(16):
                P.mm([pb], pb[:], [wub, hT], wub[:, kc, fc * 128:(fc + 1) * 128], hT[:, kc, tb * 512:(tb + 1) * 512], kc == 0, kc == 15)
            P.act([r], r[:], [pb], pb[:], AF.Relu)
            P.act([u], u[:, fc, :], [r], r[:], AF.Square)

    def down(gi, tb):
        nonlocal no
        wdb = wd[gi % 2]
        u = uT[(gi * 2 + tb) % 2]
        for tt in range(4):
            j = tb * 4 + tt
            for dbp in range(2):
                ob, of = C.O[no % 2], C.Of[no % 2]
                no += 1
                for dbi in range(2):
                    db = dbp * 2 + dbi
                    for fc in range(4):
                        P.mm([ob], of[:, dbi * 512:(dbi + 1) * 512], [u, wdb], u[:, fc, tt * 128:(tt + 1) * 128],
                             wdb[:, fc, db * 512:(db + 1) * 512], fc == 0, fc == 3)
                ys = yacc[:, j, dbp * 1024:(dbp + 1) * 1024]
                if gi == 0:
                    P.copy("dve", [yacc], ys, [ob], of[:, :])
                else:
                    P.tt("dve", [yacc], ys, [ob, yacc], of[:, :], ys, ALU.add)

    for blk in range(NT // TB):
        P.dma("sp", gbc[:], bcast_row(I["norm_mlp"][layer:layer + 1, :]), [], [gbc], gbc)
        load_group(0)
        load_group(1)
        norm_transpose(C, range(blk * TB, (blk + 1) * TB), gbc, hT, 0, xst, hb, tmp)
        units = [(gi, tb) for gi in range(NG) for tb in range(TB // 4)]
        up(*units[0])
        for k, (gi, tb) in enumerate(units):
            if k + 1 < len(units):
                up(*units[k + 1])
            down(gi, tb)
            if tb == TB // 4 - 1 and 1 <= gi + 1 and gi + 2 < NG:
                load_group(gi + 2)
        if final:
            P.dma("sp", gfin[:], bcast_row(I["norm_final"]), [], [gfin], gfin)
        for j in range(TB):
            t = blk * TB + j
            if not final:
                P.dma("pool", C.xs[t * 128:(t + 1) * 128, :], yacc[:, j, :], [yacc, C.xsB[t]], [C.xsB[t]], yacc, accum_op=ALU.add)
                continue
            xa = xst[j % 2]
            load_x_tile(C, xa, t)
            P.tt("dve", [xa], xa[:], [xa, yacc], xa[:], yacc[:, j, :], ALU.add)
            if True:
                ot = hb
                yo = yacc[:, j, :]
                norm_rows(C, xa[:], xa, D, gfin, yo, yacc, tmp)
                oid = P.dma("sp", C.out[t * 128:(t + 1) * 128, :], yo, [yacc], [], yacc)
                C.out_ops.append(oid)
    C.x_src = C.xs
    C.x_srcB = C.xsB
    P.release(m)


def layer1_mixer(C):
    P, I, B = C.P, C.I, C.B
    m_all = P.mark()
    SCALE = 192 ** -0.5
    cqnT = P.sb("cqnT", [128, 6, S], BF16)
    ckvT = P.sb("ckvT", [128, 4, S], BF16)
    krT = P.sb("krT", [64, S], BF16)
    cs = P.sb("cs", [64, 2, S], F32)
    P.dma("sp", cs[:], I["cs"], [], [cs], cs)
    mmask = P.sb("mmask", [128, 128], BF16)
    P.dma("pool", mmask[:], I["mlamask"], [], [mmask], mmask)
    m0 = P.mark()
    hT = P.sb("hT1", [128, 16, S], BF16)
    gbc = P.sb("gbc1", [128, D], F32)
    P.dma("sp", gbc[:], bcast_row(I["norm_mix_o"]), [], [gbc], gbc)
    qg = P.sb("qg", [128, 768], F32)
    kg = P.sb("kg", [128, 512], F32)
    P.dma("sp", qg[:], bcast_row(I["q_norm"]), [], [qg], qg)
    P.dma("sp", kg[:], bcast_row(I["kv_norm"]), [], [kg], kg)
    hb = P.sb("hb1", [128, D], BF16)
    tmp = (P.sb("junk1", [128, D], BF16), P.sb("ssq1", [128, 1], F32), P.sb("sd1", [128, 2], F32), P.sb("rstd1", [128, 1], F32))
    WI = P.sb("WI", [128, 16, 1344], BF16)
    wiv = I["w_in_o"].rearrange("(kc p) c -> p kc c", p=128)
    load_w_cast(C, WI, WI[:, 0:8, :], wiv[:, 0:8, :])
    load_w_cast(C, WI, WI[:, 8:16, :], wiv[:, 8:16, :])
    WIr = P.sb("WIr", [128, 16, 64], BF16)
    P.ts("pool", [WIr], WIr[:, :, 0:32], [WI], WI[:, :, 1312:1344], -1.0, None, ALU.mult)
    P.copy("pool", [WIr], WIr[:, :, 32:64], [WI], WI[:, :, 1280:1312])
    m_x = P.mark()
    xst = [P.sb(f"xs1{i}", [128, D], F32) for i in range(2)]
    norm_transpose(C, range(NT), gbc, hT, 0, xst, hb, tmp)
    P.release(m_x)
    cn = P.sb("cn", [128, 1280], BF16)
    ssb = P.sb("ssb", [128, 4], F32)
    for t in range(NT):
        pa, pbk, pc = B[0], B[1], C.O[t % 2]
        paa, pba, pca = B[0].t, B[1].t, C.Of[t % 2]
        for (pb, pap, c0, w) in ((pa, paa, 0, 384), (pbk, pba, 384, 384), (pc, pca, 768, 512)):
            for kc in range(16):
                P.mm([pb], pap[:, 0:w], [hT, WI], hT[:, kc, t * 128:(t + 1) * 128], WI[:, kc, c0:c0 + w], kc == 0, kc == 15)
        junk = tmp[0]
        P.act([junk, ssb], junk[:, 0:384], [pa], pa[:, 0:384], AF.Square, accum_out=ssb[:, 0:1])
        P.act([junk, ssb], junk[:, 384:768], [pbk], pbk[:, 0:384], AF.Square, accum_out=ssb[:, 1:2])
        P.act([junk, ssb], junk[:, 768:1280], [pc], pca[:, 0:512], AF.Square, accum_out=ssb[:, 2:3])
        sd = tmp[2]
        rq = tmp[3]
        rk = tmp[1]
        P.tt("dve", [ssb], ssb[:, 3:4], [ssb], ssb[:, 0:1], ssb[:, 1:2], ALU.add)
        P.ts("dve", [sd], sd[:, 0:1], [ssb], ssb[:, 3:4], 1.0 / 768, EPS, ALU.mult, ALU.add)
        P.ts("dve", [sd], sd[:, 1:2], [ssb], ssb[:, 2:3], 1.0 / 512, EPS, ALU.mult, ALU.add)
        P.tt("pool", [rq], rq[:, 0:1], [sd, C.neghalf], sd[:, 0:1], C.neghalf[:, 0:1], ALU.pow)
        P.tt("pool", [rk], rk[:, 0:1], [sd, C.neghalf], sd[:, 1:2], C.neghalf[:, 0:1], ALU.pow)
        P.stt([cn], cn[:, 0:384], [pa, rq, qg], pa[:, 0:384], rq[:, 0:1], qg[:, 0:384], ALU.mult, ALU.mult)
        P.stt([cn], cn[:, 384:768], [pbk, rq, qg], pbk[:, 0:384], rq[:, 0:1], qg[:, 384:768], ALU.mult, ALU.mult)
        P.stt([cn], cn[:, 768:1280], [pc, rk, kg], pca[:, 0:512], rk[:, 0:1], kg[:, 0:512], ALU.mult, ALU.mult)
        for grp, (dstT, nck, cbase) in enumerate(((cqnT, 4, 0), (cqnT, 2, 512), (ckvT, 4, 768))):
            trb, trap = C.TR[grp % 2], C.TRap[grp % 2]
            for k in range(nck):
                P.tr([trb], trap[:, k * 128:(k + 1) * 128], [cn, C.ident], cn[:, cbase + k * 128:cbase + (k + 1) * 128], C.ident[:])
            k0 = 0 if grp != 1 else 4
            P.copy("act" if grp % 2 == 0 else "dve", [dstT], dstT[:, k0:k0 + nck, t * 128:(t + 1) * 128], [trb],
                   trap[:, 0:nck * 128].rearrange("p (a b) -> p a b", a=nck))
    t1 = P.sb("rt1", [64, 512], F32)
    t2 = P.sb("rt2", [64, 512], F32)
    for tb in range(4):
        p1, p2 = B[0], B[1]
        for kc in range(16):
            P.mm([p1], p1[0:64, :], [WI, hT], WI[:, kc, 1280:1344], hT[:, kc, tb * 512:(tb + 1) * 512], kc == 0, kc == 15)
        for kc in range(16):
            P.mm([p2], p2[0:64, :], [WIr, hT], WIr[:, kc, :], hT[:, kc, tb * 512:(tb + 1) * 512], kc == 0, kc == 15)
        P.tt("dve", [t1], t1[:], [p1, cs], p1[0:64, :], cs[:, 0, tb * 512:(tb + 1) * 512], ALU.mult)
        P.tt("dve", [t2], t2[:], [p2, cs], p2[0:64, :], cs[:, 1, tb * 512:(tb + 1) * 512], ALU.mult)
        P.tt("pool", [krT], krT[:, tb * 512:(tb + 1) * 512], [t1, t2], t1[:], t2[:], ALU.add)
    if "l1lat" in C.dbg_want:
        dbg_dump(C, "cqnT", cqnT, cqnT[:], [128, 6, S], BF16)
        dbg_dump(C, "ckvT", ckvT, ckvT[:], [128, 4, S], BF16)
        dbg_dump(C, "krT", krT, krT[:], [64, S], BF16)
    P.release(m0)
    oT = P.sb("oT1", [128, 16, S], BF16)
    m_after_oT = P.mark()
    wq = [P.sb(f"wq{i}", [128, 6, 192], BF16) for i in range(2)]
    wqr = [P.sb(f"wqr{i}", [128, 6, 64], BF16) for i in range(2)]
    wkv = [P.sb(f"wkv{i}", [128, 4, 256], BF16) for i in range(2)]
    kT = [P.sb(f"kTh{i}", [128, S], BF16) for i in range(2)]
    vh = [P.sb(f"vh{i}", [128, NT, 129], BF16) for i in range(2)]
    for v_ in vh:
        P.memset("pool", [v_], v_[:, :, 128:129], 1.0)
    qn = [P.sb(f"qnh{i}", [128, S], BF16) for i in range(2)]
    qr = [P.sb(f"qrh{i}", [64, S], BF16) for i in range(2)]
    PT = [P.sb(f"PT1{i}", [128, 512], BF16) for i in range(3)]
    ob = [P.sb(f"ob{i}", [128, 4, 128], BF16) for i in range(2)]
    rz = [P.sb(f"rz1{i}", [128, 4], F32) for i in range(2)]
    wqv = I["w_q_up"].rearrange("(kc p) c -> p kc c", p=128)
    wkvv = I["w_kv_up"].rearrange("(kc p) c -> p kc c", p=128)
    stc = 0
    ptc = 0
    def load_head(h):
        s2 = h % 2
        load_w_cast(C, wq[s2], wq[s2][:], wqv[:, :, h * 192:(h + 1) * 192])
        load_w_cast(C, wkv[s2], wkv[s2][:], wkvv[:, :, h * 256:(h + 1) * 256])
    load_head(0)
    for h in range(16):
        s2 = h % 2
        if h + 1 < 16:
            load_head(h + 1)
        P.ts("pool", [wqr[s2]], wqr[s2][:, :, 0:32], [wq[s2]], wq[s2][:, :, 160:192], -1.0, None, ALU.mult)
        P.copy("pool", [wqr[s2]], wqr[s2][:, :, 32:64], [wq[s2]], wq[s2][:, :, 128:160])
        pjB = C.TR[0]
        pj = C.TR[0].t[:, :].bitcast(F32)
        for tb in range(4):
            for kc in range(4):
                P.mm([pjB], pj[:], [wkv[s2], ckvT], wkv[s2][:, kc, 0:128], ckvT[:, kc, tb * 512:(tb + 1) * 512], kc == 0, kc == 3)
            P.copy("dve", [kT[s2]], kT[s2][:, tb * 512:(tb + 1) * 512], [pjB], pj[:])
        for t4 in range(4):
            for tt in range(4):
                t = t4 * 4 + tt
                for kc in range(4):
                    P.mm([pjB], pj[:, tt * 128:(tt + 1) * 128], [ckvT, wkv[s2]], ckvT[:, kc, t * 128:(t + 1) * 128], wkv[s2][:, kc, 128:256], kc == 0, kc == 3)
            P.copy("dve", [vh[s2]], vh[s2][:, t4 * 4:(t4 + 1) * 4, 0:128], [pjB], pj[:].rearrange("p (a b) -> p a b", a=4))
        for tb in range(4):
            for kc in range(6):
                P.mm([pjB], pj[:], [wq[s2], cqnT], wq[s2][:, kc, 0:128], cqnT[:, kc, tb * 512:(tb + 1) * 512], kc == 0, kc == 5)
            P.copy("dve", [qn[s2]], qn[s2][:, tb * 512:(tb + 1) * 512], [pjB], pj[:])
            for kc in range(6):
                P.mm([pjB], pj[0:64, :], [wq[s2], cqnT], wq[s2][:, kc, 128:192], cqnT[:, kc, tb * 512:(tb + 1) * 512], kc == 0, kc == 5)
            P.tt("dve", [t1], t1[:], [pjB, cs], pj[0:64, :], cs[:, 0, tb * 512:(tb + 1) * 512], ALU.mult)
            for kc in range(6):
                P.mm([pjB], pj[0:64, :], [wqr[s2], cqnT], wqr[s2][:, kc, :], cqnT[:, kc, tb * 512:(tb + 1) * 512], kc == 0, kc == 5)
            P.tt("dve", [t2], t2[:], [pjB, cs], pj[0:64, :], cs[:, 1, tb * 512:(tb + 1) * 512], ALU.mult)
            P.tt("pool", [qr[s2]], qr[s2][:, tb * 512:(tb + 1) * 512], [t1, t2], t1[:], t2[:], ALU.add)
        for Qb in range(4):
            Ob, Of = C.O[Qb % 2], C.Of[Qb % 2]
            nkt = 4 * Qb + 4
            pend = None

            def fin(stb, kt, c0, Qb=Qb, Ob=Ob, Of=Of, s2=s2):
                nonlocal ptc
                pt = PT[ptc % 3]
                ptc += 1
                P.act([pt], pt[:, c0:512], [stb], stb[:, c0:512], AF.Exp, scale=SCALE)
                for jj in range(c0 // 128, 4):
                    last = (kt == 4 * Qb + jj)
                    P.mm([Ob], Of[:, jj * 256:jj * 256 + 129], [pt, vh[s2]], pt[:, jj * 128:(jj + 1) * 128], vh[s2][:, kt, :], kt == 0 and jj % 2 == 0, last, skip=True)

            for kt in range(nkt):
                stb = B[stc % 2]
                stc += 1
                c0 = max(0, kt - 4 * Qb) * 128
                q0 = Qb * 512
                kl = kT[s2][:, kt * 128:(kt + 1) * 128]
                krl = krT[0:64, kt * 128:(kt + 1) * 128]
                if kt >= 4 * Qb:
                    P.mm([stb], stb[:, c0:c0 + 128], [C.ident, mmask], C.ident[:], mmask[:], True, False)
                    P.mm([stb], stb[:, c0:c0 + 128], [kT[s2], qn[s2]], kl, qn[s2][:, q0 + c0:q0 + c0 + 128], False, False)
                    P.mm([stb], stb[:, c0:c0 + 128], [krT, qr[s2]], krl, qr[s2][0:64, q0 + c0:q0 + c0 + 128], False, True)
                    c1 = c0 + 128
                else:
                    c1 = c0
                if c1 < 512:
                    P.mm([stb], stb[:, c1:512], [kT[s2], qn[s2]], kl, qn[s2][:, q0 + c1:q0 + 512], True, False)
                    P.mm([stb], stb[:, c1:512], [krT, qr[s2]], krl, qr[s2][0:64, q0 + c1:q0 + 512], False, True)
                if pend is not None:
                    fin(*pend)
                pend = (stb, kt, c0)
            fin(*pend)
            rzb = rz[Qb % 2]
            obb = ob[Qb % 2]
            O4 = Of.rearrange("p (a b) -> p a b", a=4)
            P.recip([rzb], rzb[:], [Ob], O4[:, :, 128])
            P.tt("dve", [obb], obb[:], [Ob, rzb], O4[:, :, 0:128], rzb[:].unsqueeze(2).to_broadcast([128, 4, 128]), ALU.mult)
            trb, trap = C.TR[1], C.TRap[1]
            for jj in range(4):
                P.tr([trb], trap[:, jj * 128:(jj + 1) * 128], [obb, C.ident], obb[:, jj, :], C.ident[:])
            P.copy("act", [oT], oT[:, h, Qb * 512:(Qb + 1) * 512], [trb], trap[:, :])
    if "l1o" in C.dbg_want:
        dbg_dump(C, "oT1", oT, oT[:], [128, 16, S], BF16)
    P.release(m_after_oT)
    WO = P.sb("WO1", [128, 16, D], BF16)
    wov = I["w_out_o"].rearrange("(kc p) c -> p kc c", p=128)
    for q4 in range(4):
        load_w_cast(C, WO, WO[:, q4 * 4:(q4 + 1) * 4, :], wov[:, q4 * 4:(q4 + 1) * 4, :])
    xst = [P.sb(f"xo1{i}", [128, D], F32) for i in range(2)]
    for t in range(NT):
        xa = xst[t % 2]
        load_x_tile(C, xa, t)
        for db in range(4):
            pb = B[db % 2]
            for kc in range(16):
                P.mm([pb], pb[:], [oT, WO], oT[:, kc, t * 128:(t + 1) * 128], WO[:, kc, db * 512:(db + 1) * 512], kc == 0, kc == 15)
            P.tt("dve", [xa], xa[:, db * 512:(db + 1) * 512], [pb, xa], pb[:], xa[:, db * 512:(db + 1) * 512], ALU.add)
        P.dma("sp", C.xs[t * 128:(t + 1) * 128, :], xa[:], [xa], [C.xsB[t]], xa)
    C.x_src = C.xs
    C.x_srcB = C.xsB
    P.release(m_all)


_CACHE = {}


def _prep_inputs(inputs, used=None):
    tabs = _host_tables(np.asarray(inputs["rel_bias"], np.float32))
    shared = dict(tabs)
    sq = lambda k: np.ascontiguousarray(np.asarray(inputs[k], np.float32)[0])
    for k in ("w_in_e", "cmp_pos_k", "cmp_pos_v", "cmp_k_w1", "cmp_k_w2", "cmp_v_w1", "cmp_v_w2", "w_out_e",
              "w_in_o", "w_q_up", "w_kv_up", "w_out_o"):
        shared[k] = sq(k)
    for k in ("norm_mix_e", "sinks", "norm_mix_o", "q_norm", "kv_norm"):
        shared[k] = np.ascontiguousarray(np.asarray(inputs[k], np.float32).reshape(1, -1))
    shared["norm_mlp"] = np.ascontiguousarray(np.asarray(inputs["norm_mlp"], np.float32))
    shared["norm_final"] = np.ascontiguousarray(np.asarray(inputs["norm_final"], np.float32).reshape(1, -1))
    for l in range(2):
        shared[f"w_up{l}"] = np.ascontiguousarray(np.asarray(inputs["w_up"], np.float32)[l])
        shared[f"w_down{l}"] = np.ascontiguousarray(np.asarray(inputs["w_down"], np.float32)[l])
    x = np.asarray(inputs["x"], np.float32)
    in_maps = []
    for c in range(NCORES):
        m = dict(shared)
        m["x"] = np.ascontiguousarray(x[c])
        if used is not None:
            m = {k: v for k, v in m.items() if k in used}
        in_maps.append(m)
    return in_maps


def kernel(**inputs):
    if "nc" not in _CACHE:
        _CACHE["nc"], _CACHE["P"] = build_program()
    nc = _CACHE["nc"]
    in_maps = _prep_inputs(inputs, _CACHE["P"].used_inputs)
    res = run_bass_kernel_spmd(nc, in_maps, core_ids=list(range(NCORES)))
    return np.stack([np.asarray(r["out"], np.float32) for r in res.results], axis=0)
```

```python
import math
import numpy as np
import concourse.bass as bass
import concourse.mybir as mybir
from concourse.bass_utils import run_bass_kernel_spmd

F32 = mybir.dt.float32
BF16 = mybir.dt.bfloat16
AF = mybir.ActivationFunctionType
ALU = mybir.AluOpType
AX = mybir.AxisListType

S = 2048
D = 2048
NT = 16
DFF = 8192
NEGM = -30000.0
EPS = 1e-6
NCORES = 8


class Buf:
    __slots__ = ("name", "t", "last_w", "readers", "off", "size", "psum")

    def __init__(self, name, t=None):
        self.psum = False
        self.name = name
        self.t = t
        self.last_w = None
        self.readers = []
        self.off = None
        self.size = 0

    def __getitem__(self, k):
        return self.t[k]


class Prog:
    ENGS = ("pe", "act", "dve", "pool", "sp")
    SB_LO = 16512
    SB_HI = 229344

    def __init__(self, nc):
        self.nc = nc
        self.eng = {"pe": nc.tensor, "act": nc.scalar, "dve": nc.vector,
                    "pool": nc.gpsimd, "sp": nc.sync}
        self.ops = []
        self.top = self.SB_LO
        self.allocs = []
        self.uid = 0

    def sb(self, name, shape, dtype):
        esz = 2 if dtype == BF16 else 4
        n = 1
        for s_ in shape[1:]:
            n *= s_
        size = (n * esz + 63) // 64 * 64
        off = self.top
        assert off + size <= self.SB_HI, f"SBUF overflow allocating {name}: {off}+{size}"
        self.top = off + size
        self.uid += 1
        t = self.nc.alloc_sbuf_tensor_at(f"{name}_{self.uid}", list(shape), dtype, offset=off)
        b = Buf(f"{name}_{self.uid}", t)
        b.off, b.size = off, size
        inh = set()
        for (o2, s2, b2) in self.allocs:
            if o2 < off + size and off < o2 + s2:
                if b2.last_w is not None:
                    inh.add(b2.last_w)
                inh.update(b2.readers)
        b.readers = list(inh)
        self.allocs.append((off, size, b))
        return b

    def mark(self):
        return self.top

    def release(self, m):
        self.top = m

    def ps(self, name, shape, dtype=F32):
        b = Buf(name, self.nc.alloc_psum_tensor(name, list(shape), dtype))
        b.psum = True
        return b

    def tok(self, name):
        return Buf(name, None)

    def op(self, engine, fn, reads=(), writes=(), dma=None, extra=()):
        oid = len(self.ops)
        deps = set(extra)
        for b in reads:
            if b.last_w is not None:
                deps.add(b.last_w)
            if b.psum:
                deps.update(r for r in b.readers if self.ops[r][0] != engine)
        for b in writes:
            if b.last_w is not None:
                deps.add(b.last_w)
            deps.update(b.readers)
        for b in writes:
            b.last_w = oid
            b.readers = []
        for b in reads:
            if b not in writes:
                if dma is None:
                    b.readers = [r for r in b.readers if not (self.ops[r][0] == engine and self.ops[r][3] is None)]
                b.readers.append(oid)
        self.ops.append((engine, fn, deps, dma))
        return oid

    def dma(self, engine, out_ap, in_ap, reads, writes, chan, **kw):
        return self.op(engine, lambda e: e.dma_start(out=out_ap, in_=in_ap, **kw), reads, writes, dma=chan)

    def mm(self, W, out, R, lhsT, rhs, start, stop, skip=False):
        return self.op("pe", lambda e: e.matmul(out, lhsT=lhsT, rhs=rhs, start=start, stop=stop, skip_group_check=skip), R, W)

    def tr(self, W, out, R, in_, ident):
        return self.op("pe", lambda e: e.transpose(out=out, in_=in_, identity=ident), R, W)

    def act(self, W, out, R, in_, func, **kw):
        return self.op("act", lambda e: e.activation(out=out, in_=in_, func=func, **kw), R, W)

    def tt(self, eng, W, out, R, in0, in1, op):
        return self.op(eng, lambda e: e.tensor_tensor(out=out, in0=in0, in1=in1, op=op), R, W)

    def ts(self, eng, W, out, R, in0, s1, s2, op0, op1=None):
        if op1 is None:
            return self.op(eng, lambda e: e.tensor_scalar(out=out, in0=in0, scalar1=s1, scalar2=None, op0=op0), R, W)
        return self.op(eng, lambda e: e.tensor_scalar(out=out, in0=in0, scalar1=s1, scalar2=s2, op0=op0, op1=op1), R, W)

    def stt(self, W, out, R, in0, scalar, in1, op0, op1):
        return self.op("dve", lambda e: e.scalar_tensor_tensor(out=out, in0=in0, scalar=scalar, in1=in1, op0=op0, op1=op1), R, W)

    def copy(self, eng, W, out, R, in_):
        if eng == "act":
            return self.op("act", lambda e: e.activation(out=out, in_=in_, func=AF.Copy), R, W)
        return self.op(eng, lambda e: e.tensor_copy(out=out, in_=in_), R, W)

    def recip(self, W, out, R, in_):
        return self.op("dve", lambda e: e.reciprocal(out=out, in_=in_), R, W)

    def memset(self, eng, W, out, val):
        return self.op(eng, lambda e: e.memset(out, val), (), W)

    def emit(self):
        nc = self.nc
        ops = self.ops
        n = len(ops)

        def skip(e, dma, d):
            return e == "pe" and dma is None and ops[d][0] == "pe" and ops[d][3] is None

        needed = [False] * n
        for (e, fn, deps, dma) in ops:
            for d in deps:
                if not skip(e, dma, d):
                    needed[d] = True
        sems = {"e_" + e: nc.alloc_semaphore(name=f"sem_{e}") for e in self.ENGS}
        ecount = {e: 0 for e in self.ENGS}
        chan_count = {}
        event = [None] * n
        waited = {e: {} for e in self.ENGS}
        nwaits = 0
        for i, (e, fn, deps, dma) in enumerate(ops):
            eng = self.eng[e]
            req = {}
            for d in deps:
                if skip(e, dma, d):
                    continue
                k, v = event[d]
                if k in chan_count:
                    v = chan_count[k]
                if req.get(k, 0) < v:
                    req[k] = v
            for k, v in req.items():
                if waited[e].get(k, 0) >= v:
                    continue
                eng.wait_ge(sems[k], v)
                waited[e][k] = v
                nwaits += 1
            ins = fn(eng)
            if dma is not None:
                key = "c_" + dma.name + "_" + e
                if key not in sems:
                    sems[key] = nc.alloc_semaphore(name="sem_" + key)
                    chan_count[key] = 0
                chan_count[key] += 16
                ins.then_inc(sems[key], 16)
                event[i] = (key, chan_count[key])
            elif needed[i]:
                ecount[e] += 1
                ins.then_inc(sems["e_" + e], 1)
                event[i] = ("e_" + e, ecount[e])
            else:
                event[i] = ("e_" + e, ecount[e] + 1)
        self.stats = dict(n_ops=n, n_waits=nwaits, counts=dict(ecount), n_sems=len(sems))


def _t5_bucket(dist):
    dist = np.maximum(dist, 0)
    d = np.maximum(dist, 1).astype(np.float32)
    large = 16 + (np.log(d / np.float32(16)) / np.float32(math.log(128 / 16)) * np.float32(16)).astype(np.int32)
    large = np.minimum(large, 31)
    return np.where(dist < 16, dist, large).astype(np.int64)


def _host_tables(rel_bias):
    rb = np.concatenate([rel_bias.astype(np.float32), np.full((1, 32), NEGM, np.float32)], axis=0)
    k = np.arange(128)[:, None]
    q = np.arange(128)[None, :]

    def tile(dist, valid, heads):
        idx = np.where(valid, _t5_bucket(dist), 32)
        return rb[idx][:, :, heads].transpose(0, 2, 1)

    hA = np.arange(0, 16)
    hB = np.arange(16, 32)
    d0 = q - k
    d1 = q - k + 128
    d4 = q - k + 512
    ones = np.ones((128, 128), bool)
    biasA = np.stack([tile(d0, (d0 >= 0) & (d0 < 128), hA), tile(d1, (d1 >= 0) & (d1 < 128), hA)])
    biasB = np.stack([tile(d0, d0 >= 0, hB), tile(d1, ones, hB), tile(np.full((128, 128), 1000), ones, hB),
                      tile(d4, d4 < 512, hB)])
    cc = (np.arange(248) - 120)[:, None]
    tq = np.arange(128)[None, :]
    dc = tq - 16 * cc - 31
    idx = np.where(dc >= 0, _t5_bucket(dc), 32)
    cbiasU = rb[idx][:, :, hB].transpose(0, 2, 1)
    cfar = np.zeros((16, 32, 16), np.float32)
    for i in range(16):
        for n in range(32):
            if n < 2 * i - 2:
                cfar[i, n, :] = rel_bias[31, 16:32]
    t = np.arange(S)[:, None]
    n = np.arange(32)[None, :]
    cur = t // 64
    future = n * 64 > t
    forced = (n == 0) | (n == cur) | (n == cur - 1)
    keep = np.where(future | forced, 0.0, 1.0).astype(np.float32)
    add = np.where(future, -1e30, np.where(forced, 1e4, 0.0)).astype(np.float32)
    keepadd = np.stack([keep, add], axis=1).reshape(16, 128, 2, 32).transpose(1, 0, 2, 3)
    c_start = np.arange(127) * 16
    s_start = np.arange(32) * 64
    overlap = ((c_start[:, None] <= s_start[None] + 63) & (c_start[:, None] + 31 >= s_start[None])).astype(np.float32)
    bind = (np.arange(S)[None, :] // 64 == np.arange(32)[:, None]).astype(np.float32)
    inv = 1.0 / (10000.0 ** (np.arange(0, 64, 2, dtype=np.float32) / 64))
    ang = np.arange(S, dtype=np.float32)[:, None] * inv[None].astype(np.float32)
    cos = np.cos(ang.astype(np.float32)).astype(np.float32).T
    sin = np.sin(ang.astype(np.float32)).astype(np.float32).T
    cs = np.stack([np.concatenate([cos, cos], 0), np.concatenate([sin, sin], 0)], axis=1)
    mlamask = np.where(k <= q, 0.0, NEGM).astype(np.float32)
    return dict(biasA=np.ascontiguousarray(biasA.transpose(1, 0, 2, 3)),
                biasB=np.ascontiguousarray(biasB.transpose(1, 0, 2, 3)),
                cbiasU=np.ascontiguousarray(cbiasU), cfar=np.ascontiguousarray(cfar.transpose(1, 0, 2)),
                keepadd=np.ascontiguousarray(keepadd), overlap=overlap, bind=bind,
                cs=np.ascontiguousarray(cs), mlamask=mlamask, ident=np.eye(128, dtype=np.float32))


INPUT_SHAPES = dict(
    x=[S, D], norm_mix_e=[1, D], w_in_e=[D, 3120], sinks=[1, 16], cmp_pos_k=[32, 64], cmp_pos_v=[32, 64],
    cmp_k_w1=[2048, 256], cmp_k_w2=[256, 64], cmp_v_w1=[2048, 256], cmp_v_w2=[256, 64], w_out_e=[D, D],
    norm_mix_o=[1, D], w_in_o=[D, 1344], q_norm=[1, 768], w_q_up=[768, 3072], kv_norm=[1, 512],
    w_kv_up=[512, 4096], w_out_o=[D, D], norm_mlp=[2, D], w_up0=[D, DFF], w_up1=[D, DFF],
    w_down0=[DFF, D], w_down1=[DFF, D], norm_final=[1, D],
    biasA=[128, 2, 16, 128], biasB=[128, 4, 16, 128], cbiasU=[248, 16, 128], cfar=[32, 16, 16],
    keepadd=[128, 16, 2, 32], overlap=[127, 32], bind=[32, S], cs=[64, 2, S], mlamask=[128, 128], ident=[128, 128],
)


class Ctx:
    pass


class LazyInputs:
    def __init__(self, nc):
        self.nc = nc
        self.d = {}

    def __getitem__(self, k):
        if k not in self.d:
            self.d[k] = self.nc.dram_tensor(k, INPUT_SHAPES[k], F32, kind="ExternalInput").ap()
        return self.d[k]


def build_program(stages=("l0mix", "l0mlp", "l1mix", "l1mlp"), dbg=(), cut=None):
    nc = bass.Bass("TRN2", target_bir_lowering=False)
    P = Prog(nc)
    C = Ctx()
    C.nc, C.P = nc, P
    I = LazyInputs(nc)
    C.I = I
    C.out = nc.dram_tensor("out", [S, D], F32, kind="ExternalOutput").ap()
    C.xs = nc.dram_tensor("xs", [S, D], F32, kind="Internal").ap()
    C.xsB = [P.tok(f"xs{t}") for t in range(NT)]
    C.out_ops = []
    C.dbg = {}
    C.dbg_want = dbg
    C.cut = cut
    C.B = [P.ps(f"pb{i}", [128, 512], F32) for i in range(2)]
    C.O = [P.ps(f"po{i}", [128, 8, 128], F32) for i in range(2)]
    C.Of = [o.t[:, :, :].rearrange("p h c -> p (h c)") for o in C.O]
    C.TR = [P.ps(f"ptr{i}", [128, 1024], BF16) for i in range(2)]
    C.TRap = [t.t[:, 0:512] for t in C.TR]
    C.ident = P.sb("ident", [128, 128], BF16)
    P.dma("pool", C.ident[:], I["ident"], [], [C.ident], C.ident)
    C.ones = P.sb("ones", [128, 1], BF16)
    P.memset("dve", [C.ones], C.ones[:], 1.0)
    C.neghalf = P.sb("neghalf", [128, 2], F32)
    P.memset("pool", [C.neghalf], C.neghalf[:], -0.5)
    C.x_src = I["x"]
    C.x_srcB = None

    if "l0mix" in stages:
        layer0_mixer(C)
    if "l0mlp" in stages:
        mlp(C, 0, final=False)
    if "l1mix" in stages:
        layer1_mixer(C)
    if "l1mlp" in stages:
        mlp(C, 1, final=True)
    if "xs" in dbg:
        d = nc.dram_tensor("dbg_xs", [S, D], F32, kind="ExternalOutput").ap()
        db = P.tok("dbgxs")
        for t in range(NT):
            C.out_ops.append(P.dma("sp", d[t * 128:(t + 1) * 128, :], C.xs[t * 128:(t + 1) * 128, :], [C.xsB[t]], [], db))
    P.op("sp", lambda e: e.nop(), extra=C.out_ops)
    P.emit()
    P.used_inputs = list(I.d.keys())
    return nc, P


def bcast_row(ap_row, n=128):
    return ap_row.rearrange("o n -> (o n)").partition_broadcast(n)


def load_x_tile(C, dst, t):
    P = C.P
    reads = [] if C.x_srcB is None else [C.x_srcB[t]]
    P.dma("sp", dst[:], C.x_src[t * 128:(t + 1) * 128, :], reads, [dst], dst)


def norm_rows(C, src_ap, srcB, n, gbc, out_ap, outB, tmp):
    P = C.P
    junk, ssq, sd, rstd = tmp
    P.act([junk, ssq], junk[:, 0:n], [srcB], src_ap, AF.Square, accum_out=ssq[:, 0:1])
    P.ts("dve", [sd], sd[:, 0:1], [ssq], ssq[:, 0:1], 1.0 / n, EPS, ALU.mult, ALU.add)
    P.tt("pool", [rstd], rstd[:, 0:1], [sd, C.neghalf], sd[:, 0:1], C.neghalf[:, 0:1], ALU.pow)
    P.stt([outB], out_ap, [srcB, rstd, gbc], src_ap, rstd[:, 0:1], gbc[:, 0:n], ALU.mult, ALU.mult)


def norm_transpose(C, tiles, gbc, hT, col0, xst, hb, tmp):
    P = C.P
    for j, t in enumerate(tiles):
        xa = xst[j % 2]
        load_x_tile(C, xa, t)
        norm_rows(C, xa[:], xa, D, gbc, hb[:], hb, tmp)
        for q4 in range(4):
            tb = C.TR[q4 % 2]
            tap = C.TRap[q4 % 2]
            for k in range(4):
                kc = q4 * 4 + k
                P.tr([tb], tap[:, k * 128:(k + 1) * 128], [hb, C.ident], hb[:, kc * 128:(kc + 1) * 128], C.ident[:])
            dst = hT[:, q4 * 4:q4 * 4 + 4, col0 + j * 128:col0 + (j + 1) * 128]
            P.copy("act" if q4 % 2 == 0 else "dve", [hT], dst, [tb], tap.rearrange("p (a b) -> p a b", a=4))


def load_w_cast(C, dst, dst_ap, src_ap):
    C.P.dma("pool", dst_ap, src_ap, [], [dst], dst)


def layer0_mixer(C):
    P, I = C.P, C.I
    B = C.B
    m_persist = P.mark()
    kaT = P.sb("kaT", [64, 2, S], BF16)
    kwT = P.sb("kwT", [64, 2, S], BF16)
    ksA = P.sb("ksA", [96, 2, S], BF16)
    va = P.sb("va", [128, NT, 2, 65], BF16)
    vs = P.sb("vs", [128, NT, 2, 65], BF16)
    vw = P.sb("vw", [128, NT, 2, 65], BF16)
    kcT = P.sb("kcT", [64, 2, 128], BF16)
    vcA = P.sb("vcA", [128, 2, 97], BF16)
    gates = P.sb("gates", [128, NT, 48], F32)
    sinkexp = P.sb("sinkexp", [128, 16], F32)
    qsa = C.nc.dram_tensor("qsa", [1024, S], BF16, kind="Internal").ap()
    qsb = C.nc.dram_tensor("qsb", [1024, S], BF16, kind="Internal").ap()
    qsB = {(w, c, tb): P.tok(f"qs{w}{c}_{tb}") for w in "ab" for c in range(8) for tb in range(4)}
    for vt in (va, vs, vw):
        P.memset("pool", [vt], vt[:, :, :, 64:65], 1.0)
    P.memset("pool", [vcA], vcA[:], 0.0)
    P.memset("pool", [kcT], kcT[:], 0.0)
    P.memset("pool", [vcA], vcA[:, :, 64:65], 1.0)
    for g in range(2):
        P.dma("pool", ksA[64:96, g, :], I["bind"], [], [ksA], ksA)
        P.dma("pool", vcA[0:127, g, 65:97], I["overlap"], [], [vcA], vcA)
    P.dma("sp", sinkexp[:], bcast_row(I["sinks"]), [], [sinkexp], sinkexp)
    P.act([sinkexp], sinkexp[:], [sinkexp], sinkexp[:], AF.Exp)

    m_raw = P.mark()
    rawk = P.sb("rawk", [64, 2, S], BF16)
    rawv = P.sb("rawv", [64, 2, S], BF16)
    m0 = P.mark()
    hT = P.sb("hT", [128, 16, S], BF16)
    gbc = P.sb("gbc", [128, D], F32)
    P.dma("sp", gbc[:], bcast_row(I["norm_mix_e"]), [], [gbc], gbc)
    xst = [P.sb(f"xst{i}", [128, D], F32) for i in range(2)]
    hb = P.sb("hb", [128, D], BF16)
    tmp = (P.sb("junk", [128, D], BF16), P.sb("ssq", [128, 1], F32), P.sb("sd", [128, 2], F32), P.sb("rstd", [128, 1], F32))
    WKV = P.sb("WKV", [128, 16, 1072], BF16)
    wv = I["w_in_e"].rearrange("(kc p) c -> p kc c", p=128)
    load_w_cast(C, WKV, WKV[:, :, 0:256], wv[:, :, 1024:1280])
    load_w_cast(C, WKV, WKV[:, :, 256:1072], wv[:, :, 2304:3120])
    WQ = [P.sb(f"WQ{i}", [128, 16, 256], BF16) for i in range(2)]
    qblocks = [(which, col0, blk) for (which, col0) in (("a", 0), ("b", 1280)) for blk in range(4)]

    def load_q(n):
        which, col0, blk = qblocks[n]
        load_w_cast(C, WQ[n % 2], WQ[n % 2][:], wv[:, :, col0 + blk * 256:col0 + (blk + 1) * 256])
    load_q(0)
    load_q(1)
    qstage = [P.sb(f"qst{i}", [128, 512], BF16) for i in range(3)]
    norm_transpose(C, range(NT), gbc, hT, 0, xst, hb, tmp)
    if C.cut == "a":
        return
    kdst = [(kaT, 0), (None, 256), (None, 384), (ksA, 512), (kwT, 768)]
    kdst[1] = (rawk, 256)
    kdst[2] = (rawv, 384)
    nb = 0
    for (dst, c0) in kdst:
        for g in range(2):
            for tb in range(4):
                pb = B[nb % 2]
                for kc in range(16):
                    P.mm([pb], pb[0:64, :], [WKV, hT], WKV[:, kc, c0 + g * 64:c0 + (g + 1) * 64],
                         hT[:, kc, tb * 512:(tb + 1) * 512], kc == 0, kc == 15)
                P.copy("act" if nb % 2 == 0 else "dve", [dst], dst[0:64, g, tb * 512:(tb + 1) * 512], [pb], pb[0:64, :])
                nb += 1
    if C.cut == "b":
        return
    for t in range(NT):
        pbB, pb = C.O[t % 2], C.Of[t % 2]
        for (o0, c0, w) in ((0, 128, 128), (128, 640, 128), (256, 896, 176)):
            for kc in range(16):
                P.mm([pbB], pb[:, o0:o0 + w], [WKV, hT], hT[:, kc, t * 128:(t + 1) * 128], WKV[:, kc, c0:c0 + w], kc == 0, kc == 15)
        import os
        VV = int(os.environ.get("VV", "9"))
        for vi, vt in enumerate((va, vs, vw)):
            if VV < 1 or (VV < 2 and vi == 1):
                continue
            P.copy("dve" if vi != 1 else "act", [vt], vt[:, t, :, 0:64], [pbB],
                   pb[:, vi * 128:(vi + 1) * 128].rearrange("p (g d) -> p g d", g=2))
        if VV >= 3:
            P.act([gates], gates[:, t, :], [pbB], pb[:, 384:432], AF.Tanh, scale=0.5)
            P.ts("pool", [gates], gates[:, t, :], [gates], gates[:, t, :], 0.5, 0.5, ALU.mult, ALU.add)
    if C.cut == "c":
        return
    nb = 0
    for n, (which, col0, blk) in enumerate(qblocks):
        qs = qsa if which == "a" else qsb
        W = WQ[n % 2]
        for c4 in range(2):
            c = blk * 2 + c4
            for tb in range(4):
                pb = B[nb % 2]
                for kc in range(16):
                    P.mm([pb], pb[:], [W, hT], W[:, kc, c4 * 128:(c4 + 1) * 128], hT[:, kc, tb * 512:(tb + 1) * 512], kc == 0, kc == 15)
                st = qstage[nb % 3]
                if nb % 2 == 0:
                    P.op("act", lambda e, o=st[:], a=pb[:]: e.mul(o, a, 0.125), [pb], [st])
                else:
                    P.ts("dve", [st], st[:], [pb], pb[:], 0.125, None, ALU.mult)
                P.dma("sp", qs[c * 128:(c + 1) * 128, tb * 512:(tb + 1) * 512], st[:], [st], [qsB[(which, c, tb)]], st)
                nb += 1
        if n + 2 < len(qblocks):
            load_q(n + 2)
    if C.cut == "d":
        return
    P.release(m0)
    hid = P.sb("hid", [128, 2, 128], BF16)
    zt = [P.sb(f"cz{i}", [128, 128], F32) for i in range(4)]
    pbias = P.sb("pbias", [128, 1], F32)
    for kv, (w1n, w2n, posn, raw) in enumerate((("cmp_k_w1", "cmp_k_w2", "cmp_pos_k", rawk), ("cmp_v_w1", "cmp_v_w2", "cmp_pos_v", rawv))):
        w1 = P.sb(f"cw1{kv}", [64, 32, 256], BF16)
        w2 = P.sb(f"cw2{kv}", [128, 2, 64], BF16)
        posT = P.sb(f"cpos{kv}", [64, 32], BF16)
        P.dma("pool", w1[:], I[w1n].rearrange("(l d) h -> d l h", d=64), [], [w1], w1)
        P.dma("pool", w2[:], I[w2n].rearrange("(c p) d -> p c d", p=128), [], [w2], w2)
        P.dma("pool", posT[:], I[posn].rearrange("l d -> d l"), [], [posT], posT, allow_slow_non_contiguous=True)
        for g in range(2):
            for hc in range(2):
                pm, pp = B[0], B[1]
                for l in range(32):
                    P.mm([pm], pm[:, 0:127], [w1, raw], w1[:, l, hc * 128:(hc + 1) * 128], raw[:, g, l:l + 16 * 126 + 1:16], l == 0, l == 31)
                for l in range(32):
                    P.mm([pp], pp[:, 0:1], [w1, posT], w1[:, l, hc * 128:(hc + 1) * 128], posT[:, l:l + 1], l == 0, l == 31)
                z, z2, u, sg = zt
                P.copy("dve", [pbias], pbias[:], [pp], pp[:, 0:1])
                P.ts("dve", [z], z[:, 0:127], [pm, pbias], pm[:, 0:127], pbias[:, 0:1], None, ALU.add)
                P.tt("dve", [z2], z2[:, 0:127], [z], z[:, 0:127], z[:, 0:127], ALU.mult)
                P.ts("dve", [z2], z2[:, 0:127], [z2], z2[:, 0:127], 0.044715, 1.0, ALU.mult, ALU.add)
                P.tt("dve", [u], u[:, 0:127], [z2, z], z2[:, 0:127], z[:, 0:127], ALU.mult)
                P.act([sg], sg[:, 0:127], [u], u[:, 0:127], AF.Tanh, scale=0.7978845608028654)
                P.stt([sg], sg[:, 0:127], [sg, z], sg[:, 0:127], 1.0, z[:, 0:127], ALU.add, ALU.mult)
                P.ts("dve", [hid], hid[:, hc, 0:127], [sg], sg[:, 0:127], 0.5, None, ALU.mult)
            if kv == 0:
                poB, po = C.O[0], C.Of[0]
                for hc in range(2):
                    P.mm([poB], po[0:64, 0:127], [w2, hid], w2[:, hc, :], hid[:, hc, 0:127], hc == 0, hc == 1)
                P.copy("dve", [kcT], kcT[:, g, 0:127], [poB], po[0:64, 0:127])
            else:
                poB, po = C.O[1], C.Of[1]
                for hc in range(2):
                    P.mm([poB], po[0:127, 0:64], [hid, w2], hid[:, hc, 0:127], w2[:, hc, :], hc == 0, hc == 1)
                P.copy("dve", [vcA], vcA[0:127, g, 0:64], [poB], po[0:127, 0:64])
    if "l0proj" in C.dbg_want:
        dbg_dump(C, "kaT", kaT, kaT[:], [64, 2, S], BF16)
        dbg_dump(C, "ksA", ksA, ksA[:], [96, 2, S], BF16)
        dbg_dump(C, "va", va, va[:], [128, NT, 2, 65], BF16)
        dbg_dump(C, "kcT", kcT, kcT[:], [64, 2, 128], BF16)
        dbg_dump(C, "vcA", vcA, vcA[:], [128, 2, 97], BF16)
        dbg_dump(C, "gates", gates, gates[:], [128, NT, 48], F32)
    P.release(m_raw)

    if C.cut == "e":
        return
    WO = P.sb("WO", [128, 16, D], BF16)
    wov = I["w_out_e"].rearrange("(kc p) c -> p kc c", p=128)
    for q4 in range(4):
        load_w_cast(C, WO, WO[:, q4 * 4:(q4 + 1) * 4, :], wov[:, q4 * 4:(q4 + 1) * 4, :])
    biasA = P.sb("biasA", [128, 2, 16, 128], BF16)
    biasB = P.sb("biasB", [128, 4, 16, 128], BF16)
    P.dma("pool", biasA[:], I["biasA"], [], [biasA], biasA)
    P.dma("pool", biasB[:], I["biasB"], [], [biasB], biasB)
    keepadd = P.sb("keepadd", [128, NT, 2, 32], F32)
    P.dma("sp", keepadd[:], I["keepadd"], [], [keepadd], keepadd)
    cfar = P.sb("cfar", [128, NT, 16], F32)
    P.dma("sp", cfar[64:96, :, :], I["cfar"], [], [cfar], cfar)
    cb = [P.sb(f"cb{i}", [128, 16, 128], BF16) for i in range(2)]
    qa = [P.sb(f"qa{i}", [64, 16, 128], BF16) for i in range(2)]
    qb = [P.sb(f"qb{i}", [96, 16, 128], BF16) for i in range(2)]
    PT = [P.sb(f"PT{i}", [128, 512], BF16) for i in range(4)]
    TM = P.sb("TM", [128, 128], BF16)
    P.memset("dve", [TM], TM[:], 0.0)
    ocat = P.sb("ocat", [128, D], BF16)
    oT = P.sb("oT", [128, 16, 128], BF16)
    xst = [P.sb(f"xat{i}", [128, D], F32) for i in range(2)]
    acc = P.sb("oacc", [128, 16, 64], F32)
    tmpo = P.sb("otmp", [128, 8, 64], F32)
    sm = {k: P.sb(f"sm_{k}", shp, F32) for k, shp in dict(z=[128, 8], rz=[128, 8], w=[128, 8], ps3=[128, 8, 32],
                                                          pslc=[128, 32], sc=[128, 32], m8=[128, 8], sel=[128, 32]).items()}
    ST = [B[0], B[1]]
    Ot = C.O
    osl = [0]

    def attn_group(items, PVrhs, nheads_cols, g, Ob):
        pend = []
        for idx, it in enumerate(items):
            stb, sta = STB[st3[0] % 3]
            st3[0] += 1
            mml, (half, kt, nrows, first, last) = it
            for mi, (l, r, rb) in enumerate(mml):
                P.mm([stb], sta[0:nrows, :], rb, l, r, mi == 0, mi == len(mml) - 1)
            pend.append((stb, sta, half, kt, nrows, first, last, PVrhs, nheads_cols, g, Ob))
            if len(pend) > 2:
                finish(*pend.pop(0))
        while pend:
            finish(*pend.pop(0))

    def finish(stb, sta, half, kt, nrows, first, last, PVrhs, ncols, g, Ob):
        pt = PT[pt_ctr[0] % len(PT)]
        pt_ctr[0] += 1
        P.act([pt], pt[0:nrows, :], [stb], sta[0:nrows, :], AF.Exp)
        vt = PVrhs
        for hl in range(4):
            h8 = half * 4 + hl
            if vt is vcA:
                rhs = vcA[0:nrows, g, 0:ncols]
            else:
                rhs = vt[:, kt, g, 0:ncols]
            P.mm([Ob], Ob[:, h8, 0:ncols], [pt, vt], pt[0:nrows, hl * 128:(hl + 1) * 128], rhs, first and hl == 0, last and hl == 3, skip=True)

    st_ctr = [0]
    st3 = [0]
    pt_ctr = [0]
    STB = [(B[0], B[0].t), (B[1], B[1].t), (C.TR[0], C.TR[0].t[:, :].bitcast(F32))]
    def load_tile_inputs(i):
        qat, qbt, cbt = qa[i % 2], qb[i % 2], cb[i % 2]
        tb = i // 4
        P.dma("sp", qat[:], qsa.rearrange("(h d) t -> d h t", d=64)[:, :, i * 128:(i + 1) * 128],
              [qsB[("a", c, tb)] for c in range(8)], [qat], qat)
        P.dma("sp", qbt[0:64, :, :], qsb.rearrange("(h d) t -> d h t", d=64)[:, :, i * 128:(i + 1) * 128],
              [qsB[("b", c, tb)] for c in range(8)], [qbt], qbt)
        P.dma("pool", cbt[:, :, :], I["cbiasU"][120 - 8 * i:248 - 8 * i, :, :], [], [cbt], cbt)

    def out_proj(i):
        xa = xst[i % 2]
        oc = ocats[i % 2]
        for q4 in range(4):
            trb, trap = C.TR[1], C.TRap[1]
            for k in range(4):
                kc = q4 * 4 + k
                P.tr([trb], trap[:, k * 128:(k + 1) * 128], [oc, C.ident], oc[:, kc * 128:(kc + 1) * 128], C.ident[:])
            P.copy("act" if q4 % 2 == 0 else "dve", [oT], oT[:, q4 * 4:q4 * 4 + 4, :], [trb], trap.rearrange("p (a b) -> p a b", a=4))
        for db in range(4):
            pb = ST[st_ctr[0] % 2]
            st_ctr[0] += 1
            for kc in range(16):
                P.mm([pb], pb[:], [oT, WO], oT[:, kc, :], WO[:, kc, db * 512:(db + 1) * 512], kc == 0, kc == 15)
            P.tt("dve", [xa], xa[:, db * 512:(db + 1) * 512], [pb, xa], pb[:], xa[:, db * 512:(db + 1) * 512], ALU.add)
        P.dma("sp", C.xs[i * 128:(i + 1) * 128, :], xa[:], [xa], [C.xsB[i]], xa)

    ocats = [ocat, P.sb("ocat2", [128, D], BF16)]
    load_tile_inputs(0)
    for i in range(NT):
        if C.cut is not None and C.cut.startswith("f") and i >= int(C.cut[1:]):
            return
        qat, qbt, cbt = qa[i % 2], qb[i % 2], cb[i % 2]
        ocat = ocats[i % 2]
        if i + 1 < NT:
            load_tile_inputs(i + 1)
        load_x_tile(C, xst[i % 2], i)
        xa = xst[i % 2]
        bo = 0
        for g in range(2):
            Ob = Ot[bo % 2]
            bo += 1
            items = []
            for half in range(2):
                hs = slice(g * 8 + half * 4, g * 8 + half * 4 + 4)
                mml = [(kcT[0:64, g, 0:127], qbt[0:64, hs, :], [kcT, qbt]),
                       (C.ident[0:127, 0:127], cbt[0:127, hs, :], [C.ident, cbt])]
                items.append((mml, (half, 0, 127, True, True)))
            attn_group(items, vcA, 97, g, Ob)
            z, rz, w = sm["z"], sm["rz"], sm["w"]
            P.ts("dve", [z], z[:], [Ob], Ob[:, :, 64], 1e-30, None, ALU.max)
            P.recip([rz], rz[:], [z], z[:])
            P.tt("dve", [w], w[:], [rz, gates], rz[:], gates[:, i, g * 24:(g + 1) * 24].rearrange("p (h k) -> p h k", k=3)[:, :, 0], ALU.mult)
            P.tt("dve", [acc], acc[:, g * 8:(g + 1) * 8, :], [Ob, w], Ob[:, :, 0:64], w[:].unsqueeze(2).to_broadcast([128, 8, 64]), ALU.mult)
            ps3 = sm["ps3"]
            P.tt("dve", [ps3], ps3[:], [Ob, rz], Ob[:, :, 65:97], rz[:].unsqueeze(2).to_broadcast([128, 8, 32]), ALU.mult)
            pslc, sc, m8, sel = sm["pslc"], sm["sc"], sm["m8"], sm["sel"]
            P.op("dve", lambda e, o=pslc[:], a=ps3[:].rearrange("p h n -> p n h"): e.tensor_reduce(out=o, in_=a, axis=AX.X, op=ALU.add), [ps3], [pslc])
            P.tt("dve", [sc], sc[:], [pslc, keepadd], pslc[:], keepadd[:, i, 0, :], ALU.mult)
            P.tt("dve", [sc], sc[:], [sc, keepadd], sc[:], keepadd[:, i, 1, :], ALU.add)
            P.op("dve", lambda e, o=m8[:], a=sc[:]: e.max(out=o, in_=a), [sc], [m8])
            P.ts("dve", [sel], sel[:], [sc, m8], sc[:], m8[:, 7:8], None, ALU.is_ge)
            P.ts("dve", [TM], TM[:, 64:96], [sel], sel[:], 1.0, -NEGM, ALU.subtract, ALU.mult)
            trb, trap = C.TR[1], C.TRap[1]
            P.tr([trb], trap[:, 0:128], [TM, C.ident], TM[:], C.ident[:])
            P.tt("dve", [qbt], qbt[64:96, g * 8:(g + 1) * 8, :], [trb, cfar],
                 trap[64:96, 0:128].unsqueeze(1).to_broadcast([32, 8, 128]),
                 cfar[64:96, i, g * 8:(g + 1) * 8].unsqueeze(2).to_broadcast([32, 8, 128]), ALU.add)
        for g in range(2):
            Ob = Ot[bo % 2]
            bo += 1
            kts = [kt for kt in (i - 1, i) if kt >= 0]
            items = []
            for half in range(2):
                hs = slice(g * 8 + half * 4, g * 8 + half * 4 + 4)
                for kt in kts:
                    kind = 0 if kt == i else 1
                    mml = [(kaT[0:64, g, kt * 128:(kt + 1) * 128], qat[0:64, hs, :], [kaT, qat]),
                           (C.ident[:], biasA[:, kind, hs, :], [C.ident, biasA])]
                    items.append((mml, (half, kt, 128, kt == kts[0], kt == kts[-1])))
            attn_group(items, va, 65, g, Ob)
            z, rz = sm["z"], sm["rz"]
            P.tt("dve", [z], z[:], [Ob, sinkexp], Ob[:, :, 64], sinkexp[:, g * 8:(g + 1) * 8], ALU.add)
            P.recip([rz], rz[:], [z], z[:])
            P.tt("dve", [ocat], ocat[:, g * 512:(g + 1) * 512].rearrange("p (h d) -> p h d", d=64), [Ob, rz], Ob[:, :, 0:64],
                 rz[:].unsqueeze(2).to_broadcast([128, 8, 64]), ALU.mult)
        if i > 0:
            out_proj(i - 1)
        for br in ("win", "slc"):
            for g in range(2):
                Ob = Ot[bo % 2]
                bo += 1
                items = []
                kts = list(range(max(0, i - 4), i + 1)) if br == "win" else list(range(0, i + 1))
                for half in range(2):
                    hs = slice(g * 8 + half * 4, g * 8 + half * 4 + 4)
                    for kt in kts:
                        dk = i - kt
                        if br == "win":
                            kind = {0: 0, 1: 1, 2: 2, 3: 2, 4: 3}[dk]
                            mml = [(kwT[0:64, g, kt * 128:(kt + 1) * 128], qbt[0:64, hs, :], [kwT, qbt]),
                                   (C.ident[:], biasB[:, kind, hs, :], [C.ident, biasB])]
                        else:
                            mml = [(ksA[0:96, g, kt * 128:(kt + 1) * 128], qbt[0:96, hs, :], [ksA, qbt])]
                            if dk <= 1:
                                mml.append((C.ident[:], biasB[:, dk, hs, :], [C.ident, biasB]))
                        items.append((mml, (half, kt, 128, kt == kts[0], kt == kts[-1])))
                attn_group(items, vw if br == "win" else vs, 65, g, Ob)
                rz, w = sm["rz"], sm["w"]
                P.recip([rz], rz[:], [Ob], Ob[:, :, 64])
                gi = 2 if br == "win" else 1
                P.tt("dve", [w], w[:], [rz, gates], rz[:], gates[:, i, g * 24:(g + 1) * 24].rearrange("p (h k) -> p h k", k=3)[:, :, gi], ALU.mult)
                P.tt("dve", [tmpo], tmpo[:], [Ob, w], Ob[:, :, 0:64], w[:].unsqueeze(2).to_broadcast([128, 8, 64]), ALU.mult)
                if br == "win":
                    P.tt("pool", [acc], acc[:, g * 8:(g + 1) * 8, :], [acc, tmpo], acc[:, g * 8:(g + 1) * 8, :], tmpo[:], ALU.add)
                else:
                    P.tt("pool", [ocat], ocat[:, 1024 + g * 512:1024 + (g + 1) * 512].rearrange("p (h d) -> p h d", d=64), [acc, tmpo],
                         acc[:, g * 8:(g + 1) * 8, :], tmpo[:], ALU.add)
        if "ocat" in C.dbg_want:
            dbg_dump(C, f"ocat{i}", ocat, ocat[:], [128, D], BF16)
    out_proj(NT - 1)
    C.x_src = C.xs
    C.x_srcB = C.xsB
    P.release(m_persist)


def dbg_dump(C, name, buf, ap, shape, dtype):
    P = C.P
    d = C.nc.dram_tensor("dbg_" + name, list(shape), dtype, kind="ExternalOutput").ap()
    C.dbg[name] = (shape, dtype)
    oid = P.dma("sp", d, ap, [buf], [], buf)
    C.out_ops.append(oid)


def mlp(C, layer, final):
    P, I, B = C.P, C.I, C.B
    m = P.mark()
    gbc = P.sb("gbcm", [128, D], F32)
    gfin = gbc
    TB = 8
    hT = P.sb("hTm", [128, 16, TB * 128], BF16)
    yacc = P.sb("yacc", [128, TB, D], F32)
    xst = [P.sb(f"xsm{i}", [128, D], F32) for i in range(2)]
    hb = P.sb("hbm", [128, D], BF16)
    tmp = (hb, P.sb("ssqm", [128, 1], F32), P.sb("sdm", [128, 2], F32), P.sb("rstdm", [128, 1], F32))
    wu = [P.sb(f"wu{i}", [128, 16, 512], BF16) for i in range(2)]
    wd = [P.sb(f"wd{i}", [128, 4, D], BF16) for i in range(2)]
    uT = [P.sb(f"uT{i}", [128, 4, 512], BF16) for i in range(2)]
    rl = [P.sb(f"rl{i}", [128, 512], F32) for i in range(2)]
    wuv = I[f"w_up{layer}"].rearrange("(kc p) f -> p kc f", p=128)
    wdv = I[f"w_down{layer}"].rearrange("(fc p) d -> p fc d", p=128)
    nu = 0
    no = 0
    NG = 16

    def load_group(gi):
        wub, wdb = wu[gi % 2], wd[gi % 2]
        load_w_cast(C, wub, wub[:], wuv[:, :, gi * 512:(gi + 1) * 512])
        load_w_cast(C, wdb, wdb[:], wdv[:, gi * 4:(gi + 1) * 4, :])

    def up(gi, tb):
        nonlocal nu
        wub = wu[gi % 2]
        u = uT[(gi * 2 + tb) % 2]
        for fc in range(4):
            pb = B[nu % 2]
            r = rl[nu % 2]
            nu += 1
            for kc in range(16):
                P.mm([pb], pb[:], [wub, hT], wub[:, kc, fc * 128:(fc + 1) * 128], hT[:, kc, tb * 512:(tb + 1) * 512], kc == 0, kc == 15)
            P.act([r], r[:], [pb], pb[:], AF.Relu)
            P.act([u], u[:, fc, :], [r], r[:], AF.Square)

    def down(gi, tb):
        nonlocal no
        wdb = wd[gi % 2]
        u = uT[(gi * 2 + tb) % 2]
        for tt in range(4):
            j = tb * 4 + tt
            for dbp in range(2):
                ob, of = C.O[no % 2], C.Of[no % 2]
                no += 1
                for dbi in range(2):
                    db = dbp * 2 + dbi
                    for fc in range(4):
                        P.mm([ob], of[:, dbi * 512:(dbi + 1) * 512], [u, wdb], u[:, fc, tt * 128:(tt + 1) * 128],
                             wdb[:, fc, db * 512:(db + 1) * 512], fc == 0, fc == 3)
                ys = yacc[:, j, dbp * 1024:(dbp + 1) * 1024]
                if gi == 0:
                    P.copy("dve", [yacc], ys, [ob], of[:, :])
                else:
                    P.tt("dve", [yacc], ys, [ob, yacc], of[:, :], ys, ALU.add)

    for blk in range(NT // TB):
        P.dma("sp", gbc[:], bcast_row(I["norm_mlp"][layer:layer + 1, :]), [], [gbc], gbc)
        load_group(0)
        load_group(1)
        norm_transpose(C, range(blk * TB, (blk + 1) * TB), gbc, hT, 0, xst, hb, tmp)
        units = [(gi, tb) for gi in range(NG) for tb in range(TB // 4)]
        up(*units[0])
        for k, (gi, tb) in enumerate(units):
            if k + 1 < len(units):
                up(*units[k + 1])
            down(gi, tb)
            if tb == TB // 4 - 1 and 1 <= gi + 1 and gi + 2 < NG:
                load_group(gi + 2)
        if final:
            P.dma("sp", gfin[:], bcast_row(I["norm_final"]), [], [gfin], gfin)
        for j in range(TB):
            t = blk * TB + j
            if not final:
                P.dma("pool", C.xs[t * 128:(t + 1) * 128, :], yacc[:, j, :], [yacc, C.xsB[t]], [C.xsB[t]], yacc, accum_op=ALU.add)
                continue
            xa = xst[j % 2]
            load_x_tile(C, xa, t)
            P.tt("dve", [xa], xa[:], [xa, yacc], xa[:], yacc[:, j, :], ALU.add)
            if True:
                ot = hb
                yo = yacc[:, j, :]
                norm_rows(C, xa[:], xa, D, gfin, yo, yacc, tmp)
                oid = P.dma("sp", C.out[t * 128:(t + 1) * 128, :], yo, [yacc], [], yacc)
                C.out_ops.append(oid)
    C.x_src = C.xs
    C.x_srcB = C.xsB
    P.release(m)


def layer1_mixer(C):
    P, I, B = C.P, C.I, C.B
    m_all = P.mark()
    SCALE = 192 ** -0.5
    cqnT = P.sb("cqnT", [128, 6, S], BF16)
    ckvT = P.sb("ckvT", [128, 4, S], BF16)
    krT = P.sb("krT", [64, S], BF16)
    cs = P.sb("cs", [64, 2, S], F32)
    P.dma("sp", cs[:], I["cs"], [], [cs], cs)
    mmask = P.sb("mmask", [128, 128], BF16)
    P.dma("pool", mmask[:], I["mlamask"], [], [mmask], mmask)
    m0 = P.mark()
    hT = P.sb("hT1", [128, 16, S], BF16)
    gbc = P.sb("gbc1", [128, D], F32)
    P.dma("sp", gbc[:], bcast_row(I["norm_mix_o"]), [], [gbc], gbc)
    qg = P.sb("qg", [128, 768], F32)
    kg = P.sb("kg", [128, 512], F32)
    P.dma("sp", qg[:], bcast_row(I["q_norm"]), [], [qg], qg)
    P.dma("sp", kg[:], bcast_row(I["kv_norm"]), [], [kg], kg)
    hb = P.sb("hb1", [128, D], BF16)
    tmp = (P.sb("junk1", [128, D], BF16), P.sb("ssq1", [128, 1], F32), P.sb("sd1", [128, 2], F32), P.sb("rstd1", [128, 1], F32))
    WI = P.sb("WI", [128, 16, 1344], BF16)
    wiv = I["w_in_o"].rearrange("(kc p) c -> p kc c", p=128)
    load_w_cast(C, WI, WI[:, 0:8, :], wiv[:, 0:8, :])
    load_w_cast(C, WI, WI[:, 8:16, :], wiv[:, 8:16, :])
    WIr = P.sb("WIr", [128, 16, 64], BF16)
    P.ts("pool", [WIr], WIr[:, :, 0:32], [WI], WI[:, :, 1312:1344], -1.0, None, ALU.mult)
    P.copy("pool", [WIr], WIr[:, :, 32:64], [WI], WI[:, :, 1280:1312])
    m_x = P.mark()
    xst = [P.sb(f"xs1{i}", [128, D], F32) for i in range(2)]
    norm_transpose(C, range(NT), gbc, hT, 0, xst, hb, tmp)
    P.release(m_x)
    cn = P.sb("cn", [128, 1280], BF16)
    ssb = P.sb("ssb", [128, 4], F32)
    for t in range(NT):
        pa, pbk, pc = B[0], B[1], C.O[t % 2]
        paa, pba, pca = B[0].t, B[1].t, C.Of[t % 2]
        for (pb, pap, c0, w) in ((pa, paa, 0, 384), (pbk, pba, 384, 384), (pc, pca, 768, 512)):
            for kc in range(16):
                P.mm([pb], pap[:, 0:w], [hT, WI], hT[:, kc, t * 128:(t + 1) * 128], WI[:, kc, c0:c0 + w], kc == 0, kc == 15)
        junk = tmp[0]
        P.act([junk, ssb], junk[:, 0:384], [pa], pa[:, 0:384], AF.Square, accum_out=ssb[:, 0:1])
        P.act([junk, ssb], junk[:, 384:768], [pbk], pbk[:, 0:384], AF.Square, accum_out=ssb[:, 1:2])
        P.act([junk, ssb], junk[:, 768:1280], [pc], pca[:, 0:512], AF.Square, accum_out=ssb[:, 2:3])
        sd = tmp[2]
        rq = tmp[3]
        rk = tmp[1]
        P.tt("dve", [ssb], ssb[:, 3:4], [ssb], ssb[:, 0:1], ssb[:, 1:2], ALU.add)
        P.ts("dve", [sd], sd[:, 0:1], [ssb], ssb[:, 3:4], 1.0 / 768, EPS, ALU.mult, ALU.add)
        P.ts("dve", [sd], sd[:, 1:2], [ssb], ssb[:, 2:3], 1.0 / 512, EPS, ALU.mult, ALU.add)
        P.tt("pool", [rq], rq[:, 0:1], [sd, C.neghalf], sd[:, 0:1], C.neghalf[:, 0:1], ALU.pow)
        P.tt("pool", [rk], rk[:, 0:1], [sd, C.neghalf], sd[:, 1:2], C.neghalf[:, 0:1], ALU.pow)
        P.stt([cn], cn[:, 0:384], [pa, rq, qg], pa[:, 0:384], rq[:, 0:1], qg[:, 0:384], ALU.mult, ALU.mult)
        P.stt([cn], cn[:, 384:768], [pbk, rq, qg], pbk[:, 0:384], rq[:, 0:1], qg[:, 384:768], ALU.mult, ALU.mult)
        P.stt([cn], cn[:, 768:1280], [pc, rk, kg], pca[:, 0:512], rk[:, 0:1], kg[:, 0:512], ALU.mult, ALU.mult)
        for grp, (dstT, nck, cbase) in enumerate(((cqnT, 4, 0), (cqnT, 2, 512), (ckvT, 4, 768))):
            trb, trap = C.TR[grp % 2], C.TRap[grp % 2]
            for k in range(nck):
                P.tr([trb], trap[:, k * 128:(k + 1) * 128], [cn, C.ident], cn[:, cbase + k * 128:cbase + (k + 1) * 128], C.ident[:])
            k0 = 0 if grp != 1 else 4
            P.copy("act" if grp % 2 == 0 else "dve", [dstT], dstT[:, k0:k0 + nck, t * 128:(t + 1) * 128], [trb],
                   trap[:, 0:nck * 128].rearrange("p (a b) -> p a b", a=nck))
    t1 = P.sb("rt1", [64, 512], F32)
    t2 = P.sb("rt2", [64, 512], F32)
    for tb in range(4):
        p1, p2 = B[0], B[1]
        for kc in range(16):
            P.mm([p1], p1[0:64, :], [WI, hT], WI[:, kc, 1280:1344], hT[:, kc, tb * 512:(tb + 1) * 512], kc == 0, kc == 15)
        for kc in range(16):
            P.mm([p2], p2[0:64, :], [WIr, hT], WIr[:, kc, :], hT[:, kc, tb * 512:(tb + 1) * 512], kc == 0, kc == 15)
        P.tt("dve", [t1], t1[:], [p1, cs], p1[0:64, :], cs[:, 0, tb * 512:(tb + 1) * 512], ALU.mult)
        P.tt("dve", [t2], t2[:], [p2, cs], p2[0:64, :], cs[:, 1, tb * 512:(tb + 1) * 512], ALU.mult)
        P.tt("pool", [krT], krT[:, tb * 512:(tb + 1) * 512], [t1, t2], t1[:], t2[:], ALU.add)
    if "l1lat" in C.dbg_want:
        dbg_dump(C, "cqnT", cqnT, cqnT[:], [128, 6, S], BF16)
        dbg_dump(C, "ckvT", ckvT, ckvT[:], [128, 4, S], BF16)
        dbg_dump(C, "krT", krT, krT[:], [64, S], BF16)
    P.release(m0)
    oT = P.sb("oT1", [128, 16, S], BF16)
    m_after_oT = P.mark()
    wq = [P.sb(f"wq{i}", [128, 6, 192], BF16) for i in range(2)]
    wqr = [P.sb(f"wqr{i}", [128, 6, 64], BF16) for i in range(2)]
    wkv = [P.sb(f"wkv{i}", [128, 4, 256], BF16) for i in range(2)]
    kT = [P.sb(f"kTh{i}", [128, S], BF16) for i in range(2)]
    vh = [P.sb(f"vh{i}", [128, NT, 129], BF16) for i in range(2)]
    for v_ in vh:
        P.memset("pool", [v_], v_[:, :, 128:129], 1.0)
    qn = [P.sb(f"qnh{i}", [128, S], BF16) for i in range(2)]
    qr = [P.sb(f"qrh{i}", [64, S], BF16) for i in range(2)]
    PT = [P.sb(f"PT1{i}", [128, 512], BF16) for i in range(4)]
    STB = [(B[0], B[0].t), (B[1], B[1].t), (C.TR[0], C.TR[0].t[:, :].bitcast(F32))]
    ob = [P.sb(f"ob{i}", [128, 4, 128], BF16) for i in range(2)]
    rz = [P.sb(f"rz1{i}", [128, 4], F32) for i in range(2)]
    wqv = I["w_q_up"].rearrange("(kc p) c -> p kc c", p=128)
    wkvv = I["w_kv_up"].rearrange("(kc p) c -> p kc c", p=128)
    stc = 0
    ptc = 0
    def load_head(h):
        s2 = h % 2
        load_w_cast(C, wq[s2], wq[s2][:], wqv[:, :, h * 192:(h + 1) * 192])
        load_w_cast(C, wkv[s2], wkv[s2][:], wkvv[:, :, h * 256:(h + 1) * 256])
    load_head(0)
    for h in range(16):
        s2 = h % 2
        if h + 1 < 16:
            load_head(h + 1)
        P.ts("pool", [wqr[s2]], wqr[s2][:, :, 0:32], [wq[s2]], wq[s2][:, :, 160:192], -1.0, None, ALU.mult)
        P.copy("pool", [wqr[s2]], wqr[s2][:, :, 32:64], [wq[s2]], wq[s2][:, :, 128:160])
        pjB = C.TR[0]
        pj = C.TR[0].t[:, :].bitcast(F32)
        for tb in range(4):
            for kc in range(4):
                P.mm([pjB], pj[:], [wkv[s2], ckvT], wkv[s2][:, kc, 0:128], ckvT[:, kc, tb * 512:(tb + 1) * 512], kc == 0, kc == 3)
            P.copy("dve", [kT[s2]], kT[s2][:, tb * 512:(tb + 1) * 512], [pjB], pj[:])
        for t4 in range(4):
            for tt in range(4):
                t = t4 * 4 + tt
                for kc in range(4):
                    P.mm([pjB], pj[:, tt * 128:(tt + 1) * 128], [ckvT, wkv[s2]], ckvT[:, kc, t * 128:(t + 1) * 128], wkv[s2][:, kc, 128:256], kc == 0, kc == 3)
            P.copy("dve", [vh[s2]], vh[s2][:, t4 * 4:(t4 + 1) * 4, 0:128], [pjB], pj[:].rearrange("p (a b) -> p a b", a=4))
        for tb in range(4):
            for kc in range(6):
                P.mm([pjB], pj[:], [wq[s2], cqnT], wq[s2][:, kc, 0:128], cqnT[:, kc, tb * 512:(tb + 1) * 512], kc == 0, kc == 5)
            P.copy("dve", [qn[s2]], qn[s2][:, tb * 512:(tb + 1) * 512], [pjB], pj[:])
            for kc in range(6):
                P.mm([pjB], pj[0:64, :], [wq[s2], cqnT], wq[s2][:, kc, 128:192], cqnT[:, kc, tb * 512:(tb + 1) * 512], kc == 0, kc == 5)
            P.tt("dve", [t1], t1[:], [pjB, cs], pj[0:64, :], cs[:, 0, tb * 512:(tb + 1) * 512], ALU.mult)
            for kc in range(6):
                P.mm([pjB], pj[0:64, :], [wqr[s2], cqnT], wqr[s2][:, kc, :], cqnT[:, kc, tb * 512:(tb + 1) * 512], kc == 0, kc == 5)
            P.tt("dve", [t2], t2[:], [pjB, cs], pj[0:64, :], cs[:, 1, tb * 512:(tb + 1) * 512], ALU.mult)
            P.tt("pool", [qr[s2]], qr[s2][:, tb * 512:(tb + 1) * 512], [t1, t2], t1[:], t2[:], ALU.add)
        for Qb in range(4):
            Ob, Of = C.O[Qb % 2], C.Of[Qb % 2]
            nkt = 4 * Qb + 4
            pend = []

            def fin(stb, sta, kt, c0, Qb=Qb, Ob=Ob, Of=Of, s2=s2):
                nonlocal ptc
                pt = PT[ptc % len(PT)]
                ptc += 1
                P.act([pt], pt[:, c0:512], [stb], sta[:, c0:512], AF.Exp, scale=SCALE)
                for jj in range(c0 // 128, 4):
                    last = (kt == 4 * Qb + jj)
                    P.mm([Ob], Of[:, jj * 256:jj * 256 + 129], [pt, vh[s2]], pt[:, jj * 128:(jj + 1) * 128], vh[s2][:, kt, :], kt == 0 and jj % 2 == 0, last, skip=True)

            for kt in range(nkt):
                stb, sta = STB[stc % 3]
                stc += 1
                c0 = max(0, kt - 4 * Qb) * 128
                q0 = Qb * 512
                kl = kT[s2][:, kt * 128:(kt + 1) * 128]
                krl = krT[0:64, kt * 128:(kt + 1) * 128]
                if kt >= 4 * Qb:
                    P.mm([stb], sta[:, c0:c0 + 128], [C.ident, mmask], C.ident[:], mmask[:], True, False)
                    P.mm([stb], sta[:, c0:c0 + 128], [kT[s2], qn[s2]], kl, qn[s2][:, q0 + c0:q0 + c0 + 128], False, False)
                    P.mm([stb], sta[:, c0:c0 + 128], [krT, qr[s2]], krl, qr[s2][0:64, q0 + c0:q0 + c0 + 128], False, True)
                    c1 = c0 + 128
                else:
                    c1 = c0
                if c1 < 512:
                    P.mm([stb], sta[:, c1:512], [kT[s2], qn[s2]], kl, qn[s2][:, q0 + c1:q0 + 512], True, False)
                    P.mm([stb], sta[:, c1:512], [krT, qr[s2]], krl, qr[s2][0:64, q0 + c1:q0 + 512], False, True)
                pend.append((stb, sta, kt, c0))
                if len(pend) > 2:
                    fin(*pend.pop(0))
            while pend:
                fin(*pend.pop(0))
            rzb = rz[Qb % 2]
            obb = ob[Qb % 2]
            O4 = Of.rearrange("p (a b) -> p a b", a=4)
            P.recip([rzb], rzb[:], [Ob], O4[:, :, 128])
            P.tt("dve", [obb], obb[:], [Ob, rzb], O4[:, :, 0:128], rzb[:].unsqueeze(2).to_broadcast([128, 4, 128]), ALU.mult)
            trb, trap = C.TR[1], C.TRap[1]
            for jj in range(4):
                P.tr([trb], trap[:, jj * 128:(jj + 1) * 128], [obb, C.ident], obb[:, jj, :], C.ident[:])
            P.copy("act", [oT], oT[:, h, Qb * 512:(Qb + 1) * 512], [trb], trap[:, :])
    if "l1o" in C.dbg_want:
        dbg_dump(C, "oT1", oT, oT[:], [128, 16, S], BF16)
    P.release(m_after_oT)
    WO = P.sb("WO1", [128, 16, D], BF16)
    wov = I["w_out_o"].rearrange("(kc p) c -> p kc c", p=128)
    for q4 in range(4):
        load_w_cast(C, WO, WO[:, q4 * 4:(q4 + 1) * 4, :], wov[:, q4 * 4:(q4 + 1) * 4, :])
    xst = [P.sb(f"xo1{i}", [128, D], F32) for i in range(2)]
    for t in range(NT):
        xa = xst[t % 2]
        load_x_tile(C, xa, t)
        for db in range(4):
            pb = B[db % 2]
            for kc in range(16):
                P.mm([pb], pb[:], [oT, WO], oT[:, kc, t * 128:(t + 1) * 128], WO[:, kc, db * 512:(db + 1) * 512], kc == 0, kc == 15)
            P.tt("dve", [xa], xa[:, db * 512:(db + 1) * 512], [pb, xa], pb[:], xa[:, db * 512:(db + 1) * 512], ALU.add)
        P.dma("sp", C.xs[t * 128:(t + 1) * 128, :], xa[:], [xa], [C.xsB[t]], xa)
    C.x_src = C.xs
    C.x_srcB = C.xsB
    P.release(m_all)


_CACHE = {}


def _prep_inputs(inputs, used=None):
    tabs = _host_tables(np.asarray(inputs["rel_bias"], np.float32))
    shared = dict(tabs)
    sq = lambda k: np.ascontiguousarray(np.asarray(inputs[k], np.float32)[0])
    for k in ("w_in_e", "cmp_pos_k", "cmp_pos_v", "cmp_k_w1", "cmp_k_w2", "cmp_v_w1", "cmp_v_w2", "w_out_e",
              "w_in_o", "w_q_up", "w_kv_up", "w_out_o"):
        shared[k] = sq(k)
    for k in ("norm_mix_e", "sinks", "norm_mix_o", "q_norm", "kv_norm"):
        shared[k] = np.ascontiguousarray(np.asarray(inputs[k], np.float32).reshape(1, -1))
    shared["norm_mlp"] = np.ascontiguousarray(np.asarray(inputs["norm_mlp"], np.float32))
    shared["norm_final"] = np.ascontiguousarray(np.asarray(inputs["norm_final"], np.float32).reshape(1, -1))
    for l in range(2):
        shared[f"w_up{l}"] = np.ascontiguousarray(np.asarray(inputs["w_up"], np.float32)[l])
        shared[f"w_down{l}"] = np.ascontiguousarray(np.asarray(inputs["w_down"], np.float32)[l])
    x = np.asarray(inputs["x"], np.float32)
    in_maps = []
    for c in range(NCORES):
        m = dict(shared)
        m["x"] = np.ascontiguousarray(x[c])
        if used is not None:
            m = {k: v for k, v in m.items() if k in used}
        in_maps.append(m)
    return in_maps


def kernel(**inputs):
    if "nc" not in _CACHE:
        _CACHE["nc"], _CACHE["P"] = build_program()
    nc = _CACHE["nc"]
    in_maps = _prep_inputs(inputs, _CACHE["P"].used_inputs)
    res = run_bass_kernel_spmd(nc, in_maps, core_ids=list(range(NCORES)))
    return np.stack([np.asarray(r["out"], np.float32) for r in res.results], axis=0)
```

```python
import math
import numpy as np
import concourse.bass as bass
import concourse.mybir as mybir
from concourse.bass_utils import run_bass_kernel_spmd

F32 = mybir.dt.float32
BF16 = mybir.dt.bfloat16
AF = mybir.ActivationFunctionType
ALU = mybir.AluOpType
AX = mybir.AxisListType

S = 2048
D = 2048
NT = 16
DFF = 8192
NEGM = -30000.0
EPS = 1e-6
NCORES = 8


class Buf:
    __slots__ = ("name", "t", "last_w", "readers", "off", "size", "psum")

    def __init__(self, name, t=None):
        self.psum = False
        self.name = name
        self.t = t
        self.last_w = None
        self.readers = []
        self.off = None
        self.size = 0

    def __getitem__(self, k):
        return self.t[k]


class Prog:
    ENGS = ("pe", "act", "dve", "pool", "sp")
    SB_LO = 16512
    SB_HI = 229344

    def __init__(self, nc):
        self.nc = nc
        self.eng = {"pe": nc.tensor, "act": nc.scalar, "dve": nc.vector,
                    "pool": nc.gpsimd, "sp": nc.sync}
        self.ops = []
        self.top = self.SB_LO
        self.allocs = []
        self.uid = 0

    def sb(self, name, shape, dtype):
        esz = 2 if dtype == BF16 else 4
        n = 1
        for s_ in shape[1:]:
            n *= s_
        size = (n * esz + 63) // 64 * 64
        off = self.top
        assert off + size <= self.SB_HI, f"SBUF overflow allocating {name}: {off}+{size}"
        self.top = off + size
        self.uid += 1
        t = self.nc.alloc_sbuf_tensor_at(f"{name}_{self.uid}", list(shape), dtype, offset=off)
        b = Buf(f"{name}_{self.uid}", t)
        b.off, b.size = off, size
        inh = set()
        for (o2, s2, b2) in self.allocs:
            if o2 < off + size and off < o2 + s2:
                if b2.last_w is not None:
                    inh.add(b2.last_w)
                inh.update(b2.readers)
        b.readers = list(inh)
        self.allocs.append((off, size, b))
        return b

    def mark(self):
        return self.top

    def release(self, m):
        self.top = m

    def ps(self, name, shape, dtype=F32):
        b = Buf(name, self.nc.alloc_psum_tensor(name, list(shape), dtype))
        b.psum = True
        return b

    def tok(self, name):
        return Buf(name, None)

    def op(self, engine, fn, reads=(), writes=(), dma=None, extra=()):
        oid = len(self.ops)
        deps = set(extra)
        for b in reads:
            if b.last_w is not None:
                deps.add(b.last_w)
            if b.psum:
                deps.update(r for r in b.readers if self.ops[r][0] != engine)
        for b in writes:
            if b.last_w is not None:
                deps.add(b.last_w)
            deps.update(b.readers)
        for b in writes:
            b.last_w = oid
            b.readers = []
        for b in reads:
            if b not in writes:
                if dma is None:
                    b.readers = [r for r in b.readers if not (self.ops[r][0] == engine and self.ops[r][3] is None)]
                b.readers.append(oid)
        self.ops.append((engine, fn, deps, dma))
        return oid

    def dma(self, engine, out_ap, in_ap, reads, writes, chan, **kw):
        return self.op(engine, lambda e: e.dma_start(out=out_ap, in_=in_ap, **kw), reads, writes, dma=chan)

    def mm(self, W, out, R, lhsT, rhs, start, stop, skip=False):
        return self.op("pe", lambda e: e.matmul(out, lhsT=lhsT, rhs=rhs, start=start, stop=stop, skip_group_check=skip), R, W)

    def tr(self, W, out, R, in_, ident):
        return self.op("pe", lambda e: e.transpose(out=out, in_=in_, identity=ident), R, W)

    def act(self, W, out, R, in_, func, **kw):
        return self.op("act", lambda e: e.activation(out=out, in_=in_, func=func, **kw), R, W)

    def tt(self, eng, W, out, R, in0, in1, op):
        return self.op(eng, lambda e: e.tensor_tensor(out=out, in0=in0, in1=in1, op=op), R, W)

    def ts(self, eng, W, out, R, in0, s1, s2, op0, op1=None):
        if op1 is None:
            return self.op(eng, lambda e: e.tensor_scalar(out=out, in0=in0, scalar1=s1, scalar2=None, op0=op0), R, W)
        return self.op(eng, lambda e: e.tensor_scalar(out=out, in0=in0, scalar1=s1, scalar2=s2, op0=op0, op1=op1), R, W)

    def stt(self, W, out, R, in0, scalar, in1, op0, op1):
        return self.op("dve", lambda e: e.scalar_tensor_tensor(out=out, in0=in0, scalar=scalar, in1=in1, op0=op0, op1=op1), R, W)

    def copy(self, eng, W, out, R, in_):
        if eng == "act":
            return self.op("act", lambda e: e.activation(out=out, in_=in_, func=AF.Copy), R, W)
        return self.op(eng, lambda e: e.tensor_copy(out=out, in_=in_), R, W)

    def recip(self, W, out, R, in_):
        return self.op("dve", lambda e: e.reciprocal(out=out, in_=in_), R, W)

    def memset(self, eng, W, out, val):
        return self.op(eng, lambda e: e.memset(out, val), (), W)

    def emit(self):
        nc = self.nc
        ops = self.ops
        n = len(ops)

        def skip(e, dma, d):
            return e == "pe" and dma is None and ops[d][0] == "pe" and ops[d][3] is None

        needed = [False] * n
        for (e, fn, deps, dma) in ops:
            for d in deps:
                if not skip(e, dma, d):
                    needed[d] = True
        sems = {"e_" + e: nc.alloc_semaphore(name=f"sem_{e}") for e in self.ENGS}
        ecount = {e: 0 for e in self.ENGS}
        chan_count = {}
        event = [None] * n
        waited = {e: {} for e in self.ENGS}
        nwaits = 0
        for i, (e, fn, deps, dma) in enumerate(ops):
            eng = self.eng[e]
            req = {}
            for d in deps:
                if skip(e, dma, d):
                    continue
                k, v = event[d]
                if k in chan_count:
                    v = chan_count[k]
                if req.get(k, 0) < v:
                    req[k] = v
            for k, v in req.items():
                if waited[e].get(k, 0) >= v:
                    continue
                eng.wait_ge(sems[k], v)
                waited[e][k] = v
                nwaits += 1
            ins = fn(eng)
            if dma is not None:
                key = "c_" + dma.name + "_" + e
                if key not in sems:
                    sems[key] = nc.alloc_semaphore(name="sem_" + key)
                    chan_count[key] = 0
                chan_count[key] += 16
                ins.then_inc(sems[key], 16)
                event[i] = (key, chan_count[key])
            elif needed[i]:
                ecount[e] += 1
                ins.then_inc(sems["e_" + e], 1)
                event[i] = ("e_" + e, ecount[e])
            else:
                event[i] = ("e_" + e, ecount[e] + 1)
        self.stats = dict(n_ops=n, n_waits=nwaits, counts=dict(ecount), n_sems=len(sems))


def _t5_bucket(dist):
    dist = np.maximum(dist, 0)
    d = np.maximum(dist, 1).astype(np.float32)
    large = 16 + (np.log(d / np.float32(16)) / np.float32(math.log(128 / 16)) * np.float32(16)).astype(np.int32)
    large = np.minimum(large, 31)
    return np.where(dist < 16, dist, large).astype(np.int64)


def _host_tables(rel_bias):
    rb = np.concatenate([rel_bias.astype(np.float32), np.full((1, 32), NEGM, np.float32)], axis=0)
    k = np.arange(128)[:, None]
    q = np.arange(128)[None, :]

    def tile(dist, valid, heads):
        idx = np.where(valid, _t5_bucket(dist), 32)
        return rb[idx][:, :, heads].transpose(0, 2, 1)

    hA = np.arange(0, 16)
    hB = np.arange(16, 32)
    d0 = q - k
    d1 = q - k + 128
    d4 = q - k + 512
    ones = np.ones((128, 128), bool)
    biasA = np.stack([tile(d0, (d0 >= 0) & (d0 < 128), hA), tile(d1, (d1 >= 0) & (d1 < 128), hA)])
    biasB = np.stack([tile(d0, d0 >= 0, hB), tile(d1, ones, hB), tile(np.full((128, 128), 1000), ones, hB),
                      tile(d4, d4 < 512, hB)])
    cc = (np.arange(248) - 120)[:, None]
    tq = np.arange(128)[None, :]
    dc = tq - 16 * cc - 31
    idx = np.where(dc >= 0, _t5_bucket(dc), 32)
    cbiasU = rb[idx][:, :, hB].transpose(0, 2, 1)
    cfar = np.zeros((16, 32, 16), np.float32)
    for i in range(16):
        for n in range(32):
            if n < 2 * i - 2:
                cfar[i, n, :] = rel_bias[31, 16:32]
    t = np.arange(S)[:, None]
    n = np.arange(32)[None, :]
    cur = t // 64
    future = n * 64 > t
    forced = (n == 0) | (n == cur) | (n == cur - 1)
    keep = np.where(future | forced, 0.0, 1.0).astype(np.float32)
    add = np.where(future, -1e30, np.where(forced, 1e4, 0.0)).astype(np.float32)
    keepadd = np.stack([keep, add], axis=1).reshape(16, 128, 2, 32).transpose(1, 0, 2, 3)
    c_start = np.arange(127) * 16
    s_start = np.arange(32) * 64
    overlap = ((c_start[:, None] <= s_start[None] + 63) & (c_start[:, None] + 31 >= s_start[None])).astype(np.float32)
    bind = (np.arange(S)[None, :] // 64 == np.arange(32)[:, None]).astype(np.float32)
    inv = 1.0 / (10000.0 ** (np.arange(0, 64, 2, dtype=np.float32) / 64))
    ang = np.arange(S, dtype=np.float32)[:, None] * inv[None].astype(np.float32)
    cos = np.cos(ang.astype(np.float32)).astype(np.float32).T
    sin = np.sin(ang.astype(np.float32)).astype(np.float32).T
    cs = np.stack([np.concatenate([cos, cos], 0), np.concatenate([sin, sin], 0)], axis=1)
    mlamask = np.where(k <= q, 0.0, NEGM).astype(np.float32)
    return dict(biasA=np.ascontiguousarray(biasA.transpose(1, 0, 2, 3)),
                biasB=np.ascontiguousarray(biasB.transpose(1, 0, 2, 3)),
                cbiasU=np.ascontiguousarray(cbiasU), cfar=np.ascontiguousarray(cfar.transpose(1, 0, 2)),
                keepadd=np.ascontiguousarray(keepadd), overlap=overlap, bind=bind,
                cs=np.ascontiguousarray(cs), mlamask=mlamask, ident=np.eye(128, dtype=np.float32))


INPUT_SHAPES = dict(
    x=[S, D], norm_mix_e=[1, D], w_in_e=[D, 3120], sinks=[1, 16], cmp_pos_k=[32, 64], cmp_pos_v=[32, 64],
    cmp_k_w1=[2048, 256], cmp_k_w2=[256, 64], cmp_v_w1=[2048, 256], cmp_v_w2=[256, 64], w_out_e=[D, D],
    norm_mix_o=[1, D], w_in_o=[D, 1344], q_norm=[1, 768], w_q_up=[768, 3072], kv_norm=[1, 512],
    w_kv_up=[512, 4096], w_out_o=[D, D], norm_mlp=[2, D], w_up0=[D, DFF], w_up1=[D, DFF],
    w_down0=[DFF, D], w_down1=[DFF, D], norm_final=[1, D],
    biasA=[128, 2, 16, 128], biasB=[128, 4, 16, 128], cbiasU=[248, 16, 128], cfar=[32, 16, 16],
    keepadd=[128, 16, 2, 32], overlap=[127, 32], bind=[32, S], cs=[64, 2, S], mlamask=[128, 128], ident=[128, 128],
)


class Ctx:
    pass


class LazyInputs:
    def __init__(self, nc):
        self.nc = nc
        self.d = {}

    def __getitem__(self, k):
        if k not in self.d:
            self.d[k] = self.nc.dram_tensor(k, INPUT_SHAPES[k], F32, kind="ExternalInput").ap()
        return self.d[k]


def build_program(stages=("l0mix", "l0mlp", "l1mix", "l1mlp"), dbg=(), cut=None):
    nc = bass.Bass("TRN2", target_bir_lowering=False)
    P = Prog(nc)
    C = Ctx()
    C.nc, C.P = nc, P
    I = LazyInputs(nc)
    C.I = I
    C.out = nc.dram_tensor("out", [S, D], F32, kind="ExternalOutput").ap()
    C.xs = nc.dram_tensor("xs", [S, D], F32, kind="Internal").ap()
    C.xsB = [P.tok(f"xs{t}") for t in range(NT)]
    C.out_ops = []
    C.dbg = {}
    C.dbg_want = dbg
    C.cut = cut
    C.B = [P.ps(f"pb{i}", [128, 512], F32) for i in range(2)]
    C.O = [P.ps(f"po{i}", [128, 8, 128], F32) for i in range(2)]
    C.Of = [o.t[:, :, :].rearrange("p h c -> p (h c)") for o in C.O]
    C.TR = [P.ps(f"ptr{i}", [128, 1024], BF16) for i in range(2)]
    C.TRap = [t.t[:, 0:512] for t in C.TR]
    C.ident = P.sb("ident", [128, 128], BF16)
    P.dma("pool", C.ident[:], I["ident"], [], [C.ident], C.ident)
    C.ones = P.sb("ones", [128, 1], BF16)
    P.memset("dve", [C.ones], C.ones[:], 1.0)
    C.neghalf = P.sb("neghalf", [128, 2], F32)
    P.memset("pool", [C.neghalf], C.neghalf[:], -0.5)
    C.x_src = I["x"]
    C.x_srcB = None

    if "l0mix" in stages:
        layer0_mixer(C)
    if "l0mlp" in stages:
        mlp(C, 0, final=False)
    if "l1mix" in stages:
        layer1_mixer(C)
    if "l1mlp" in stages:
        mlp(C, 1, final=True)
    if "xs" in dbg:
        d = nc.dram_tensor("dbg_xs", [S, D], F32, kind="ExternalOutput").ap()
        db = P.tok("dbgxs")
        for t in range(NT):
            C.out_ops.append(P.dma("sp", d[t * 128:(t + 1) * 128, :], C.xs[t * 128:(t + 1) * 128, :], [C.xsB[t]], [], db))
    P.op("sp", lambda e: e.nop(), extra=C.out_ops)
    P.emit()
    P.used_inputs = list(I.d.keys())
    return nc, P


def bcast_row(ap_row, n=128):
    return ap_row.rearrange("o n -> (o n)").partition_broadcast(n)


def load_x_tile(C, dst, t):
    P = C.P
    reads = [] if C.x_srcB is None else [C.x_srcB[t]]
    P.dma("sp", dst[:], C.x_src[t * 128:(t + 1) * 128, :], reads, [dst], dst)


def norm_rows(C, src_ap, srcB, n, gbc, out_ap, outB, tmp):
    P = C.P
    junk, ssq, sd, rstd = tmp
    P.act([junk, ssq], junk[:, 0:n], [srcB], src_ap, AF.Square, accum_out=ssq[:, 0:1])
    P.ts("dve", [sd], sd[:, 0:1], [ssq], ssq[:, 0:1], 1.0 / n, EPS, ALU.mult, ALU.add)
    P.tt("pool", [rstd], rstd[:, 0:1], [sd, C.neghalf], sd[:, 0:1], C.neghalf[:, 0:1], ALU.pow)
    P.stt([outB], out_ap, [srcB, rstd, gbc], src_ap, rstd[:, 0:1], gbc[:, 0:n], ALU.mult, ALU.mult)


def norm_transpose(C, tiles, gbc, hT, col0, xst, hb, tmp):
    P = C.P
    for j, t in enumerate(tiles):
        xa = xst[j % 2]
        load_x_tile(C, xa, t)
        norm_rows(C, xa[:], xa, D, gbc, hb[:], hb, tmp)
        for q4 in range(4):
            tb = C.TR[q4 % 2]
            tap = C.TRap[q4 % 2]
            for k in range(4):
                kc = q4 * 4 + k
                P.tr([tb], tap[:, k * 128:(k + 1) * 128], [hb, C.ident], hb[:, kc * 128:(kc + 1) * 128], C.ident[:])
            dst = hT[:, q4 * 4:q4 * 4 + 4, col0 + j * 128:col0 + (j + 1) * 128]
            P.copy("act" if q4 % 2 == 0 else "dve", [hT], dst, [tb], tap.rearrange("p (a b) -> p a b", a=4))


def load_w_cast(C, dst, dst_ap, src_ap):
    C.P.dma("pool", dst_ap, src_ap, [], [dst], dst)


def layer0_mixer(C):
    P, I = C.P, C.I
    B = C.B
    m_persist = P.mark()
    kaT = P.sb("kaT", [128, 2, S], BF16)
    kwT = P.sb("kwT", [128, 2, S], BF16)
    ksA = P.sb("ksA", [128, 2, S], BF16)
    for kt_ in (kaT, kwT, ksA):
        P.memset("pool", [kt_], kt_[:], 0.0)
    va = P.sb("va", [128, NT, 2, 65], BF16)
    vs = P.sb("vs", [128, NT, 2, 65], BF16)
    vw = P.sb("vw", [128, NT, 2, 65], BF16)
    kcT = P.sb("kcT", [128, 2, 128], BF16)
    vcA = P.sb("vcA", [128, 2, 97], BF16)
    gates = P.sb("gates", [128, NT, 48], F32)
    sinkexp = P.sb("sinkexp", [128, 16], F32)
    qsa = C.nc.dram_tensor("qsa", [1024, S], BF16, kind="Internal").ap()
    qsb = C.nc.dram_tensor("qsb", [1024, S], BF16, kind="Internal").ap()
    qsB = {(w, c, tb): P.tok(f"qs{w}{c}_{tb}") for w in "ab" for c in range(8) for tb in range(4)}
    for vt in (va, vs, vw):
        P.memset("pool", [vt], vt[:, :, :, 64:65], 1.0)
    P.memset("pool", [vcA], vcA[:], 0.0)
    P.memset("pool", [kcT], kcT[:], 0.0)
    P.memset("pool", [vcA], vcA[:, :, 64:65], 1.0)
    for g in range(2):
        P.dma("pool", ksA[64:96, g, :], I["bind"], [], [ksA], ksA)
        P.dma("pool", vcA[0:127, g, 65:97], I["overlap"], [], [vcA], vcA)
    P.dma("sp", sinkexp[:], bcast_row(I["sinks"]), [], [sinkexp], sinkexp)
    P.act([sinkexp], sinkexp[:], [sinkexp], sinkexp[:], AF.Exp)

    m_raw = P.mark()
    rawk = P.sb("rawk", [64, 2, S], BF16)
    rawv = P.sb("rawv", [64, 2, S], BF16)
    m0 = P.mark()
    hT = P.sb("hT", [128, 16, S], BF16)
    gbc = P.sb("gbc", [128, D], F32)
    P.dma("sp", gbc[:], bcast_row(I["norm_mix_e"]), [], [gbc], gbc)
    xst = [P.sb(f"xst{i}", [128, D], F32) for i in range(2)]
    hb = P.sb("hb", [128, D], BF16)
    tmp = (P.sb("junk", [128, D], BF16), P.sb("ssq", [128, 1], F32), P.sb("sd", [128, 2], F32), P.sb("rstd", [128, 1], F32))
    WKV = P.sb("WKV", [128, 16, 1072], BF16)
    wv = I["w_in_e"].rearrange("(kc p) c -> p kc c", p=128)
    load_w_cast(C, WKV, WKV[:, :, 0:256], wv[:, :, 1024:1280])
    load_w_cast(C, WKV, WKV[:, :, 256:1072], wv[:, :, 2304:3120])
    WQ = [P.sb(f"WQ{i}", [128, 16, 256], BF16) for i in range(2)]
    qblocks = [(which, col0, blk) for (which, col0) in (("a", 0), ("b", 1280)) for blk in range(4)]

    def load_q(n):
        which, col0, blk = qblocks[n]
        load_w_cast(C, WQ[n % 2], WQ[n % 2][:], wv[:, :, col0 + blk * 256:col0 + (blk + 1) * 256])
    load_q(0)
    load_q(1)
    qstage = [P.sb(f"qst{i}", [128, 512], BF16) for i in range(3)]
    norm_transpose(C, range(NT), gbc, hT, 0, xst, hb, tmp)
    if C.cut == "a":
        return
    kdst = [(kaT, 0), (None, 256), (None, 384), (ksA, 512), (kwT, 768)]
    kdst[1] = (rawk, 256)
    kdst[2] = (rawv, 384)
    nb = 0
    for (dst, c0) in kdst:
        for g in range(2):
            for tb in range(4):
                pb = B[nb % 2]
                for kc in range(16):
                    P.mm([pb], pb[0:64, :], [WKV, hT], WKV[:, kc, c0 + g * 64:c0 + (g + 1) * 64],
                         hT[:, kc, tb * 512:(tb + 1) * 512], kc == 0, kc == 15)
                P.copy("act" if nb % 2 == 0 else "dve", [dst], dst[0:64, g, tb * 512:(tb + 1) * 512], [pb], pb[0:64, :])
                nb += 1
    if C.cut == "b":
        return
    for t in range(NT):
        pbB, pb = C.O[t % 2], C.Of[t % 2]
        for (o0, c0, w) in ((0, 128, 128), (128, 640, 128), (256, 896, 176)):
            for kc in range(16):
                P.mm([pbB], pb[:, o0:o0 + w], [WKV, hT], hT[:, kc, t * 128:(t + 1) * 128], WKV[:, kc, c0:c0 + w], kc == 0, kc == 15)
        import os
        VV = int(os.environ.get("VV", "9"))
        for vi, vt in enumerate((va, vs, vw)):
            if VV < 1 or (VV < 2 and vi == 1):
                continue
            P.copy("dve" if vi != 1 else "act", [vt], vt[:, t, :, 0:64], [pbB],
                   pb[:, vi * 128:(vi + 1) * 128].rearrange("p (g d) -> p g d", g=2))
        if VV >= 3:
            P.act([gates], gates[:, t, :], [pbB], pb[:, 384:432], AF.Tanh, scale=0.5)
            P.ts("pool", [gates], gates[:, t, :], [gates], gates[:, t, :], 0.5, 0.5, ALU.mult, ALU.add)
    if C.cut == "c":
        return
    nb = 0
    for n, (which, col0, blk) in enumerate(qblocks):
        qs = qsa if which == "a" else qsb
        W = WQ[n % 2]
        for c4 in range(2):
            c = blk * 2 + c4
            for tb in range(4):
                pb = B[nb % 2]
                for kc in range(16):
                    P.mm([pb], pb[:], [W, hT], W[:, kc, c4 * 128:(c4 + 1) * 128], hT[:, kc, tb * 512:(tb + 1) * 512], kc == 0, kc == 15)
                st = qstage[nb % 3]
                if nb % 2 == 0:
                    P.op("act", lambda e, o=st[:], a=pb[:]: e.mul(o, a, 0.125), [pb], [st])
                else:
                    P.ts("dve", [st], st[:], [pb], pb[:], 0.125, None, ALU.mult)
                P.dma("sp", qs[c * 128:(c + 1) * 128, tb * 512:(tb + 1) * 512], st[:], [st], [qsB[(which, c, tb)]], st)
                nb += 1
        if n + 2 < len(qblocks):
            load_q(n + 2)
    if C.cut == "d":
        return
    P.release(m0)
    hid = P.sb("hid", [128, 2, 128], BF16)
    zt = [P.sb(f"cz{i}", [128, 128], F32) for i in range(4)]
    pbias = P.sb("pbias", [128, 1], F32)
    for kv, (w1n, w2n, posn, raw) in enumerate((("cmp_k_w1", "cmp_k_w2", "cmp_pos_k", rawk), ("cmp_v_w1", "cmp_v_w2", "cmp_pos_v", rawv))):
        w1 = P.sb(f"cw1{kv}", [64, 32, 256], BF16)
        w2 = P.sb(f"cw2{kv}", [128, 2, 64], BF16)
        posT = P.sb(f"cpos{kv}", [64, 32], BF16)
        P.dma("pool", w1[:], I[w1n].rearrange("(l d) h -> d l h", d=64), [], [w1], w1)
        P.dma("pool", w2[:], I[w2n].rearrange("(c p) d -> p c d", p=128), [], [w2], w2)
        P.dma("pool", posT[:], I[posn].rearrange("l d -> d l"), [], [posT], posT, allow_slow_non_contiguous=True)
        for g in range(2):
            for hc in range(2):
                pm, pp = B[0], B[1]
                for l in range(32):
                    P.mm([pm], pm[:, 0:127], [w1, raw], w1[:, l, hc * 128:(hc + 1) * 128], raw[:, g, l:l + 16 * 126 + 1:16], l == 0, l == 31)
                for l in range(32):
                    P.mm([pp], pp[:, 0:1], [w1, posT], w1[:, l, hc * 128:(hc + 1) * 128], posT[:, l:l + 1], l == 0, l == 31)
                z, z2, u, sg = zt
                P.copy("dve", [pbias], pbias[:], [pp], pp[:, 0:1])
                P.ts("dve", [z], z[:, 0:127], [pm, pbias], pm[:, 0:127], pbias[:, 0:1], None, ALU.add)
                P.tt("dve", [z2], z2[:, 0:127], [z], z[:, 0:127], z[:, 0:127], ALU.mult)
                P.ts("dve", [z2], z2[:, 0:127], [z2], z2[:, 0:127], 0.044715, 1.0, ALU.mult, ALU.add)
                P.tt("dve", [u], u[:, 0:127], [z2, z], z2[:, 0:127], z[:, 0:127], ALU.mult)
                P.act([sg], sg[:, 0:127], [u], u[:, 0:127], AF.Tanh, scale=0.7978845608028654)
                P.stt([sg], sg[:, 0:127], [sg, z], sg[:, 0:127], 1.0, z[:, 0:127], ALU.add, ALU.mult)
                P.ts("dve", [hid], hid[:, hc, 0:127], [sg], sg[:, 0:127], 0.5, None, ALU.mult)
            if kv == 0:
                poB, po = C.O[0], C.Of[0]
                for hc in range(2):
                    P.mm([poB], po[0:64, 0:127], [w2, hid], w2[:, hc, :], hid[:, hc, 0:127], hc == 0, hc == 1)
                P.copy("dve", [kcT], kcT[0:64, g, 0:127], [poB], po[0:64, 0:127])
            else:
                poB, po = C.O[1], C.Of[1]
                for hc in range(2):
                    P.mm([poB], po[0:127, 0:64], [hid, w2], hid[:, hc, 0:127], w2[:, hc, :], hc == 0, hc == 1)
                P.copy("dve", [vcA], vcA[0:127, g, 0:64], [poB], po[0:127, 0:64])
    if "l0proj" in C.dbg_want:
        dbg_dump(C, "kaT", kaT, kaT[0:64], [64, 2, S], BF16)
        dbg_dump(C, "ksA", ksA, ksA[0:96], [96, 2, S], BF16)
        dbg_dump(C, "va", va, va[:], [128, NT, 2, 65], BF16)
        dbg_dump(C, "kcT", kcT, kcT[0:64], [64, 2, 128], BF16)
        dbg_dump(C, "vcA", vcA, vcA[:], [128, 2, 97], BF16)
        dbg_dump(C, "gates", gates, gates[:], [128, NT, 48], F32)
    P.release(m_raw)

    if C.cut == "e":
        return
    WO = P.sb("WO", [128, 16, D], BF16)
    wov = I["w_out_e"].rearrange("(kc p) c -> p kc c", p=128)
    for q4 in range(4):
        load_w_cast(C, WO, WO[:, q4 * 4:(q4 + 1) * 4, :], wov[:, q4 * 4:(q4 + 1) * 4, :])
    biasA = P.sb("biasA", [128, 2, 16, 128], BF16)
    biasB = P.sb("biasB", [128, 4, 16, 128], BF16)
    P.dma("pool", biasA[:], I["biasA"], [], [biasA], biasA)
    P.dma("pool", biasB[:], I["biasB"], [], [biasB], biasB)
    keepadd = P.sb("keepadd", [128, NT, 2, 32], F32)
    P.dma("sp", keepadd[:], I["keepadd"], [], [keepadd], keepadd)
    cfar = P.sb("cfar", [128, NT, 16], F32)
    P.dma("sp", cfar[64:96, :, :], I["cfar"], [], [cfar], cfar)
    cb = [P.sb(f"cb{i}", [128, 16, 128], BF16) for i in range(2)]
    qa = [P.sb(f"qa{i}", [128, 16, 128], BF16) for i in range(2)]
    qb = [P.sb(f"qb{i}", [128, 16, 128], BF16) for i in range(2)]
    for q_ in qa + qb:
        P.memset("pool", [q_], q_[:], 0.0)
    PT = [P.sb(f"PT{i}", [128, 512], BF16) for i in range(4)]
    TM = P.sb("TM", [128, 128], BF16)
    P.memset("dve", [TM], TM[:], 0.0)
    ocat = P.sb("ocat", [128, D], BF16)
    oT = P.sb("oT", [128, 16, 128], BF16)
    xst = [P.sb(f"xat{i}", [128, D], F32) for i in range(2)]
    acc = P.sb("oacc", [128, 16, 64], F32)
    tmpo = P.sb("otmp", [128, 8, 64], F32)
    sm = {k: P.sb(f"sm_{k}", shp, F32) for k, shp in dict(z=[128, 8], rz=[128, 8], w=[128, 8], ps3=[128, 8, 32],
                                                          pslc=[128, 32], sc=[128, 32], m8=[128, 8], sel=[128, 32]).items()}
    ST = [B[0], B[1]]
    Ot = C.O
    osl = [0]

    def attn_group(items, PVrhs, nheads_cols, g, Ob):
        pend = []
        for idx, it in enumerate(items):
            stb, sta = STB[st3[0] % 3]
            st3[0] += 1
            mml, (half, kt, nrows, first, last) = it
            for mi, (l, r, rb) in enumerate(mml):
                P.mm([stb], sta[0:nrows, :], rb, l, r, mi == 0, mi == len(mml) - 1)
            pend.append((stb, sta, half, kt, nrows, first, last, PVrhs, nheads_cols, g, Ob))
            if len(pend) > 2:
                finish(*pend.pop(0))
        while pend:
            finish(*pend.pop(0))

    def finish(stb, sta, half, kt, nrows, first, last, PVrhs, ncols, g, Ob):
        pt = PT[pt_ctr[0] % len(PT)]
        pt_ctr[0] += 1
        P.act([pt], pt[0:nrows, :], [stb], sta[0:nrows, :], AF.Exp)
        vt = PVrhs
        for hl in range(4):
            h8 = half * 4 + hl
            if vt is vcA:
                rhs = vcA[0:nrows, g, 0:ncols]
            else:
                rhs = vt[:, kt, g, 0:ncols]
            P.mm([Ob], Ob[:, h8, 0:ncols], [pt, vt], pt[0:nrows, hl * 128:(hl + 1) * 128], rhs, first and hl == 0, last and hl == 3, skip=True)

    st_ctr = [0]
    st3 = [0]
    pt_ctr = [0]
    STB = [(B[0], B[0].t), (B[1], B[1].t), (C.TR[0], C.TR[0].t[:, :].bitcast(F32))]
    def load_tile_inputs(i):
        qat, qbt, cbt = qa[i % 2], qb[i % 2], cb[i % 2]
        tb = i // 4
        P.dma("sp", qat[0:64, :, :], qsa.rearrange("(h d) t -> d h t", d=64)[:, :, i * 128:(i + 1) * 128],
              [qsB[("a", c, tb)] for c in range(8)], [qat], qat)
        P.dma("sp", qbt[0:64, :, :], qsb.rearrange("(h d) t -> d h t", d=64)[:, :, i * 128:(i + 1) * 128],
              [qsB[("b", c, tb)] for c in range(8)], [qbt], qbt)
        P.dma("pool", cbt[:, :, :], I["cbiasU"][120 - 8 * i:248 - 8 * i, :, :], [], [cbt], cbt)

    def out_proj(i):
        xa = xst[i % 2]
        oc = ocats[i % 2]
        for q4 in range(4):
            trb, trap = C.TR[1], C.TRap[1]
            for k in range(4):
                kc = q4 * 4 + k
                P.tr([trb], trap[:, k * 128:(k + 1) * 128], [oc, C.ident], oc[:, kc * 128:(kc + 1) * 128], C.ident[:])
            P.copy("act" if q4 % 2 == 0 else "dve", [oT], oT[:, q4 * 4:q4 * 4 + 4, :], [trb], trap.rearrange("p (a b) -> p a b", a=4))
        for db in range(4):
            pb = ST[st_ctr[0] % 2]
            st_ctr[0] += 1
            for kc in range(16):
                P.mm([pb], pb[:], [oT, WO], oT[:, kc, :], WO[:, kc, db * 512:(db + 1) * 512], kc == 0, kc == 15)
            P.tt("dve", [xa], xa[:, db * 512:(db + 1) * 512], [pb, xa], pb[:], xa[:, db * 512:(db + 1) * 512], ALU.add)
        P.dma("sp", C.xs[i * 128:(i + 1) * 128, :], xa[:], [xa], [C.xsB[i]], xa)

    ocats = [ocat, P.sb("ocat2", [128, D], BF16)]
    load_tile_inputs(0)
    for i in range(NT):
        if C.cut is not None and C.cut.startswith("f") and i >= int(C.cut[1:]):
            return
        qat, qbt, cbt = qa[i % 2], qb[i % 2], cb[i % 2]
        ocat = ocats[i % 2]
        if i + 1 < NT:
            load_tile_inputs(i + 1)
        load_x_tile(C, xst[i % 2], i)
        xa = xst[i % 2]
        bo = 0
        for g in range(2):
            Ob = Ot[bo % 2]
            bo += 1
            items = []
            for half in range(2):
                hs = slice(g * 8 + half * 4, g * 8 + half * 4 + 4)
                mml = [(kcT[:, g, 0:128], qbt[:, hs, :], [kcT, qbt]),
                       (C.ident[:], cbt[:, hs, :], [C.ident, cbt])]
                items.append((mml, (half, 0, 128, True, True)))
            attn_group(items, vcA, 97, g, Ob)
            z, rz, w = sm["z"], sm["rz"], sm["w"]
            P.ts("dve", [z], z[:], [Ob], Ob[:, :, 64], 1e-30, None, ALU.max)
            P.recip([rz], rz[:], [z], z[:])
            P.tt("dve", [w], w[:], [rz, gates], rz[:], gates[:, i, g * 24:(g + 1) * 24].rearrange("p (h k) -> p h k", k=3)[:, :, 0], ALU.mult)
            P.tt("dve", [acc], acc[:, g * 8:(g + 1) * 8, :], [Ob, w], Ob[:, :, 0:64], w[:].unsqueeze(2).to_broadcast([128, 8, 64]), ALU.mult)
            ps3 = sm["ps3"]
            P.tt("dve", [ps3], ps3[:], [Ob, rz], Ob[:, :, 65:97], rz[:].unsqueeze(2).to_broadcast([128, 8, 32]), ALU.mult)
            pslc, sc, m8, sel = sm["pslc"], sm["sc"], sm["m8"], sm["sel"]
            P.op("dve", lambda e, o=pslc[:], a=ps3[:].rearrange("p h n -> p n h"): e.tensor_reduce(out=o, in_=a, axis=AX.X, op=ALU.add), [ps3], [pslc])
            P.tt("dve", [sc], sc[:], [pslc, keepadd], pslc[:], keepadd[:, i, 0, :], ALU.mult)
            P.tt("dve", [sc], sc[:], [sc, keepadd], sc[:], keepadd[:, i, 1, :], ALU.add)
            P.op("dve", lambda e, o=m8[:], a=sc[:]: e.max(out=o, in_=a), [sc], [m8])
            P.ts("dve", [sel], sel[:], [sc, m8], sc[:], m8[:, 7:8], None, ALU.is_ge)
            P.ts("dve", [TM], TM[:, 64:96], [sel], sel[:], 1.0, -NEGM, ALU.subtract, ALU.mult)
            trb, trap = C.TR[1], C.TRap[1]
            P.tr([trb], trap[:, 0:128], [TM, C.ident], TM[:], C.ident[:])
            P.tt("dve", [qbt], qbt[64:96, g * 8:(g + 1) * 8, :], [trb, cfar],
                 trap[64:96, 0:128].unsqueeze(1).to_broadcast([32, 8, 128]),
                 cfar[64:96, i, g * 8:(g + 1) * 8].unsqueeze(2).to_broadcast([32, 8, 128]), ALU.add)
        for g in range(2):
            Ob = Ot[bo % 2]
            bo += 1
            kts = [kt for kt in (i - 1, i) if kt >= 0]
            items = []
            for half in range(2):
                hs = slice(g * 8 + half * 4, g * 8 + half * 4 + 4)
                for kt in kts:
                    kind = 0 if kt == i else 1
                    mml = [(kaT[:, g, kt * 128:(kt + 1) * 128], qat[:, hs, :], [kaT, qat]),
                           (C.ident[:], biasA[:, kind, hs, :], [C.ident, biasA])]
                    items.append((mml, (half, kt, 128, kt == kts[0], kt == kts[-1])))
            attn_group(items, va, 65, g, Ob)
            z, rz = sm["z"], sm["rz"]
            P.tt("dve", [z], z[:], [Ob, sinkexp], Ob[:, :, 64], sinkexp[:, g * 8:(g + 1) * 8], ALU.add)
            P.recip([rz], rz[:], [z], z[:])
            P.tt("dve", [ocat], ocat[:, g * 512:(g + 1) * 512].rearrange("p (h d) -> p h d", d=64), [Ob, rz], Ob[:, :, 0:64],
                 rz[:].unsqueeze(2).to_broadcast([128, 8, 64]), ALU.mult)
        if i > 0:
            out_proj(i - 1)
        for br in ("win", "slc"):
            for g in range(2):
                Ob = Ot[bo % 2]
                bo += 1
                items = []
                kts = list(range(max(0, i - 4), i + 1)) if br == "win" else list(range(0, i + 1))
                for half in range(2):
                    hs = slice(g * 8 + half * 4, g * 8 + half * 4 + 4)
                    for kt in kts:
                        dk = i - kt
                        if br == "win":
                            kind = {0: 0, 1: 1, 2: 2, 3: 2, 4: 3}[dk]
                            mml = [(kwT[:, g, kt * 128:(kt + 1) * 128], qbt[:, hs, :], [kwT, qbt]),
                                   (C.ident[:], biasB[:, kind, hs, :], [C.ident, biasB])]
                        else:
                            mml = [(ksA[:, g, kt * 128:(kt + 1) * 128], qbt[:, hs, :], [ksA, qbt])]
                            if dk <= 1:
                                mml.append((C.ident[:], biasB[:, dk, hs, :], [C.ident, biasB]))
                        items.append((mml, (half, kt, 128, kt == kts[0], kt == kts[-1])))
                attn_group(items, vw if br == "win" else vs, 65, g, Ob)
                rz, w = sm["rz"], sm["w"]
                P.recip([rz], rz[:], [Ob], Ob[:, :, 64])
                gi = 2 if br == "win" else 1
                P.tt("dve", [w], w[:], [rz, gates], rz[:], gates[:, i, g * 24:(g + 1) * 24].rearrange("p (h k) -> p h k", k=3)[:, :, gi], ALU.mult)
                P.tt("dve", [tmpo], tmpo[:], [Ob, w], Ob[:, :, 0:64], w[:].unsqueeze(2).to_broadcast([128, 8, 64]), ALU.mult)
                if br == "win":
                    P.tt("pool", [acc], acc[:, g * 8:(g + 1) * 8, :], [acc, tmpo], acc[:, g * 8:(g + 1) * 8, :], tmpo[:], ALU.add)
                else:
                    P.tt("pool", [ocat], ocat[:, 1024 + g * 512:1024 + (g + 1) * 512].rearrange("p (h d) -> p h d", d=64), [acc, tmpo],
                         acc[:, g * 8:(g + 1) * 8, :], tmpo[:], ALU.add)
        if "ocat" in C.dbg_want:
            dbg_dump(C, f"ocat{i}", ocat, ocat[:], [128, D], BF16)
    out_proj(NT - 1)
    C.x_src = C.xs
    C.x_srcB = C.xsB
    P.release(m_persist)


def dbg_dump(C, name, buf, ap, shape, dtype):
    P = C.P
    d = C.nc.dram_tensor("dbg_" + name, list(shape), dtype, kind="ExternalOutput").ap()
    C.dbg[name] = (shape, dtype)
    oid = P.dma("sp", d, ap, [buf], [], buf)
    C.out_ops.append(oid)


def mlp(C, layer, final):
    P, I, B = C.P, C.I, C.B
    m = P.mark()
    gbc = P.sb("gbcm", [128, D], F32)
    gfin = gbc
    TB = 8
    hT = P.sb("hTm", [128, 16, TB * 128], BF16)
    yacc = P.sb("yacc", [128, TB, D], F32)
    xst = [P.sb(f"xsm{i}", [128, D], F32) for i in range(2)]
    hb = P.sb("hbm", [128, D], BF16)
    tmp = (hb, P.sb("ssqm", [128, 1], F32), P.sb("sdm", [128, 2], F32), P.sb("rstdm", [128, 1], F32))
    wu = [P.sb(f"wu{i}", [128, 16, 512], BF16) for i in range(2)]
    wd = [P.sb(f"wd{i}", [128, 4, D], BF16) for i in range(2)]
    uT = [P.sb(f"uT{i}", [128, 4, 512], BF16) for i in range(2)]
    rl = [P.sb(f"rl{i}", [128, 512], F32) for i in range(2)]
    wuv = I[f"w_up{layer}"].rearrange("(kc p) f -> p kc f", p=128)
    wdv = I[f"w_down{layer}"].rearrange("(fc p) d -> p fc d", p=128)
    nu = 0
    no = 0
    NG = 16

    def load_group(gi):
        wub, wdb = wu[gi % 2], wd[gi % 2]
        load_w_cast(C, wub, wub[:], wuv[:, :, gi * 512:(gi + 1) * 512])
        load_w_cast(C, wdb, wdb[:], wdv[:, gi * 4:(gi + 1) * 4, :])

    def up(gi, tb):
        nonlocal nu
        wub = wu[gi % 2]
        u = uT[(gi * 2 + tb) % 2]
        for fc in range(4):
            pb = B[nu % 2]
            r = rl[nu % 2]
            nu += 1
            for kc in range(16):
                P.mm([pb], pb[:], [wub, hT], wub[:, kc, fc * 128:(fc + 1) * 128], hT[:, kc, tb * 512:(tb + 1) * 512], kc == 0, kc == 15)
            P.act([r], r[:], [pb], pb[:], AF.Relu)
            P.act([u], u[:, fc, :], [r], r[:], AF.Square)

    def down(gi, tb):
        nonlocal no
        wdb = wd[gi % 2]
        u = uT[(gi * 2 + tb) % 2]
        for tt in range(4):
            j = tb * 4 + tt
            for dbp in range(2):
                ob, of = C.O[no % 2], C.Of[no % 2]
                no += 1
                for dbi in range(2):
                    db = dbp * 2 + dbi
                    for fc in range(4):
                        P.mm([ob], of[:, dbi * 512:(dbi + 1) * 512], [u, wdb], u[:, fc, tt * 128:(tt + 1) * 128],
                             wdb[:, fc, db * 512:(db + 1) * 512], fc == 0, fc == 3)
                ys = yacc[:, j, dbp * 1024:(dbp + 1) * 1024]
                if gi == 0:
                    P.copy("dve", [yacc], ys, [ob], of[:, :])
                else:
                    P.tt("dve", [yacc], ys, [ob, yacc], of[:, :], ys, ALU.add)

    for blk in range(NT // TB):
        P.dma("sp", gbc[:], bcast_row(I["norm_mlp"][layer:layer + 1, :]), [], [gbc], gbc)
        load_group(0)
        load_group(1)
        norm_transpose(C, range(blk * TB, (blk + 1) * TB), gbc, hT, 0, xst, hb, tmp)
        units = [(gi, tb) for gi in range(NG) for tb in range(TB // 4)]
        up(*units[0])
        for k, (gi, tb) in enumerate(units):
            if k + 1 < len(units):
                up(*units[k + 1])
            down(gi, tb)
            if tb == TB // 4 - 1 and 1 <= gi + 1 and gi + 2 < NG:
                load_group(gi + 2)
        if final:
            P.dma("sp", gfin[:], bcast_row(I["norm_final"]), [], [gfin], gfin)
        for j in range(TB):
            t = blk * TB + j
            if not final:
                P.dma("pool", C.xs[t * 128:(t + 1) * 128, :], yacc[:, j, :], [yacc, C.xsB[t]], [C.xsB[t]], yacc, accum_op=ALU.add)
                continue
            xa = xst[j % 2]
            load_x_tile(C, xa, t)
            P.tt("dve", [xa], xa[:], [xa, yacc], xa[:], yacc[:, j, :], ALU.add)
            if True:
                ot = hb
                yo = yacc[:, j, :]
                norm_rows(C, xa[:], xa, D, gfin, yo, yacc, tmp)
                oid = P.dma("sp", C.out[t * 128:(t + 1) * 128, :], yo, [yacc], [], yacc)
                C.out_ops.append(oid)
    C.x_src = C.xs
    C.x_srcB = C.xsB
    P.release(m)


def layer1_mixer(C):
    P, I, B = C.P, C.I, C.B
    m_all = P.mark()
    SCALE = 192 ** -0.5
    cqnT = P.sb("cqnT", [128, 6, S], BF16)
    ckvT = P.sb("ckvT", [128, 4, S], BF16)
    krT = P.sb("krT", [128, S], BF16)
    P.memset("pool", [krT], krT[:], 0.0)
    cs = P.sb("cs", [128, S], F32)
    P.dma("sp", cs[0:64, :], I["cs"][:, 0, :], [], [cs], cs)
    P.dma("sp", cs[64:128, :], I["cs"][:, 1, :], [], [cs], cs)
    mmask = P.sb("mmask", [128, 128], BF16)
    P.dma("pool", mmask[:], I["mlamask"], [], [mmask], mmask)
    m0 = P.mark()
    hT = P.sb("hT1", [128, 16, S], BF16)
    gbc = P.sb("gbc1", [128, D], F32)
    P.dma("sp", gbc[:], bcast_row(I["norm_mix_o"]), [], [gbc], gbc)
    qg = P.sb("qg", [128, 768], F32)
    kg = P.sb("kg", [128, 512], F32)
    P.dma("sp", qg[:], bcast_row(I["q_norm"]), [], [qg], qg)
    P.dma("sp", kg[:], bcast_row(I["kv_norm"]), [], [kg], kg)
    hb = P.sb("hb1", [128, D], BF16)
    tmp = (P.sb("junk1", [128, D], BF16), P.sb("ssq1", [128, 1], F32), P.sb("sd1", [128, 2], F32), P.sb("rstd1", [128, 1], F32))
    WI = P.sb("WI", [128, 16, 1344], BF16)
    wiv = I["w_in_o"].rearrange("(kc p) c -> p kc c", p=128)
    load_w_cast(C, WI, WI[:, 0:8, :], wiv[:, 0:8, :])
    load_w_cast(C, WI, WI[:, 8:16, :], wiv[:, 8:16, :])
    WIr = P.sb("WIr", [128, 16, 128], BF16)
    P.copy("pool", [WIr], WIr[:, :, 0:64], [WI], WI[:, :, 1280:1344])
    P.ts("pool", [WIr], WIr[:, :, 64:96], [WI], WI[:, :, 1312:1344], -1.0, None, ALU.mult)
    P.copy("pool", [WIr], WIr[:, :, 96:128], [WI], WI[:, :, 1280:1312])
    m_x = P.mark()
    xst = [P.sb(f"xs1{i}", [128, D], F32) for i in range(2)]
    norm_transpose(C, range(NT), gbc, hT, 0, xst, hb, tmp)
    P.release(m_x)
    cn = P.sb("cn", [128, 1280], BF16)
    ssb = P.sb("ssb", [128, 4], F32)
    for t in range(NT):
        pa, pbk, pc = B[0], B[1], C.O[t % 2]
        paa, pba, pca = B[0].t, B[1].t, C.Of[t % 2]
        for (pb, pap, c0, w) in ((pa, paa, 0, 384), (pbk, pba, 384, 384), (pc, pca, 768, 512)):
            for kc in range(16):
                P.mm([pb], pap[:, 0:w], [hT, WI], hT[:, kc, t * 128:(t + 1) * 128], WI[:, kc, c0:c0 + w], kc == 0, kc == 15)
        junk = tmp[0]
        P.act([junk, ssb], junk[:, 0:384], [pa], pa[:, 0:384], AF.Square, accum_out=ssb[:, 0:1])
        P.act([junk, ssb], junk[:, 384:768], [pbk], pbk[:, 0:384], AF.Square, accum_out=ssb[:, 1:2])
        P.act([junk, ssb], junk[:, 768:1280], [pc], pca[:, 0:512], AF.Square, accum_out=ssb[:, 2:3])
        sd = tmp[2]
        rq = tmp[3]
        rk = tmp[1]
        P.tt("dve", [ssb], ssb[:, 3:4], [ssb], ssb[:, 0:1], ssb[:, 1:2], ALU.add)
        P.ts("dve", [sd], sd[:, 0:1], [ssb], ssb[:, 3:4], 1.0 / 768, EPS, ALU.mult, ALU.add)
        P.ts("dve", [sd], sd[:, 1:2], [ssb], ssb[:, 2:3], 1.0 / 512, EPS, ALU.mult, ALU.add)
        P.tt("pool", [rq], rq[:, 0:1], [sd, C.neghalf], sd[:, 0:1], C.neghalf[:, 0:1], ALU.pow)
        P.tt("pool", [rk], rk[:, 0:1], [sd, C.neghalf], sd[:, 1:2], C.neghalf[:, 0:1], ALU.pow)
        P.stt([cn], cn[:, 0:384], [pa, rq, qg], pa[:, 0:384], rq[:, 0:1], qg[:, 0:384], ALU.mult, ALU.mult)
        P.stt([cn], cn[:, 384:768], [pbk, rq, qg], pbk[:, 0:384], rq[:, 0:1], qg[:, 384:768], ALU.mult, ALU.mult)
        P.stt([cn], cn[:, 768:1280], [pc, rk, kg], pca[:, 0:512], rk[:, 0:1], kg[:, 0:512], ALU.mult, ALU.mult)
        for grp, (dstT, nck, cbase) in enumerate(((cqnT, 4, 0), (cqnT, 2, 512), (ckvT, 4, 768))):
            trb, trap = C.TR[grp % 2], C.TRap[grp % 2]
            for k in range(nck):
                P.tr([trb], trap[:, k * 128:(k + 1) * 128], [cn, C.ident], cn[:, cbase + k * 128:cbase + (k + 1) * 128], C.ident[:])
            k0 = 0 if grp != 1 else 4
            P.copy("act" if grp % 2 == 0 else "dve", [dstT], dstT[:, k0:k0 + nck, t * 128:(t + 1) * 128], [trb],
                   trap[:, 0:nck * 128].rearrange("p (a b) -> p a b", a=nck))
    t1 = P.sb("rt1", [128, 512], F32)
    t2 = P.sb("rt2", [64, 512], F32)
    for tb in range(4):
        p1 = B[tb % 2]
        for kc in range(16):
            P.mm([p1], p1[:, :], [WIr, hT], WIr[:, kc, :], hT[:, kc, tb * 512:(tb + 1) * 512], kc == 0, kc == 15)
        P.tt("dve", [t1], t1[:], [p1, cs], p1[:, :], cs[:, tb * 512:(tb + 1) * 512], ALU.mult)
        P.copy("act", [t2], t2[:], [t1], t1[64:128, :])
        P.tt("pool", [krT], krT[0:64, tb * 512:(tb + 1) * 512], [t1, t2], t1[0:64, :], t2[:], ALU.add)
    if "l1lat" in C.dbg_want:
        dbg_dump(C, "cqnT", cqnT, cqnT[:], [128, 6, S], BF16)
        dbg_dump(C, "ckvT", ckvT, ckvT[:], [128, 4, S], BF16)
        dbg_dump(C, "krT", krT, krT[0:64], [64, S], BF16)
    P.release(m0)
    oT = P.sb("oT1", [128, 16, S], BF16)
    m_after_oT = P.mark()
    wq = [P.sb(f"wq{i}", [128, 6, 192], BF16) for i in range(2)]
    wqr = [P.sb(f"wqr{i}", [128, 6, 128], BF16) for i in range(2)]
    wkv = [P.sb(f"wkv{i}", [128, 4, 256], BF16) for i in range(2)]
    kT = [P.sb(f"kTh{i}", [128, S], BF16) for i in range(2)]
    vh = [P.sb(f"vh{i}", [128, NT, 129], BF16) for i in range(2)]
    for v_ in vh:
        P.memset("pool", [v_], v_[:, :, 128:129], 1.0)
    qn = [P.sb(f"qnh{i}", [128, S], BF16) for i in range(2)]
    qr = [P.sb(f"qrh{i}", [128, S], BF16) for i in range(2)]
    for q_ in qr:
        P.memset("pool", [q_], q_[:], 0.0)
    PT = [P.sb(f"PT1{i}", [128, 512], BF16) for i in range(4)]
    STB = [(B[0], B[0].t), (B[1], B[1].t), (C.TR[0], C.TR[0].t[:, :].bitcast(F32))]
    ob = [P.sb(f"ob{i}", [128, 4, 128], BF16) for i in range(2)]
    rz = [P.sb(f"rz1{i}", [128, 4], F32) for i in range(2)]
    wqv = I["w_q_up"].rearrange("(kc p) c -> p kc c", p=128)
    wkvv = I["w_kv_up"].rearrange("(kc p) c -> p kc c", p=128)
    stc = 0
    ptc = 0
    def load_head(h):
        s2 = h % 2
        load_w_cast(C, wq[s2], wq[s2][:], wqv[:, :, h * 192:(h + 1) * 192])
        load_w_cast(C, wkv[s2], wkv[s2][:], wkvv[:, :, h * 256:(h + 1) * 256])
    load_head(0)
    for h in range(16):
        s2 = h % 2
        if h + 1 < 16:
            load_head(h + 1)
        P.copy("pool", [wqr[s2]], wqr[s2][:, :, 0:64], [wq[s2]], wq[s2][:, :, 128:192])
        P.ts("pool", [wqr[s2]], wqr[s2][:, :, 64:96], [wq[s2]], wq[s2][:, :, 160:192], -1.0, None, ALU.mult)
        P.copy("pool", [wqr[s2]], wqr[s2][:, :, 96:128], [wq[s2]], wq[s2][:, :, 128:160])
        pjB = C.TR[0]
        pj = C.TR[0].t[:, :].bitcast(F32)
        for tb in range(4):
            for kc in range(4):
                P.mm([pjB], pj[:], [wkv[s2], ckvT], wkv[s2][:, kc, 0:128], ckvT[:, kc, tb * 512:(tb + 1) * 512], kc == 0, kc == 3)
            P.copy("dve", [kT[s2]], kT[s2][:, tb * 512:(tb + 1) * 512], [pjB], pj[:])
        for t4 in range(4):
            for tt in range(4):
                t = t4 * 4 + tt
                for kc in range(4):
                    P.mm([pjB], pj[:, tt * 128:(tt + 1) * 128], [ckvT, wkv[s2]], ckvT[:, kc, t * 128:(t + 1) * 128], wkv[s2][:, kc, 128:256], kc == 0, kc == 3)
            P.copy("dve", [vh[s2]], vh[s2][:, t4 * 4:(t4 + 1) * 4, 0:128], [pjB], pj[:].rearrange("p (a b) -> p a b", a=4))
        for tb in range(4):
            for kc in range(6):
                P.mm([pjB], pj[:], [wq[s2], cqnT], wq[s2][:, kc, 0:128], cqnT[:, kc, tb * 512:(tb + 1) * 512], kc == 0, kc == 5)
            P.copy("dve", [qn[s2]], qn[s2][:, tb * 512:(tb + 1) * 512], [pjB], pj[:])
            for kc in range(6):
                P.mm([pjB], pj[:, :], [wqr[s2], cqnT], wqr[s2][:, kc, :], cqnT[:, kc, tb * 512:(tb + 1) * 512], kc == 0, kc == 5)
            P.tt("dve", [t1], t1[:], [pjB, cs], pj[:, :], cs[:, tb * 512:(tb + 1) * 512], ALU.mult)
            P.copy("act", [t2], t2[:], [t1], t1[64:128, :])
            P.tt("pool", [qr[s2]], qr[s2][0:64, tb * 512:(tb + 1) * 512], [t1, t2], t1[0:64, :], t2[:], ALU.add)
        for Qb in range(4):
            Ob, Of = C.O[Qb % 2], C.Of[Qb % 2]
            nkt = 4 * Qb + 4
            pend = []

            def fin(stb, sta, kt, c0, Qb=Qb, Ob=Ob, Of=Of, s2=s2):
                nonlocal ptc
                pt = PT[ptc % len(PT)]
                ptc += 1
                P.act([pt], pt[:, c0:512], [stb], sta[:, c0:512], AF.Exp, scale=SCALE)
                for jj in range(c0 // 128, 4):
                    last = (kt == 4 * Qb + jj)
                    P.mm([Ob], Of[:, jj * 256:jj * 256 + 129], [pt, vh[s2]], pt[:, jj * 128:(jj + 1) * 128], vh[s2][:, kt, :], kt == 0 and jj % 2 == 0, last, skip=True)

            for kt in range(nkt):
                stb, sta = STB[stc % 3]
                stc += 1
                c0 = max(0, kt - 4 * Qb) * 128
                q0 = Qb * 512
                kl = kT[s2][:, kt * 128:(kt + 1) * 128]
                krl = krT[:, kt * 128:(kt + 1) * 128]
                if kt >= 4 * Qb:
                    P.mm([stb], sta[:, c0:c0 + 128], [C.ident, mmask], C.ident[:], mmask[:], True, False)
                    P.mm([stb], sta[:, c0:c0 + 128], [kT[s2], qn[s2]], kl, qn[s2][:, q0 + c0:q0 + c0 + 128], False, False)
                    P.mm([stb], sta[:, c0:c0 + 128], [krT, qr[s2]], krl, qr[s2][:, q0 + c0:q0 + c0 + 128], False, True)
                    c1 = c0 + 128
                else:
                    c1 = c0
                if c1 < 512:
                    P.mm([stb], sta[:, c1:512], [kT[s2], qn[s2]], kl, qn[s2][:, q0 + c1:q0 + 512], True, False)
                    P.mm([stb], sta[:, c1:512], [krT, qr[s2]], krl, qr[s2][:, q0 + c1:q0 + 512], False, True)
                pend.append((stb, sta, kt, c0))
                if len(pend) > 2:
                    fin(*pend.pop(0))
            while pend:
                fin(*pend.pop(0))
            rzb = rz[Qb % 2]
            obb = ob[Qb % 2]
            O4 = Of.rearrange("p (a b) -> p a b", a=4)
            P.recip([rzb], rzb[:], [Ob], O4[:, :, 128])
            P.tt("dve", [obb], obb[:], [Ob, rzb], O4[:, :, 0:128], rzb[:].unsqueeze(2).to_broadcast([128, 4, 128]), ALU.mult)
            trb, trap = C.TR[1], C.TRap[1]
            for jj in range(4):
                P.tr([trb], trap[:, jj * 128:(jj + 1) * 128], [obb, C.ident], obb[:, jj, :], C.ident[:])
            P.copy("act", [oT], oT[:, h, Qb * 512:(Qb + 1) * 512], [trb], trap[:, :])
    if "l1o" in C.dbg_want:
        dbg_dump(C, "oT1", oT, oT[:], [128, 16, S], BF16)
    P.release(m_after_oT)
    WO = P.sb("WO1", [128, 16, D], BF16)
    wov = I["w_out_o"].rearrange("(kc p) c -> p kc c", p=128)
    for q4 in range(4):
        load_w_cast(C, WO, WO[:, q4 * 4:(q4 + 1) * 4, :], wov[:, q4 * 4:(q4 + 1) * 4, :])
    xst = [P.sb(f"xo1{i}", [128, D], F32) for i in range(2)]
    for t in range(NT):
        xa = xst[t % 2]
        load_x_tile(C, xa, t)
        for db in range(4):
            pb = B[db % 2]
            for kc in range(16):
                P.mm([pb], pb[:], [oT, WO], oT[:, kc, t * 128:(t + 1) * 128], WO[:, kc, db * 512:(db + 1) * 512], kc == 0, kc == 15)
            P.tt("dve", [xa], xa[:, db * 512:(db + 1) * 512], [pb, xa], pb[:], xa[:, db * 512:(db + 1) * 512], ALU.add)
        P.dma("sp", C.xs[t * 128:(t + 1) * 128, :], xa[:], [xa], [C.xsB[t]], xa)
    C.x_src = C.xs
    C.x_srcB = C.xsB
    P.release(m_all)


_CACHE = {}


def _prep_inputs(inputs, used=None):
    tabs = _host_tables(np.asarray(inputs["rel_bias"], np.float32))
    shared = dict(tabs)
    sq = lambda k: np.ascontiguousarray(np.asarray(inputs[k], np.float32)[0])
    for k in ("w_in_e", "cmp_pos_k", "cmp_pos_v", "cmp_k_w1", "cmp_k_w2", "cmp_v_w1", "cmp_v_w2", "w_out_e",
              "w_in_o", "w_q_up", "w_kv_up", "w_out_o"):
        shared[k] = sq(k)
    for k in ("norm_mix_e", "sinks", "norm_mix_o", "q_norm", "kv_norm"):
        shared[k] = np.ascontiguousarray(np.asarray(inputs[k], np.float32).reshape(1, -1))
    shared["norm_mlp"] = np.ascontiguousarray(np.asarray(inputs["norm_mlp"], np.float32))
    shared["norm_final"] = np.ascontiguousarray(np.asarray(inputs["norm_final"], np.float32).reshape(1, -1))
    for l in range(2):
        shared[f"w_up{l}"] = np.ascontiguousarray(np.asarray(inputs["w_up"], np.float32)[l])
        shared[f"w_down{l}"] = np.ascontiguousarray(np.asarray(inputs["w_down"], np.float32)[l])
    x = np.asarray(inputs["x"], np.float32)
    in_maps = []
    for c in range(NCORES):
        m = dict(shared)
        m["x"] = np.ascontiguousarray(x[c])
        if used is not None:
            m = {k: v for k, v in m.items() if k in used}
        in_maps.append(m)
    return in_maps


def kernel(**inputs):
    if "nc" not in _CACHE:
        _CACHE["nc"], _CACHE["P"] = build_program()
    nc = _CACHE["nc"]
    in_maps = _prep_inputs(inputs, _CACHE["P"].used_inputs)
    res = run_bass_kernel_spmd(nc, in_maps, core_ids=list(range(NCORES)))
    return np.stack([np.asarray(r["out"], np.float32) for r in res.results], axis=0)
```

```python
import math
import numpy as np
import concourse.bass as bass
import concourse.mybir as mybir
from concourse.bass_utils import run_bass_kernel_spmd

F32 = mybir.dt.float32
BF16 = mybir.dt.bfloat16
AF = mybir.ActivationFunctionType
ALU = mybir.AluOpType
AX = mybir.AxisListType

S = 2048
D = 2048
NT = 16
DFF = 8192
NEGM = -30000.0
EPS = 1e-6
NCORES = 8


class Buf:
    __slots__ = ("name", "t", "last_w", "readers", "off", "size", "psum")

    def __init__(self, name, t=None):
        self.psum = False
        self.name = name
        self.t = t
        self.last_w = None
        self.readers = []
        self.off = None
        self.size = 0

    def __getitem__(self, k):
        return self.t[k]


class Prog:
    ENGS = ("pe", "act", "dve", "pool", "sp")
    SB_LO = 16512
    SB_HI = 229344

    def __init__(self, nc):
        self.nc = nc
        self.eng = {"pe": nc.tensor, "act": nc.scalar, "dve": nc.vector,
                    "pool": nc.gpsimd, "sp": nc.sync}
        self.ops = []
        self.top = self.SB_LO
        self.allocs = []
        self.uid = 0

    def sb(self, name, shape, dtype):
        esz = 2 if dtype == BF16 else 4
        n = 1
        for s_ in shape[1:]:
            n *= s_
        size = (n * esz + 63) // 64 * 64
        off = self.top
        assert off + size <= self.SB_HI, f"SBUF overflow allocating {name}: {off}+{size}"
        self.top = off + size
        self.uid += 1
        t = self.nc.alloc_sbuf_tensor_at(f"{name}_{self.uid}", list(shape), dtype, offset=off)
        b = Buf(f"{name}_{self.uid}", t)
        b.off, b.size = off, size
        inh = set()
        for (o2, s2, b2) in self.allocs:
            if o2 < off + size and off < o2 + s2:
                if b2.last_w is not None:
                    inh.add(b2.last_w)
                inh.update(b2.readers)
        b.readers = list(inh)
        self.allocs.append((off, size, b))
        return b

    def mark(self):
        return self.top

    def release(self, m):
        self.top = m

    def ps(self, name, shape, dtype=F32):
        b = Buf(name, self.nc.alloc_psum_tensor(name, list(shape), dtype))
        b.psum = True
        return b

    def tok(self, name):
        return Buf(name, None)

    def op(self, engine, fn, reads=(), writes=(), dma=None, extra=()):
        oid = len(self.ops)
        deps = set(extra)
        for b in reads:
            if b.last_w is not None:
                deps.add(b.last_w)
            if b.psum:
                deps.update(r for r in b.readers if self.ops[r][0] != engine)
        for b in writes:
            if b.last_w is not None:
                deps.add(b.last_w)
            deps.update(b.readers)
        for b in writes:
            b.last_w = oid
            b.readers = []
        for b in reads:
            if b not in writes:
                if dma is None:
                    b.readers = [r for r in b.readers if not (self.ops[r][0] == engine and self.ops[r][3] is None)]
                b.readers.append(oid)
        self.ops.append((engine, fn, deps, dma))
        return oid

    def dma(self, engine, out_ap, in_ap, reads, writes, chan, **kw):
        return self.op(engine, lambda e: e.dma_start(out=out_ap, in_=in_ap, **kw), reads, writes, dma=chan)

    def mm(self, W, out, R, lhsT, rhs, start, stop, skip=False):
        return self.op("pe", lambda e: e.matmul(out, lhsT=lhsT, rhs=rhs, start=start, stop=stop, skip_group_check=skip), R, W)

    def tr(self, W, out, R, in_, ident):
        return self.op("pe", lambda e: e.transpose(out=out, in_=in_, identity=ident), R, W)

    def act(self, W, out, R, in_, func, **kw):
        return self.op("act", lambda e: e.activation(out=out, in_=in_, func=func, **kw), R, W)

    def tt(self, eng, W, out, R, in0, in1, op):
        return self.op(eng, lambda e: e.tensor_tensor(out=out, in0=in0, in1=in1, op=op), R, W)

    def ts(self, eng, W, out, R, in0, s1, s2, op0, op1=None):
        if op1 is None:
            return self.op(eng, lambda e: e.tensor_scalar(out=out, in0=in0, scalar1=s1, scalar2=None, op0=op0), R, W)
        return self.op(eng, lambda e: e.tensor_scalar(out=out, in0=in0, scalar1=s1, scalar2=s2, op0=op0, op1=op1), R, W)

    def stt(self, W, out, R, in0, scalar, in1, op0, op1):
        return self.op("dve", lambda e: e.scalar_tensor_tensor(out=out, in0=in0, scalar=scalar, in1=in1, op0=op0, op1=op1), R, W)

    def copy(self, eng, W, out, R, in_):
        if eng == "act":
            return self.op("act", lambda e: e.activation(out=out, in_=in_, func=AF.Copy), R, W)
        return self.op(eng, lambda e: e.tensor_copy(out=out, in_=in_), R, W)

    def recip(self, W, out, R, in_):
        return self.op("dve", lambda e: e.reciprocal(out=out, in_=in_), R, W)

    def memset(self, eng, W, out, val):
        return self.op(eng, lambda e: e.memset(out, val), (), W)

    def emit(self):
        nc = self.nc
        ops = self.ops
        n = len(ops)

        def skip(e, dma, d):
            return e == "pe" and dma is None and ops[d][0] == "pe" and ops[d][3] is None

        needed = [False] * n
        for (e, fn, deps, dma) in ops:
            for d in deps:
                if not skip(e, dma, d):
                    needed[d] = True
        sems = {"e_" + e: nc.alloc_semaphore(name=f"sem_{e}") for e in self.ENGS}
        ecount = {e: 0 for e in self.ENGS}
        chan_count = {}
        event = [None] * n
        waited = {e: {} for e in self.ENGS}
        nwaits = 0
        for i, (e, fn, deps, dma) in enumerate(ops):
            eng = self.eng[e]
            req = {}
            for d in deps:
                if skip(e, dma, d):
                    continue
                k, v = event[d]
                if k in chan_count:
                    v = chan_count[k]
                if req.get(k, 0) < v:
                    req[k] = v
            for k, v in req.items():
                if waited[e].get(k, 0) >= v:
                    continue
                eng.wait_ge(sems[k], v)
                waited[e][k] = v
                nwaits += 1
            ins = fn(eng)
            if dma is not None:
                key = "c_" + dma.name + "_" + e
                if key not in sems:
                    sems[key] = nc.alloc_semaphore(name="sem_" + key)
                    chan_count[key] = 0
                chan_count[key] += 16
                ins.then_inc(sems[key], 16)
                event[i] = (key, chan_count[key])
            elif needed[i]:
                ecount[e] += 1
                ins.then_inc(sems["e_" + e], 1)
                event[i] = ("e_" + e, ecount[e])
            else:
                event[i] = ("e_" + e, ecount[e] + 1)
        self.stats = dict(n_ops=n, n_waits=nwaits, counts=dict(ecount), n_sems=len(sems))


def _t5_bucket(dist):
    dist = np.maximum(dist, 0)
    d = np.maximum(dist, 1).astype(np.float32)
    large = 16 + (np.log(d / np.float32(16)) / np.float32(math.log(128 / 16)) * np.float32(16)).astype(np.int32)
    large = np.minimum(large, 31)
    return np.where(dist < 16, dist, large).astype(np.int64)


def _host_tables(rel_bias):
    rb = np.concatenate([rel_bias.astype(np.float32), np.full((1, 32), NEGM, np.float32)], axis=0)
    k = np.arange(128)[:, None]
    q = np.arange(128)[None, :]

    def tile(dist, valid, heads):
        idx = np.where(valid, _t5_bucket(dist), 32)
        return rb[idx][:, :, heads].transpose(0, 2, 1)

    hA = np.arange(0, 16)
    hB = np.arange(16, 32)
    d0 = q - k
    d1 = q - k + 128
    d4 = q - k + 512
    ones = np.ones((128, 128), bool)
    biasA = np.stack([tile(d0, (d0 >= 0) & (d0 < 128), hA), tile(d1, (d1 >= 0) & (d1 < 128), hA)])
    biasB = np.stack([tile(d0, d0 >= 0, hB), tile(d1, ones, hB), tile(np.full((128, 128), 1000), ones, hB),
                      tile(d4, d4 < 512, hB)])
    cc = (np.arange(248) - 120)[:, None]
    tq = np.arange(128)[None, :]
    dc = tq - 16 * cc - 31
    idx = np.where(dc >= 0, _t5_bucket(dc), 32)
    cbiasU = rb[idx][:, :, hB].transpose(0, 2, 1)
    cfar = np.zeros((16, 32, 16), np.float32)
    for i in range(16):
        for n in range(32):
            if n < 2 * i - 2:
                cfar[i, n, :] = rel_bias[31, 16:32]
    t = np.arange(S)[:, None]
    n = np.arange(32)[None, :]
    cur = t // 64
    future = n * 64 > t
    forced = (n == 0) | (n == cur) | (n == cur - 1)
    keep = np.where(future | forced, 0.0, 1.0).astype(np.float32)
    add = np.where(future, -1e30, np.where(forced, 1e4, 0.0)).astype(np.float32)
    keepadd = np.stack([keep, add], axis=1).reshape(16, 128, 2, 32).transpose(1, 0, 2, 3)
    c_start = np.arange(127) * 16
    s_start = np.arange(32) * 64
    overlap = ((c_start[:, None] <= s_start[None] + 63) & (c_start[:, None] + 31 >= s_start[None])).astype(np.float32)
    bind = (np.arange(S)[None, :] // 64 == np.arange(32)[:, None]).astype(np.float32)
    inv = 1.0 / (10000.0 ** (np.arange(0, 64, 2, dtype=np.float32) / 64))
    ang = np.arange(S, dtype=np.float32)[:, None] * inv[None].astype(np.float32)
    cos = np.cos(ang.astype(np.float32)).astype(np.float32).T
    sin = np.sin(ang.astype(np.float32)).astype(np.float32).T
    cs = np.stack([np.concatenate([cos, cos], 0), np.concatenate([sin, sin], 0)], axis=1)
    mlamask = np.where(k <= q, 0.0, NEGM).astype(np.float32)
    return dict(biasA=np.ascontiguousarray(biasA.transpose(1, 0, 2, 3)),
                biasB=np.ascontiguousarray(biasB.transpose(1, 0, 2, 3)),
                cbiasU=np.ascontiguousarray(cbiasU), cfar=np.ascontiguousarray(cfar.transpose(1, 0, 2)),
                keepadd=np.ascontiguousarray(keepadd), overlap=overlap, bind=bind,
                cs=np.ascontiguousarray(cs), mlamask=mlamask, ident=np.eye(128, dtype=np.float32))


INPUT_SHAPES = dict(
    x=[S, D], norm_mix_e=[1, D], w_in_e=[D, 3120], sinks=[1, 16], cmp_pos_k=[32, 64], cmp_pos_v=[32, 64],
    cmp_k_w1=[2048, 256], cmp_k_w2=[256, 64], cmp_v_w1=[2048, 256], cmp_v_w2=[256, 64], w_out_e=[D, D],
    norm_mix_o=[1, D], w_in_o=[D, 1344], q_norm=[1, 768], w_q_up=[768, 3072], kv_norm=[1, 512],
    w_kv_up=[512, 4096], w_out_o=[D, D], norm_mlp=[2, D], w_up0=[D, DFF], w_up1=[D, DFF],
    w_down0=[DFF, D], w_down1=[DFF, D], norm_final=[1, D],
    biasA=[128, 2, 16, 128], biasB=[128, 4, 16, 128], cbiasU=[248, 16, 128], cfar=[32, 16, 16],
    keepadd=[128, 16, 2, 32], overlap=[127, 32], bind=[32, S], cs=[64, 2, S], mlamask=[128, 128], ident=[128, 128],
)


class Ctx:
    pass


class LazyInputs:
    def __init__(self, nc):
        self.nc = nc
        self.d = {}

    def __getitem__(self, k):
        if k not in self.d:
            self.d[k] = self.nc.dram_tensor(k, INPUT_SHAPES[k], F32, kind="ExternalInput").ap()
        return self.d[k]


def build_program(stages=("l0mix", "l0mlp", "l1mix", "l1mlp"), dbg=(), cut=None):
    nc = bass.Bass("TRN2", target_bir_lowering=False)
    P = Prog(nc)
    C = Ctx()
    C.nc, C.P = nc, P
    I = LazyInputs(nc)
    C.I = I
    C.out = nc.dram_tensor("out", [S, D], F32, kind="ExternalOutput").ap()
    C.xs = nc.dram_tensor("xs", [S, D], F32, kind="Internal").ap()
    C.xsB = [P.tok(f"xs{t}") for t in range(NT)]
    C.out_ops = []
    C.dbg = {}
    C.dbg_want = dbg
    C.cut = cut
    C.B = [P.ps(f"pb{i}", [128, 512], F32) for i in range(2)]
    C.O = [P.ps(f"po{i}", [128, 8, 128], F32) for i in range(2)]
    C.Of = [o.t[:, :, :].rearrange("p h c -> p (h c)") for o in C.O]
    C.TR = [P.ps(f"ptr{i}", [128, 1024], BF16) for i in range(2)]
    C.TRap = [t.t[:, 0:512] for t in C.TR]
    C.ident = P.sb("ident", [128, 128], BF16)
    P.dma("pool", C.ident[:], I["ident"], [], [C.ident], C.ident)
    C.ones = P.sb("ones", [128, 1], BF16)
    P.memset("dve", [C.ones], C.ones[:], 1.0)
    C.neghalf = P.sb("neghalf", [128, 2], F32)
    P.memset("pool", [C.neghalf], C.neghalf[:], -0.5)
    C.x_src = I["x"]
    C.x_srcB = None

    if "l0mix" in stages:
        layer0_mixer(C)
    if "l0mlp" in stages:
        mlp(C, 0, final=False)
    if "l1mix" in stages:
        layer1_mixer(C)
    if "l1mlp" in stages:
        mlp(C, 1, final=True)
    if "xs" in dbg:
        d = nc.dram_tensor("dbg_xs", [S, D], F32, kind="ExternalOutput").ap()
        db = P.tok("dbgxs")
        for t in range(NT):
            C.out_ops.append(P.dma("sp", d[t * 128:(t + 1) * 128, :], C.xs[t * 128:(t + 1) * 128, :], [C.xsB[t]], [], db))
    P.op("sp", lambda e: e.nop(), extra=C.out_ops)
    P.emit()
    P.used_inputs = list(I.d.keys())
    return nc, P


def bcast_row(ap_row, n=128):
    return ap_row.rearrange("o n -> (o n)").partition_broadcast(n)


def load_x_tile(C, dst, t):
    P = C.P
    reads = [] if C.x_srcB is None else [C.x_srcB[t]]
    P.dma("sp", dst[:], C.x_src[t * 128:(t + 1) * 128, :], reads, [dst], dst)


def norm_rows(C, src_ap, srcB, n, gbc, out_ap, outB, tmp):
    P = C.P
    junk, ssq, sd, rstd = tmp
    P.act([junk, ssq], junk[:, 0:n], [srcB], src_ap, AF.Square, accum_out=ssq[:, 0:1])
    P.ts("dve", [sd], sd[:, 0:1], [ssq], ssq[:, 0:1], 1.0 / n, EPS, ALU.mult, ALU.add)
    P.tt("pool", [rstd], rstd[:, 0:1], [sd, C.neghalf], sd[:, 0:1], C.neghalf[:, 0:1], ALU.pow)
    P.stt([outB], out_ap, [srcB, rstd, gbc], src_ap, rstd[:, 0:1], gbc[:, 0:n], ALU.mult, ALU.mult)


def norm_transpose(C, tiles, gbc, hT, col0, xst, hb, tmp):
    P = C.P
    for j, t in enumerate(tiles):
        xa = xst[j % 2]
        load_x_tile(C, xa, t)
        norm_rows(C, xa[:], xa, D, gbc, hb[:], hb, tmp)
        for q4 in range(4):
            tb = C.TR[q4 % 2]
            tap = C.TRap[q4 % 2]
            for k in range(4):
                kc = q4 * 4 + k
                P.tr([tb], tap[:, k * 128:(k + 1) * 128], [hb, C.ident], hb[:, kc * 128:(kc + 1) * 128], C.ident[:])
            dst = hT[:, q4 * 4:q4 * 4 + 4, col0 + j * 128:col0 + (j + 1) * 128]
            P.copy("act" if q4 % 2 == 0 else "dve", [hT], dst, [tb], tap.rearrange("p (a b) -> p a b", a=4))


def load_w_cast(C, dst, dst_ap, src_ap):
    C.P.dma("pool", dst_ap, src_ap, [], [dst], dst)


def layer0_mixer(C):
    P, I = C.P, C.I
    B = C.B
    m_persist = P.mark()
    kaT = P.sb("kaT", [128, 2, S], BF16)
    kwT = P.sb("kwT", [128, 2, S], BF16)
    ksA = P.sb("ksA", [128, 2, S], BF16)
    for kt_ in (kaT, kwT, ksA):
        P.memset("pool", [kt_], kt_[:], 0.0)
    va = P.sb("va", [128, NT, 2, 65], BF16)
    vs = P.sb("vs", [128, NT, 2, 65], BF16)
    vw = P.sb("vw", [128, NT, 2, 65], BF16)
    kcT = P.sb("kcT", [128, 2, 128], BF16)
    vcA = P.sb("vcA", [128, 2, 97], BF16)
    gates = P.sb("gates", [128, NT, 48], F32)
    sinkexp = P.sb("sinkexp", [128, 16], F32)
    qsa = C.nc.dram_tensor("qsa", [1024, S], BF16, kind="Internal").ap()
    qsb = C.nc.dram_tensor("qsb", [1024, S], BF16, kind="Internal").ap()
    qsB = {(w, c, tb): P.tok(f"qs{w}{c}_{tb}") for w in "ab" for c in range(8) for tb in range(4)}
    for vt in (va, vs, vw):
        P.memset("pool", [vt], vt[:, :, :, 64:65], 1.0)
    P.memset("pool", [vcA], vcA[:], 0.0)
    P.memset("pool", [kcT], kcT[:], 0.0)
    P.memset("pool", [vcA], vcA[:, :, 64:65], 1.0)
    for g in range(2):
        P.dma("pool", ksA[64:96, g, :], I["bind"], [], [ksA], ksA)
        P.dma("pool", vcA[0:127, g, 65:97], I["overlap"], [], [vcA], vcA)
    P.dma("sp", sinkexp[:], bcast_row(I["sinks"]), [], [sinkexp], sinkexp)
    P.act([sinkexp], sinkexp[:], [sinkexp], sinkexp[:], AF.Exp)

    m_raw = P.mark()
    rawk = P.sb("rawk", [64, 2, S], BF16)
    rawv = P.sb("rawv", [64, 2, S], BF16)
    m0 = P.mark()
    hT = P.sb("hT", [128, 16, S], BF16)
    gbc = P.sb("gbc", [128, D], F32)
    P.dma("sp", gbc[:], bcast_row(I["norm_mix_e"]), [], [gbc], gbc)
    xst = [P.sb(f"xst{i}", [128, D], F32) for i in range(2)]
    hb = P.sb("hb", [128, D], BF16)
    tmp = (P.sb("junk", [128, D], BF16), P.sb("ssq", [128, 1], F32), P.sb("sd", [128, 2], F32), P.sb("rstd", [128, 1], F32))
    WKV = P.sb("WKV", [128, 16, 1072], BF16)
    wv = I["w_in_e"].rearrange("(kc p) c -> p kc c", p=128)
    load_w_cast(C, WKV, WKV[:, :, 0:256], wv[:, :, 1024:1280])
    load_w_cast(C, WKV, WKV[:, :, 256:1072], wv[:, :, 2304:3120])
    WQ = [P.sb(f"WQ{i}", [128, 16, 256], BF16) for i in range(2)]
    qblocks = [(which, col0, blk) for (which, col0) in (("a", 0), ("b", 1280)) for blk in range(4)]

    def load_q(n):
        which, col0, blk = qblocks[n]
        load_w_cast(C, WQ[n % 2], WQ[n % 2][:], wv[:, :, col0 + blk * 256:col0 + (blk + 1) * 256])
    load_q(0)
    load_q(1)
    qstage = [P.sb(f"qst{i}", [128, 512], BF16) for i in range(3)]
    norm_transpose(C, range(NT), gbc, hT, 0, xst, hb, tmp)
    if C.cut == "a":
        return
    kdst = [(kaT, 0), (None, 256), (None, 384), (ksA, 512), (kwT, 768)]
    kdst[1] = (rawk, 256)
    kdst[2] = (rawv, 384)
    nb = 0
    for (dst, c0) in kdst:
        for tb in range(4):
            pb = B[nb % 2]
            for kc in range(16):
                P.mm([pb], pb[:, :], [WKV, hT], WKV[:, kc, c0:c0 + 128],
                     hT[:, kc, tb * 512:(tb + 1) * 512], kc == 0, kc == 15)
            P.copy("dve", [dst], dst[0:64, 0, tb * 512:(tb + 1) * 512], [pb], pb[0:64, :])
            P.copy("act", [dst], dst[0:64, 1, tb * 512:(tb + 1) * 512], [pb], pb[64:128, :])
            nb += 1
    if C.cut == "b":
        return
    for t in range(NT):
        pbB, pb = C.O[t % 2], C.Of[t % 2]
        for (o0, c0, w) in ((0, 128, 128), (128, 640, 128), (256, 896, 176)):
            for kc in range(16):
                P.mm([pbB], pb[:, o0:o0 + w], [WKV, hT], hT[:, kc, t * 128:(t + 1) * 128], WKV[:, kc, c0:c0 + w], kc == 0, kc == 15)
        import os
        VV = int(os.environ.get("VV", "9"))
        for vi, vt in enumerate((va, vs, vw)):
            if VV < 1 or (VV < 2 and vi == 1):
                continue
            P.copy("dve" if vi != 1 else "act", [vt], vt[:, t, :, 0:64], [pbB],
                   pb[:, vi * 128:(vi + 1) * 128].rearrange("p (g d) -> p g d", g=2))
        if VV >= 3:
            P.act([gates], gates[:, t, :], [pbB], pb[:, 384:432], AF.Tanh, scale=0.5)
            P.ts("pool", [gates], gates[:, t, :], [gates], gates[:, t, :], 0.5, 0.5, ALU.mult, ALU.add)
    if C.cut == "c":
        return
    nb = 0
    for n, (which, col0, blk) in enumerate(qblocks):
        qs = qsa if which == "a" else qsb
        W = WQ[n % 2]
        for c4 in range(2):
            c = blk * 2 + c4
            for tb in range(4):
                pb = B[nb % 2]
                for kc in range(16):
                    P.mm([pb], pb[:], [W, hT], W[:, kc, c4 * 128:(c4 + 1) * 128], hT[:, kc, tb * 512:(tb + 1) * 512], kc == 0, kc == 15)
                st = qstage[nb % 3]
                if nb % 2 == 0:
                    P.op("act", lambda e, o=st[:], a=pb[:]: e.mul(o, a, 0.125), [pb], [st])
                else:
                    P.ts("dve", [st], st[:], [pb], pb[:], 0.125, None, ALU.mult)
                P.dma("sp", qs[c * 128:(c + 1) * 128, tb * 512:(tb + 1) * 512], st[:], [st], [qsB[(which, c, tb)]], st)
                nb += 1
        if n + 2 < len(qblocks):
            load_q(n + 2)
    if C.cut == "d":
        return
    P.release(m0)
    hid = P.sb("hid", [128, 2, 128], BF16)
    zt = [P.sb(f"cz{i}", [128, 128], F32) for i in range(4)]
    pbias = P.sb("pbias", [128, 1], F32)
    for kv, (w1n, w2n, posn, raw) in enumerate((("cmp_k_w1", "cmp_k_w2", "cmp_pos_k", rawk), ("cmp_v_w1", "cmp_v_w2", "cmp_pos_v", rawv))):
        w1 = P.sb(f"cw1{kv}", [64, 32, 256], BF16)
        w2 = P.sb(f"cw2{kv}", [128, 2, 64], BF16)
        posT = P.sb(f"cpos{kv}", [64, 32], BF16)
        P.dma("pool", w1[:], I[w1n].rearrange("(l d) h -> d l h", d=64), [], [w1], w1)
        P.dma("pool", w2[:], I[w2n].rearrange("(c p) d -> p c d", p=128), [], [w2], w2)
        P.dma("pool", posT[:], I[posn].rearrange("l d -> d l"), [], [posT], posT, allow_slow_non_contiguous=True)
        for g in range(2):
            for hc in range(2):
                pm, pp = B[0], B[1]
                for l in range(32):
                    P.mm([pm], pm[:, 0:127], [w1, raw], w1[:, l, hc * 128:(hc + 1) * 128], raw[:, g, l:l + 16 * 126 + 1:16], l == 0, l == 31)
                for l in range(32):
                    P.mm([pp], pp[:, 0:1], [w1, posT], w1[:, l, hc * 128:(hc + 1) * 128], posT[:, l:l + 1], l == 0, l == 31)
                z, z2, u, sg = zt
                P.copy("dve", [pbias], pbias[:], [pp], pp[:, 0:1])
                P.ts("dve", [z], z[:, 0:127], [pm, pbias], pm[:, 0:127], pbias[:, 0:1], None, ALU.add)
                P.tt("dve", [z2], z2[:, 0:127], [z], z[:, 0:127], z[:, 0:127], ALU.mult)
                P.ts("dve", [z2], z2[:, 0:127], [z2], z2[:, 0:127], 0.044715, 1.0, ALU.mult, ALU.add)
                P.tt("dve", [u], u[:, 0:127], [z2, z], z2[:, 0:127], z[:, 0:127], ALU.mult)
                P.act([sg], sg[:, 0:127], [u], u[:, 0:127], AF.Tanh, scale=0.7978845608028654)
                P.stt([sg], sg[:, 0:127], [sg, z], sg[:, 0:127], 1.0, z[:, 0:127], ALU.add, ALU.mult)
                P.ts("dve", [hid], hid[:, hc, 0:127], [sg], sg[:, 0:127], 0.5, None, ALU.mult)
            if kv == 0:
                poB, po = C.O[0], C.Of[0]
                for hc in range(2):
                    P.mm([poB], po[0:64, 0:127], [w2, hid], w2[:, hc, :], hid[:, hc, 0:127], hc == 0, hc == 1)
                P.copy("dve", [kcT], kcT[0:64, g, 0:127], [poB], po[0:64, 0:127])
            else:
                poB, po = C.O[1], C.Of[1]
                for hc in range(2):
                    P.mm([poB], po[0:127, 0:64], [hid, w2], hid[:, hc, 0:127], w2[:, hc, :], hc == 0, hc == 1)
                P.copy("dve", [vcA], vcA[0:127, g, 0:64], [poB], po[0:127, 0:64])
    if "l0proj" in C.dbg_want:
        dbg_dump(C, "kaT", kaT, kaT[0:64], [64, 2, S], BF16)
        dbg_dump(C, "ksA", ksA, ksA[0:96], [96, 2, S], BF16)
        dbg_dump(C, "va", va, va[:], [128, NT, 2, 65], BF16)
        dbg_dump(C, "kcT", kcT, kcT[0:64], [64, 2, 128], BF16)
        dbg_dump(C, "vcA", vcA, vcA[:], [128, 2, 97], BF16)
        dbg_dump(C, "gates", gates, gates[:], [128, NT, 48], F32)
    P.release(m_raw)

    if C.cut == "e":
        return
    WO = P.sb("WO", [128, 16, D], BF16)
    wov = I["w_out_e"].rearrange("(kc p) c -> p kc c", p=128)
    biasA = P.sb("biasA", [128, 2, 16, 128], BF16)
    biasB = P.sb("biasB", [128, 4, 16, 128], BF16)
    keepadd = P.sb("keepadd", [128, NT, 2, 32], F32)
    cfar = P.sb("cfar", [128, NT, 16], F32)
    cb = [P.sb(f"cb{i}", [128, 16, 128], BF16) for i in range(2)]
    qa = [P.sb(f"qa{i}", [128, 16, 128], BF16) for i in range(2)]
    qb = [P.sb(f"qb{i}", [128, 16, 128], BF16) for i in range(2)]
    for q_ in qa + qb:
        P.memset("pool", [q_], q_[:], 0.0)
    PT = [P.sb(f"PT{i}", [128, 512], BF16) for i in range(4)]
    TM = P.sb("TM", [128, 128], BF16)
    P.memset("dve", [TM], TM[:], 0.0)
    ocat = P.sb("ocat", [128, D], BF16)
    oT = P.sb("oT", [128, 16, 128], BF16)
    xst = [P.sb(f"xat{i}", [128, D], F32) for i in range(2)]
    acc = P.sb("oacc", [128, 16, 64], F32)
    tmpo = P.sb("otmp", [128, 8, 64], F32)
    sm = {k: P.sb(f"sm_{k}", shp, F32) for k, shp in dict(z=[128, 8], rz=[128, 8], w=[128, 8], ps3=[128, 8, 32],
                                                          pslc=[128, 32], sc=[128, 32], m8=[128, 8], sel=[128, 32]).items()}
    ST = [B[0], B[1]]
    Ot = C.O
    osl = [0]

    def attn_group(items, PVrhs, nheads_cols, g, Ob):
        pend = []
        for idx, it in enumerate(items):
            stb, sta = STB[st3[0] % 3]
            st3[0] += 1
            mml, (half, kt, nrows, first, last) = it
            for mi, (l, r, rb) in enumerate(mml):
                P.mm([stb], sta[0:nrows, :], rb, l, r, mi == 0, mi == len(mml) - 1)
            pend.append((stb, sta, half, kt, nrows, first, last, PVrhs, nheads_cols, g, Ob))
            if len(pend) > 2:
                finish(*pend.pop(0))
        while pend:
            finish(*pend.pop(0))

    def finish(stb, sta, half, kt, nrows, first, last, PVrhs, ncols, g, Ob):
        pt = PT[pt_ctr[0] % len(PT)]
        pt_ctr[0] += 1
        P.act([pt], pt[0:nrows, :], [stb], sta[0:nrows, :], AF.Exp)
        vt = PVrhs
        for hl in range(4):
            h8 = half * 4 + hl
            if vt is vcA:
                rhs = vcA[0:nrows, g, 0:ncols]
            else:
                rhs = vt[:, kt, g, 0:ncols]
            P.mm([Ob], Ob[:, h8, 0:ncols], [pt, vt], pt[0:nrows, hl * 128:(hl + 1) * 128], rhs, first and hl == 0, last and hl == 3, skip=True)

    st_ctr = [0]
    st3 = [0]
    pt_ctr = [0]
    STB = [(B[0], B[0].t), (B[1], B[1].t), (C.TR[0], C.TR[0].t[:, :].bitcast(F32))]
    def load_tile_inputs(i):
        qat, qbt, cbt = qa[i % 2], qb[i % 2], cb[i % 2]
        tb = i // 4
        P.dma("sp", qat[0:64, :, :], qsa.rearrange("(h d) t -> d h t", d=64)[:, :, i * 128:(i + 1) * 128],
              [qsB[("a", c, tb)] for c in range(8)], [qat], qat)
        P.dma("sp", qbt[0:64, :, :], qsb.rearrange("(h d) t -> d h t", d=64)[:, :, i * 128:(i + 1) * 128],
              [qsB[("b", c, tb)] for c in range(8)], [qbt], qbt)
        P.dma("pool", cbt[:, :, :], I["cbiasU"][120 - 8 * i:248 - 8 * i, :, :], [], [cbt], cbt)

    def out_proj(i):
        xa = xst[i % 2]
        oc = ocats[i % 2]
        for q4 in range(4):
            trb, trap = C.TR[1], C.TRap[1]
            for k in range(4):
                kc = q4 * 4 + k
                P.tr([trb], trap[:, k * 128:(k + 1) * 128], [oc, C.ident], oc[:, kc * 128:(kc + 1) * 128], C.ident[:])
            P.copy("act" if q4 % 2 == 0 else "dve", [oT], oT[:, q4 * 4:q4 * 4 + 4, :], [trb], trap.rearrange("p (a b) -> p a b", a=4))
        for db in range(4):
            pb = ST[st_ctr[0] % 2]
            st_ctr[0] += 1
            for kc in range(16):
                P.mm([pb], pb[:], [oT, WO], oT[:, kc, :], WO[:, kc, db * 512:(db + 1) * 512], kc == 0, kc == 15)
            P.tt("dve", [xa], xa[:, db * 512:(db + 1) * 512], [pb, xa], pb[:], xa[:, db * 512:(db + 1) * 512], ALU.add)
        P.dma("sp", C.xs[i * 128:(i + 1) * 128, :], xa[:], [xa], [C.xsB[i]], xa)

    ocats = [ocat, P.sb("ocat2", [128, D], BF16)]
    load_tile_inputs(0)
    P.dma("pool", biasB[:], I["biasB"], [], [biasB], biasB)
    P.dma("pool", biasA[:], I["biasA"], [], [biasA], biasA)
    P.dma("sp", keepadd[:], I["keepadd"], [], [keepadd], keepadd)
    P.dma("sp", cfar[64:96, :, :], I["cfar"], [], [cfar], cfar)
    for q4 in range(4):
        load_w_cast(C, WO, WO[:, q4 * 4:(q4 + 1) * 4, :], wov[:, q4 * 4:(q4 + 1) * 4, :])
    for i in range(NT):
        if C.cut is not None and C.cut.startswith("f") and i >= int(C.cut[1:]):
            return
        qat, qbt, cbt = qa[i % 2], qb[i % 2], cb[i % 2]
        ocat = ocats[i % 2]
        if i + 1 < NT:
            load_tile_inputs(i + 1)
        load_x_tile(C, xst[i % 2], i)
        xa = xst[i % 2]
        bo = 0
        for g in range(2):
            Ob = Ot[bo % 2]
            bo += 1
            items = []
            for half in range(2):
                hs = slice(g * 8 + half * 4, g * 8 + half * 4 + 4)
                mml = [(kcT[:, g, 0:128], qbt[:, hs, :], [kcT, qbt]),
                       (C.ident[:], cbt[:, hs, :], [C.ident, cbt])]
                items.append((mml, (half, 0, 128, True, True)))
            attn_group(items, vcA, 97, g, Ob)
            z, rz, w = sm["z"], sm["rz"], sm["w"]
            P.ts("dve", [z], z[:], [Ob], Ob[:, :, 64], 1e-30, None, ALU.max)
            P.recip([rz], rz[:], [z], z[:])
            P.tt("dve", [w], w[:], [rz, gates], rz[:], gates[:, i, g * 24:(g + 1) * 24].rearrange("p (h k) -> p h k", k=3)[:, :, 0], ALU.mult)
            P.tt("dve", [acc], acc[:, g * 8:(g + 1) * 8, :], [Ob, w], Ob[:, :, 0:64], w[:].unsqueeze(2).to_broadcast([128, 8, 64]), ALU.mult)
            ps3 = sm["ps3"]
            P.tt("dve", [ps3], ps3[:], [Ob, rz], Ob[:, :, 65:97], rz[:].unsqueeze(2).to_broadcast([128, 8, 32]), ALU.mult)
            pslc, sc, m8, sel = sm["pslc"], sm["sc"], sm["m8"], sm["sel"]
            P.op("dve", lambda e, o=pslc[:], a=ps3[:].rearrange("p h n -> p n h"): e.tensor_reduce(out=o, in_=a, axis=AX.X, op=ALU.add), [ps3], [pslc])
            P.tt("dve", [sc], sc[:], [pslc, keepadd], pslc[:], keepadd[:, i, 0, :], ALU.mult)
            P.tt("dve", [sc], sc[:], [sc, keepadd], sc[:], keepadd[:, i, 1, :], ALU.add)
            P.op("dve", lambda e, o=m8[:], a=sc[:]: e.max(out=o, in_=a), [sc], [m8])
            P.ts("dve", [sel], sel[:], [sc, m8], sc[:], m8[:, 7:8], None, ALU.is_ge)
            P.ts("dve", [TM], TM[:, 64:96], [sel], sel[:], 1.0, -NEGM, ALU.subtract, ALU.mult)
            trb, trap = C.TR[1], C.TRap[1]
            P.tr([trb], trap[:, 0:128], [TM, C.ident], TM[:], C.ident[:])
            P.tt("dve", [qbt], qbt[64:96, g * 8:(g + 1) * 8, :], [trb, cfar],
                 trap[64:96, 0:128].unsqueeze(1).to_broadcast([32, 8, 128]),
                 cfar[64:96, i, g * 8:(g + 1) * 8].unsqueeze(2).to_broadcast([32, 8, 128]), ALU.add)
        for g in range(2):
            Ob = Ot[bo % 2]
            bo += 1
            kts = [kt for kt in (i - 1, i) if kt >= 0]
            items = []
            for half in range(2):
                hs = slice(g * 8 + half * 4, g * 8 + half * 4 + 4)
                for kt in kts:
                    kind = 0 if kt == i else 1
                    mml = [(kaT[:, g, kt * 128:(kt + 1) * 128], qat[:, hs, :], [kaT, qat]),
                           (C.ident[:], biasA[:, kind, hs, :], [C.ident, biasA])]
                    items.append((mml, (half, kt, 128, kt == kts[0], kt == kts[-1])))
            attn_group(items, va, 65, g, Ob)
            z, rz = sm["z"], sm["rz"]
            P.tt("dve", [z], z[:], [Ob, sinkexp], Ob[:, :, 64], sinkexp[:, g * 8:(g + 1) * 8], ALU.add)
            P.recip([rz], rz[:], [z], z[:])
            P.tt("dve", [ocat], ocat[:, g * 512:(g + 1) * 512].rearrange("p (h d) -> p h d", d=64), [Ob, rz], Ob[:, :, 0:64],
                 rz[:].unsqueeze(2).to_broadcast([128, 8, 64]), ALU.mult)
        if i > 0:
            out_proj(i - 1)
        for br in ("win", "slc"):
            for g in range(2):
                Ob = Ot[bo % 2]
                bo += 1
                items = []
                kts = list(range(max(0, i - 4), i + 1)) if br == "win" else list(range(0, i + 1))
                for half in range(2):
                    hs = slice(g * 8 + half * 4, g * 8 + half * 4 + 4)
                    for kt in kts:
                        dk = i - kt
                        if br == "win":
                            kind = {0: 0, 1: 1, 2: 2, 3: 2, 4: 3}[dk]
                            mml = [(kwT[:, g, kt * 128:(kt + 1) * 128], qbt[:, hs, :], [kwT, qbt]),
                                   (C.ident[:], biasB[:, kind, hs, :], [C.ident, biasB])]
                        else:
                            mml = [(ksA[:, g, kt * 128:(kt + 1) * 128], qbt[:, hs, :], [ksA, qbt])]
                            if dk <= 1:
                                mml.append((C.ident[:], biasB[:, dk, hs, :], [C.ident, biasB]))
                        items.append((mml, (half, kt, 128, kt == kts[0], kt == kts[-1])))
                attn_group(items, vw if br == "win" else vs, 65, g, Ob)
                rz, w = sm["rz"], sm["w"]
                P.recip([rz], rz[:], [Ob], Ob[:, :, 64])
                gi = 2 if br == "win" else 1
                P.tt("dve", [w], w[:], [rz, gates], rz[:], gates[:, i, g * 24:(g + 1) * 24].rearrange("p (h k) -> p h k", k=3)[:, :, gi], ALU.mult)
                P.tt("dve", [tmpo], tmpo[:], [Ob, w], Ob[:, :, 0:64], w[:].unsqueeze(2).to_broadcast([128, 8, 64]), ALU.mult)
                if br == "win":
                    P.tt("pool", [acc], acc[:, g * 8:(g + 1) * 8, :], [acc, tmpo], acc[:, g * 8:(g + 1) * 8, :], tmpo[:], ALU.add)
                else:
                    P.tt("pool", [ocat], ocat[:, 1024 + g * 512:1024 + (g + 1) * 512].rearrange("p (h d) -> p h d", d=64), [acc, tmpo],
                         acc[:, g * 8:(g + 1) * 8, :], tmpo[:], ALU.add)
        if "ocat" in C.dbg_want:
            dbg_dump(C, f"ocat{i}", ocat, ocat[:], [128, D], BF16)
    out_proj(NT - 1)
    C.x_src = C.xs
    C.x_srcB = C.xsB
    P.release(m_persist)


def dbg_dump(C, name, buf, ap, shape, dtype):
    P = C.P
    d = C.nc.dram_tensor("dbg_" + name, list(shape), dtype, kind="ExternalOutput").ap()
    C.dbg[name] = (shape, dtype)
    oid = P.dma("sp", d, ap, [buf], [], buf)
    C.out_ops.append(oid)


def mlp(C, layer, final):
    P, I, B = C.P, C.I, C.B
    m = P.mark()
    gbc = P.sb("gbcm", [128, D], F32)
    gfin = gbc
    TB = 8
    hT = P.sb("hTm", [128, 16, TB * 128], BF16)
    yacc = P.sb("yacc", [128, TB, D], F32)
    xst = [P.sb(f"xsm{i}", [128, D], F32) for i in range(2)]
    hb = P.sb("hbm", [128, D], BF16)
    tmp = (hb, P.sb("ssqm", [128, 1], F32), P.sb("sdm", [128, 2], F32), P.sb("rstdm", [128, 1], F32))
    wu = [P.sb(f"wu{i}", [128, 16, 512], BF16) for i in range(2)]
    wd = [P.sb(f"wd{i}", [128, 4, D], BF16) for i in range(2)]
    uT = [P.sb(f"uT{i}", [128, 4, 512], BF16) for i in range(2)]
    rl = [P.sb(f"rl{i}", [128, 512], F32) for i in range(2)]
    wuv = I[f"w_up{layer}"].rearrange("(kc p) f -> p kc f", p=128)
    wdv = I[f"w_down{layer}"].rearrange("(fc p) d -> p fc d", p=128)
    nu = 0
    no = 0
    NG = 16

    def load_group(gi):
        wub, wdb = wu[gi % 2], wd[gi % 2]
        load_w_cast(C, wub, wub[:], wuv[:, :, gi * 512:(gi + 1) * 512])
        load_w_cast(C, wdb, wdb[:], wdv[:, gi * 4:(gi + 1) * 4, :])

    def up(gi, tb):
        nonlocal nu
        wub = wu[gi % 2]
        u = uT[(gi * 2 + tb) % 2]
        for fc in range(4):
            pb = B[nu % 2]
            r = rl[nu % 2]
            nu += 1
            for kc in range(16):
                P.mm([pb], pb[:], [wub, hT], wub[:, kc, fc * 128:(fc + 1) * 128], hT[:, kc, tb * 512:(tb + 1) * 512], kc == 0, kc == 15)
            P.act([r], r[:], [pb], pb[:], AF.Relu)
            P.act([u], u[:, fc, :], [r], r[:], AF.Square)

    def down(gi, tb):
        nonlocal no
        wdb = wd[gi % 2]
        u = uT[(gi * 2 + tb) % 2]
        for tt in range(4):
            j = tb * 4 + tt
            for dbp in range(2):
                ob, of = C.O[no % 2], C.Of[no % 2]
                no += 1
                for dbi in range(2):
                    db = dbp * 2 + dbi
                    for fc in range(4):
                        P.mm([ob], of[:, dbi * 512:(dbi + 1) * 512], [u, wdb], u[:, fc, tt * 128:(tt + 1) * 128],
                             wdb[:, fc, db * 512:(db + 1) * 512], fc == 0, fc == 3)
                ys = yacc[:, j, dbp * 1024:(dbp + 1) * 1024]
                if gi == 0:
                    P.copy("dve", [yacc], ys, [ob], of[:, :])
                else:
                    P.tt("dve", [yacc], ys, [ob, yacc], of[:, :], ys, ALU.add)

    for blk in range(NT // TB):
        P.dma("sp", gbc[:], bcast_row(I["norm_mlp"][layer:layer + 1, :]), [], [gbc], gbc)
        load_group(0)
        load_group(1)
        norm_transpose(C, range(blk * TB, (blk + 1) * TB), gbc, hT, 0, xst, hb, tmp)
        units = [(gi, tb) for gi in range(NG) for tb in range(TB // 4)]
        up(*units[0])
        for k, (gi, tb) in enumerate(units):
            if k + 1 < len(units):
                up(*units[k + 1])
            down(gi, tb)
            if tb == TB // 4 - 1 and 1 <= gi + 1 and gi + 2 < NG:
                load_group(gi + 2)
        if final:
            P.dma("sp", gfin[:], bcast_row(I["norm_final"]), [], [gfin], gfin)
        for j in range(TB):
            t = blk * TB + j
            if not final:
                P.dma("pool", C.xs[t * 128:(t + 1) * 128, :], yacc[:, j, :], [yacc, C.xsB[t]], [C.xsB[t]], yacc, accum_op=ALU.add)
                continue
            xa = xst[j % 2]
            load_x_tile(C, xa, t)
            P.tt("dve", [xa], xa[:], [xa, yacc], xa[:], yacc[:, j, :], ALU.add)
            if True:
                ot = hb
                yo = yacc[:, j, :]
                norm_rows(C, xa[:], xa, D, gfin, yo, yacc, tmp)
                oid = P.dma("sp", C.out[t * 128:(t + 1) * 128, :], yo, [yacc], [], yacc)
                C.out_ops.append(oid)
    C.x_src = C.xs
    C.x_srcB = C.xsB
    P.release(m)


def layer1_mixer(C):
    P, I, B = C.P, C.I, C.B
    m_all = P.mark()
    SCALE = 192 ** -0.5
    cqnT = P.sb("cqnT", [128, 6, S], BF16)
    ckvT = P.sb("ckvT", [128, 4, S], BF16)
    krT = P.sb("krT", [128, S], BF16)
    P.memset("pool", [krT], krT[:], 0.0)
    cs = P.sb("cs", [128, S], F32)
    P.dma("sp", cs[0:64, :], I["cs"][:, 0, :], [], [cs], cs)
    P.dma("sp", cs[64:128, :], I["cs"][:, 1, :], [], [cs], cs)
    mmask = P.sb("mmask", [128, 128], BF16)
    P.dma("pool", mmask[:], I["mlamask"], [], [mmask], mmask)
    m0 = P.mark()
    hT = P.sb("hT1", [128, 16, S], BF16)
    gbc = P.sb("gbc1", [128, D], F32)
    P.dma("sp", gbc[:], bcast_row(I["norm_mix_o"]), [], [gbc], gbc)
    qg = P.sb("qg", [128, 768], F32)
    kg = P.sb("kg", [128, 512], F32)
    P.dma("sp", qg[:], bcast_row(I["q_norm"]), [], [qg], qg)
    P.dma("sp", kg[:], bcast_row(I["kv_norm"]), [], [kg], kg)
    hb = P.sb("hb1", [128, D], BF16)
    tmp = (P.sb("junk1", [128, D], BF16), P.sb("ssq1", [128, 1], F32), P.sb("sd1", [128, 2], F32), P.sb("rstd1", [128, 1], F32))
    WI = P.sb("WI", [128, 16, 1344], BF16)
    wiv = I["w_in_o"].rearrange("(kc p) c -> p kc c", p=128)
    load_w_cast(C, WI, WI[:, 0:8, :], wiv[:, 0:8, :])
    load_w_cast(C, WI, WI[:, 8:16, :], wiv[:, 8:16, :])
    WIr = P.sb("WIr", [128, 16, 128], BF16)
    P.copy("pool", [WIr], WIr[:, :, 0:64], [WI], WI[:, :, 1280:1344])
    P.ts("pool", [WIr], WIr[:, :, 64:96], [WI], WI[:, :, 1312:1344], -1.0, None, ALU.mult)
    P.copy("pool", [WIr], WIr[:, :, 96:128], [WI], WI[:, :, 1280:1312])
    m_x = P.mark()
    xst = [P.sb(f"xs1{i}", [128, D], F32) for i in range(2)]
    norm_transpose(C, range(NT), gbc, hT, 0, xst, hb, tmp)
    P.release(m_x)
    cn = P.sb("cn", [128, 1280], BF16)
    ssb = P.sb("ssb", [128, 4], F32)
    for t in range(NT):
        pa, pbk, pc = B[0], B[1], C.O[t % 2]
        paa, pba, pca = B[0].t, B[1].t, C.Of[t % 2]
        for (pb, pap, c0, w) in ((pa, paa, 0, 384), (pbk, pba, 384, 384), (pc, pca, 768, 512)):
            for kc in range(16):
                P.mm([pb], pap[:, 0:w], [hT, WI], hT[:, kc, t * 128:(t + 1) * 128], WI[:, kc, c0:c0 + w], kc == 0, kc == 15)
        junk = tmp[0]
        P.act([junk, ssb], junk[:, 0:384], [pa], pa[:, 0:384], AF.Square, accum_out=ssb[:, 0:1])
        P.act([junk, ssb], junk[:, 384:768], [pbk], pbk[:, 0:384], AF.Square, accum_out=ssb[:, 1:2])
        P.act([junk, ssb], junk[:, 768:1280], [pc], pca[:, 0:512], AF.Square, accum_out=ssb[:, 2:3])
        sd = tmp[2]
        rq = tmp[3]
        rk = tmp[1]
        P.tt("dve", [ssb], ssb[:, 3:4], [ssb], ssb[:, 0:1], ssb[:, 1:2], ALU.add)
        P.ts("dve", [sd], sd[:, 0:1], [ssb], ssb[:, 3:4], 1.0 / 768, EPS, ALU.mult, ALU.add)
        P.ts("dve", [sd], sd[:, 1:2], [ssb], ssb[:, 2:3], 1.0 / 512, EPS, ALU.mult, ALU.add)
        P.tt("pool", [rq], rq[:, 0:1], [sd, C.neghalf], sd[:, 0:1], C.neghalf[:, 0:1], ALU.pow)
        P.tt("pool", [rk], rk[:, 0:1], [sd, C.neghalf], sd[:, 1:2], C.neghalf[:, 0:1], ALU.pow)
        P.stt([cn], cn[:, 0:384], [pa, rq, qg], pa[:, 0:384], rq[:, 0:1], qg[:, 0:384], ALU.mult, ALU.mult)
        P.stt([cn], cn[:, 384:768], [pbk, rq, qg], pbk[:, 0:384], rq[:, 0:1], qg[:, 384:768], ALU.mult, ALU.mult)
        P.stt([cn], cn[:, 768:1280], [pc, rk, kg], pca[:, 0:512], rk[:, 0:1], kg[:, 0:512], ALU.mult, ALU.mult)
        for grp, (dstT, nck, cbase) in enumerate(((cqnT, 4, 0), (cqnT, 2, 512), (ckvT, 4, 768))):
            trb, trap = C.TR[grp % 2], C.TRap[grp % 2]
            for k in range(nck):
                P.tr([trb], trap[:, k * 128:(k + 1) * 128], [cn, C.ident], cn[:, cbase + k * 128:cbase + (k + 1) * 128], C.ident[:])
            k0 = 0 if grp != 1 else 4
            P.copy("act" if grp % 2 == 0 else "dve", [dstT], dstT[:, k0:k0 + nck, t * 128:(t + 1) * 128], [trb],
                   trap[:, 0:nck * 128].rearrange("p (a b) -> p a b", a=nck))
    t1 = P.sb("rt1", [128, 512], F32)
    t2 = P.sb("rt2", [64, 512], F32)
    for tb in range(4):
        p1 = B[tb % 2]
        for kc in range(16):
            P.mm([p1], p1[:, :], [WIr, hT], WIr[:, kc, :], hT[:, kc, tb * 512:(tb + 1) * 512], kc == 0, kc == 15)
        P.tt("dve", [t1], t1[:], [p1, cs], p1[:, :], cs[:, tb * 512:(tb + 1) * 512], ALU.mult)
        P.copy("act", [t2], t2[:], [t1], t1[64:128, :])
        P.tt("pool", [krT], krT[0:64, tb * 512:(tb + 1) * 512], [t1, t2], t1[0:64, :], t2[:], ALU.add)
    if "l1lat" in C.dbg_want:
        dbg_dump(C, "cqnT", cqnT, cqnT[:], [128, 6, S], BF16)
        dbg_dump(C, "ckvT", ckvT, ckvT[:], [128, 4, S], BF16)
        dbg_dump(C, "krT", krT, krT[0:64], [64, S], BF16)
    P.release(m0)
    oT = P.sb("oT1", [128, 16, S], BF16)
    m_after_oT = P.mark()
    wq = [P.sb(f"wq{i}", [128, 6, 192], BF16) for i in range(2)]
    wqr = [P.sb(f"wqr{i}", [128, 6, 128], BF16) for i in range(2)]
    wkv = [P.sb(f"wkv{i}", [128, 4, 256], BF16) for i in range(2)]
    kT = [P.sb(f"kTh{i}", [128, S], BF16) for i in range(2)]
    vh = [P.sb(f"vh{i}", [128, NT, 129], BF16) for i in range(2)]
    for v_ in vh:
        P.memset("pool", [v_], v_[:, :, 128:129], 1.0)
    qn = [P.sb(f"qnh{i}", [128, S], BF16) for i in range(2)]
    qr = [P.sb(f"qrh{i}", [128, S], BF16) for i in range(2)]
    for q_ in qr:
        P.memset("pool", [q_], q_[:], 0.0)
    PT = [P.sb(f"PT1{i}", [128, 512], BF16) for i in range(4)]
    STB = [(B[0], B[0].t), (B[1], B[1].t), (C.TR[0], C.TR[0].t[:, :].bitcast(F32))]
    ob = [P.sb(f"ob{i}", [128, 4, 128], BF16) for i in range(2)]
    rz = [P.sb(f"rz1{i}", [128, 4], F32) for i in range(2)]
    wqv = I["w_q_up"].rearrange("(kc p) c -> p kc c", p=128)
    wkvv = I["w_kv_up"].rearrange("(kc p) c -> p kc c", p=128)
    stc = 0
    ptc = 0
    def load_head(h):
        s2 = h % 2
        load_w_cast(C, wq[s2], wq[s2][:], wqv[:, :, h * 192:(h + 1) * 192])
        load_w_cast(C, wkv[s2], wkv[s2][:], wkvv[:, :, h * 256:(h + 1) * 256])
    load_head(0)
    for h in range(16):
        s2 = h % 2
        if h + 1 < 16:
            load_head(h + 1)
        P.copy("pool", [wqr[s2]], wqr[s2][:, :, 0:64], [wq[s2]], wq[s2][:, :, 128:192])
        P.ts("pool", [wqr[s2]], wqr[s2][:, :, 64:96], [wq[s2]], wq[s2][:, :, 160:192], -1.0, None, ALU.mult)
        P.copy("pool", [wqr[s2]], wqr[s2][:, :, 96:128], [wq[s2]], wq[s2][:, :, 128:160])
        pjB = C.TR[0]
        pj = C.TR[0].t[:, :].bitcast(F32)
        for tb in range(4):
            for kc in range(4):
                P.mm([pjB], pj[:], [wkv[s2], ckvT], wkv[s2][:, kc, 0:128], ckvT[:, kc, tb * 512:(tb + 1) * 512], kc == 0, kc == 3)
            P.copy("dve", [kT[s2]], kT[s2][:, tb * 512:(tb + 1) * 512], [pjB], pj[:])
        for t4 in range(4):
            for tt in range(4):
                t = t4 * 4 + tt
                for kc in range(4):
                    P.mm([pjB], pj[:, tt * 128:(tt + 1) * 128], [ckvT, wkv[s2]], ckvT[:, kc, t * 128:(t + 1) * 128], wkv[s2][:, kc, 128:256], kc == 0, kc == 3)
            P.copy("dve", [vh[s2]], vh[s2][:, t4 * 4:(t4 + 1) * 4, 0:128], [pjB], pj[:].rearrange("p (a b) -> p a b", a=4))
        for tb in range(4):
            for kc in range(6):
                P.mm([pjB], pj[:], [wq[s2], cqnT], wq[s2][:, kc, 0:128], cqnT[:, kc, tb * 512:(tb + 1) * 512], kc == 0, kc == 5)
            P.copy("dve", [qn[s2]], qn[s2][:, tb * 512:(tb + 1) * 512], [pjB], pj[:])
            for kc in range(6):
                P.mm([pjB], pj[:, :], [wqr[s2], cqnT], wqr[s2][:, kc, :], cqnT[:, kc, tb * 512:(tb + 1) * 512], kc == 0, kc == 5)
            P.tt("dve", [t1], t1[:], [pjB, cs], pj[:, :], cs[:, tb * 512:(tb + 1) * 512], ALU.mult)
            P.copy("act", [t2], t2[:], [t1], t1[64:128, :])
            P.tt("pool", [qr[s2]], qr[s2][0:64, tb * 512:(tb + 1) * 512], [t1, t2], t1[0:64, :], t2[:], ALU.add)
        for Qb in range(4):
            Ob, Of = C.O[Qb % 2], C.Of[Qb % 2]
            nkt = 4 * Qb + 4
            pend = []

            def fin(stb, sta, kt, c0, Qb=Qb, Ob=Ob, Of=Of, s2=s2):
                nonlocal ptc
                pt = PT[ptc % len(PT)]
                ptc += 1
                P.act([pt], pt[:, c0:512], [stb], sta[:, c0:512], AF.Exp, scale=SCALE)
                for jj in range(c0 // 128, 4):
                    last = (kt == 4 * Qb + jj)
                    P.mm([Ob], Of[:, jj * 256:jj * 256 + 129], [pt, vh[s2]], pt[:, jj * 128:(jj + 1) * 128], vh[s2][:, kt, :], kt == 0 and jj % 2 == 0, last, skip=True)

            for kt in range(nkt):
                stb, sta = STB[stc % 3]
                stc += 1
                c0 = max(0, kt - 4 * Qb) * 128
                q0 = Qb * 512
                kl = kT[s2][:, kt * 128:(kt + 1) * 128]
                krl = krT[:, kt * 128:(kt + 1) * 128]
                if kt >= 4 * Qb:
                    P.mm([stb], sta[:, c0:c0 + 128], [C.ident, mmask], C.ident[:], mmask[:], True, False)
                    P.mm([stb], sta[:, c0:c0 + 128], [kT[s2], qn[s2]], kl, qn[s2][:, q0 + c0:q0 + c0 + 128], False, False)
                    P.mm([stb], sta[:, c0:c0 + 128], [krT, qr[s2]], krl, qr[s2][:, q0 + c0:q0 + c0 + 128], False, True)
                    c1 = c0 + 128
                else:
                    c1 = c0
                if c1 < 512:
                    P.mm([stb], sta[:, c1:512], [kT[s2], qn[s2]], kl, qn[s2][:, q0 + c1:q0 + 512], True, False)
                    P.mm([stb], sta[:, c1:512], [krT, qr[s2]], krl, qr[s2][:, q0 + c1:q0 + 512], False, True)
                pend.append((stb, sta, kt, c0))
                if len(pend) > 2:
                    fin(*pend.pop(0))
            while pend:
                fin(*pend.pop(0))
            rzb = rz[Qb % 2]
            obb = ob[Qb % 2]
            O4 = Of.rearrange("p (a b) -> p a b", a=4)
            P.recip([rzb], rzb[:], [Ob], O4[:, :, 128])
            P.tt("dve", [obb], obb[:], [Ob, rzb], O4[:, :, 0:128], rzb[:].unsqueeze(2).to_broadcast([128, 4, 128]), ALU.mult)
            trb, trap = C.TR[1], C.TRap[1]
            for jj in range(4):
                P.tr([trb], trap[:, jj * 128:(jj + 1) * 128], [obb, C.ident], obb[:, jj, :], C.ident[:])
            P.copy("act", [oT], oT[:, h, Qb * 512:(Qb + 1) * 512], [trb], trap[:, :])
    if "l1o" in C.dbg_want:
        dbg_dump(C, "oT1", oT, oT[:], [128, 16, S], BF16)
    P.release(m_after_oT)
    WO = P.sb("WO1", [128, 16, D], BF16)
    wov = I["w_out_o"].rearrange("(kc p) c -> p kc c", p=128)
    for q4 in range(4):
        load_w_cast(C, WO, WO[:, q4 * 4:(q4 + 1) * 4, :], wov[:, q4 * 4:(q4 + 1) * 4, :])
    xst = [P.sb(f"xo1{i}", [128, D], F32) for i in range(2)]
    for t in range(NT):
        xa = xst[t % 2]
        load_x_tile(C, xa, t)
        for db in range(4):
            pb = B[db % 2]
            for kc in range(16):
                P.mm([pb], pb[:], [oT, WO], oT[:, kc, t * 128:(t + 1) * 128], WO[:, kc, db * 512:(db + 1) * 512], kc == 0, kc == 15)
            P.tt("dve", [xa], xa[:, db * 512:(db + 1) * 512], [pb, xa], pb[:], xa[:, db * 512:(db + 1) * 512], ALU.add)
        P.dma("sp", C.xs[t * 128:(t + 1) * 128, :], xa[:], [xa], [C.xsB[t]], xa)
    C.x_src = C.xs
    C.x_srcB = C.xsB
    P.release(m_all)


_CACHE = {}


def _prep_inputs(inputs, used=None):
    tabs = _host_tables(np.asarray(inputs["rel_bias"], np.float32))
    shared = dict(tabs)
    sq = lambda k: np.ascontiguousarray(np.asarray(inputs[k], np.float32)[0])
    for k in ("w_in_e", "cmp_pos_k", "cmp_pos_v", "cmp_k_w1", "cmp_k_w2", "cmp_v_w1", "cmp_v_w2", "w_out_e",
              "w_in_o", "w_q_up", "w_kv_up", "w_out_o"):
        shared[k] = sq(k)
    for k in ("norm_mix_e", "sinks", "norm_mix_o", "q_norm", "kv_norm"):
        shared[k] = np.ascontiguousarray(np.asarray(inputs[k], np.float32).reshape(1, -1))
    shared["norm_mlp"] = np.ascontiguousarray(np.asarray(inputs["norm_mlp"], np.float32))
    shared["norm_final"] = np.ascontiguousarray(np.asarray(inputs["norm_final"], np.float32).reshape(1, -1))
    for l in range(2):
        shared[f"w_up{l}"] = np.ascontiguousarray(np.asarray(inputs["w_up"], np.float32)[l])
        shared[f"w_down{l}"] = np.ascontiguousarray(np.asarray(inputs["w_down"], np.float32)[l])
    x = np.asarray(inputs["x"], np.float32)
    in_maps = []
    for c in range(NCORES):
        m = dict(shared)
        m["x"] = np.ascontiguousarray(x[c])
        if used is not None:
            m = {k: v for k, v in m.items() if k in used}
        in_maps.append(m)
    return in_maps


def kernel(**inputs):
    if "nc" not in _CACHE:
        _CACHE["nc"], _CACHE["P"] = build_program()
    nc = _CACHE["nc"]
    in_maps = _prep_inputs(inputs, _CACHE["P"].used_inputs)
    res = run_bass_kernel_spmd(nc, in_maps, core_ids=list(range(NCORES)))
    return np.stack([np.asarray(r["out"], np.float32) for r in res.results], axis=0)
```

```python
import math
import numpy as np
import concourse.bass as bass
import concourse.mybir as mybir
from concourse.bass_utils import run_bass_kernel_spmd

F32 = mybir.dt.float32
BF16 = mybir.dt.bfloat16
AF = mybir.ActivationFunctionType
ALU = mybir.AluOpType
AX = mybir.AxisListType

S = 2048
D = 2048
NT = 16
DFF = 8192
NEGM = -30000.0
EPS = 1e-6
NCORES = 8


class Buf:
    __slots__ = ("name", "t", "last_w", "readers", "off", "size", "psum")

    def __init__(self, name, t=None):
        self.psum = False
        self.name = name
        self.t = t
        self.last_w = None
        self.readers = []
        self.off = None
        self.size = 0

    def __getitem__(self, k):
        return self.t[k]


class Prog:
    ENGS = ("pe", "act", "dve", "pool", "sp")
    SB_LO = 16512
    SB_HI = 229344

    def __init__(self, nc):
        self.nc = nc
        self.eng = {"pe": nc.tensor, "act": nc.scalar, "dve": nc.vector,
                    "pool": nc.gpsimd, "sp": nc.sync}
        self.ops = []
        self.top = self.SB_LO
        self.allocs = []
        self.uid = 0

    def sb(self, name, shape, dtype):
        esz = 2 if dtype == BF16 else 4
        n = 1
        for s_ in shape[1:]:
            n *= s_
        size = (n * esz + 63) // 64 * 64
        off = self.top
        assert off + size <= self.SB_HI, f"SBUF overflow allocating {name}: {off}+{size}"
        self.top = off + size
        self.uid += 1
        t = self.nc.alloc_sbuf_tensor_at(f"{name}_{self.uid}", list(shape), dtype, offset=off)
        b = Buf(f"{name}_{self.uid}", t)
        b.off, b.size = off, size
        inh = set()
        for (o2, s2, b2) in self.allocs:
            if o2 < off + size and off < o2 + s2:
                if b2.last_w is not None:
                    inh.add(b2.last_w)
                inh.update(b2.readers)
        b.readers = list(inh)
        self.allocs.append((off, size, b))
        return b

    def mark(self):
        return self.top

    def release(self, m):
        self.top = m

    def ps(self, name, shape, dtype=F32):
        b = Buf(name, self.nc.alloc_psum_tensor(name, list(shape), dtype))
        b.psum = True
        return b

    def tok(self, name):
        return Buf(name, None)

    def op(self, engine, fn, reads=(), writes=(), dma=None, extra=()):
        oid = len(self.ops)
        deps = set(extra)
        for b in reads:
            if b.last_w is not None:
                deps.add(b.last_w)
            if b.psum:
                deps.update(r for r in b.readers if self.ops[r][0] != engine)
        for b in writes:
            if b.last_w is not None:
                deps.add(b.last_w)
            deps.update(b.readers)
        for b in writes:
            b.last_w = oid
            b.readers = []
        for b in reads:
            if b not in writes:
                if dma is None:
                    b.readers = [r for r in b.readers if not (self.ops[r][0] == engine and self.ops[r][3] is None)]
                b.readers.append(oid)
        self.ops.append((engine, fn, deps, dma))
        return oid

    def dma(self, engine, out_ap, in_ap, reads, writes, chan, **kw):
        return self.op(engine, lambda e: e.dma_start(out=out_ap, in_=in_ap, **kw), reads, writes, dma=chan)

    def mm(self, W, out, R, lhsT, rhs, start, stop, skip=False):
        return self.op("pe", lambda e: e.matmul(out, lhsT=lhsT, rhs=rhs, start=start, stop=stop, skip_group_check=skip), R, W)

    def tr(self, W, out, R, in_, ident):
        return self.op("pe", lambda e: e.transpose(out=out, in_=in_, identity=ident), R, W)

    def act(self, W, out, R, in_, func, **kw):
        return self.op("act", lambda e: e.activation(out=out, in_=in_, func=func, **kw), R, W)

    def tt(self, eng, W, out, R, in0, in1, op):
        return self.op(eng, lambda e: e.tensor_tensor(out=out, in0=in0, in1=in1, op=op), R, W)

    def ts(self, eng, W, out, R, in0, s1, s2, op0, op1=None):
        if op1 is None:
            return self.op(eng, lambda e: e.tensor_scalar(out=out, in0=in0, scalar1=s1, scalar2=None, op0=op0), R, W)
        return self.op(eng, lambda e: e.tensor_scalar(out=out, in0=in0, scalar1=s1, scalar2=s2, op0=op0, op1=op1), R, W)

    def stt(self, W, out, R, in0, scalar, in1, op0, op1):
        return self.op("dve", lambda e: e.scalar_tensor_tensor(out=out, in0=in0, scalar=scalar, in1=in1, op0=op0, op1=op1), R, W)

    def copy(self, eng, W, out, R, in_):
        if eng == "act":
            return self.op("act", lambda e: e.activation(out=out, in_=in_, func=AF.Copy), R, W)
        return self.op(eng, lambda e: e.tensor_copy(out=out, in_=in_), R, W)

    def recip(self, W, out, R, in_):
        return self.op("dve", lambda e: e.reciprocal(out=out, in_=in_), R, W)

    def memset(self, eng, W, out, val):
        return self.op(eng, lambda e: e.memset(out, val), (), W)

    def emit(self):
        nc = self.nc
        ops = self.ops
        n = len(ops)

        def skip(e, dma, d):
            return e == "pe" and dma is None and ops[d][0] == "pe" and ops[d][3] is None

        needed = [False] * n
        for (e, fn, deps, dma) in ops:
            for d in deps:
                if not skip(e, dma, d):
                    needed[d] = True
        sems = {"e_" + e: nc.alloc_semaphore(name=f"sem_{e}") for e in self.ENGS}
        ecount = {e: 0 for e in self.ENGS}
        chan_count = {}
        event = [None] * n
        waited = {e: {} for e in self.ENGS}
        nwaits = 0
        for i, (e, fn, deps, dma) in enumerate(ops):
            eng = self.eng[e]
            req = {}
            for d in deps:
                if skip(e, dma, d):
                    continue
                k, v = event[d]
                if k in chan_count:
                    v = chan_count[k]
                if req.get(k, 0) < v:
                    req[k] = v
            for k, v in req.items():
                if waited[e].get(k, 0) >= v:
                    continue
                eng.wait_ge(sems[k], v)
                waited[e][k] = v
                nwaits += 1
            ins = fn(eng)
            if dma is not None:
                key = "c_" + dma.name + "_" + e
                if key not in sems:
                    sems[key] = nc.alloc_semaphore(name="sem_" + key)
                    chan_count[key] = 0
                chan_count[key] += 16
                ins.then_inc(sems[key], 16)
                event[i] = (key, chan_count[key])
            elif needed[i]:
                ecount[e] += 1
                ins.then_inc(sems["e_" + e], 1)
                event[i] = ("e_" + e, ecount[e])
            else:
                event[i] = ("e_" + e, ecount[e] + 1)
        self.stats = dict(n_ops=n, n_waits=nwaits, counts=dict(ecount), n_sems=len(sems))


def _t5_bucket(dist):
    dist = np.maximum(dist, 0)
    d = np.maximum(dist, 1).astype(np.float32)
    large = 16 + (np.log(d / np.float32(16)) / np.float32(math.log(128 / 16)) * np.float32(16)).astype(np.int32)
    large = np.minimum(large, 31)
    return np.where(dist < 16, dist, large).astype(np.int64)


def _host_tables(rel_bias):
    rb = np.concatenate([rel_bias.astype(np.float32), np.full((1, 32), NEGM, np.float32)], axis=0)
    k = np.arange(128)[:, None]
    q = np.arange(128)[None, :]

    def tile(dist, valid, heads):
        idx = np.where(valid, _t5_bucket(dist), 32)
        return rb[idx][:, :, heads].transpose(0, 2, 1)

    hA = np.arange(0, 16)
    hB = np.arange(16, 32)
    d0 = q - k
    d1 = q - k + 128
    d4 = q - k + 512
    ones = np.ones((128, 128), bool)
    biasA = np.stack([tile(d0, (d0 >= 0) & (d0 < 128), hA), tile(d1, (d1 >= 0) & (d1 < 128), hA)])
    biasB = np.stack([tile(d0, d0 >= 0, hB), tile(d1, ones, hB), tile(np.full((128, 128), 1000), ones, hB),
                      tile(d4, d4 < 512, hB)])
    cc = (np.arange(248) - 120)[:, None]
    tq = np.arange(128)[None, :]
    dc = tq - 16 * cc - 31
    idx = np.where(dc >= 0, _t5_bucket(dc), 32)
    cbiasU = rb[idx][:, :, hB].transpose(0, 2, 1)
    cfar = np.zeros((16, 32, 16), np.float32)
    for i in range(16):
        for n in range(32):
            if n < 2 * i - 2:
                cfar[i, n, :] = rel_bias[31, 16:32]
    t = np.arange(S)[:, None]
    n = np.arange(32)[None, :]
    cur = t // 64
    future = n * 64 > t
    forced = (n == 0) | (n == cur) | (n == cur - 1)
    keep = np.where(future | forced, 0.0, 1.0).astype(np.float32)
    add = np.where(future, -1e30, np.where(forced, 1e4, 0.0)).astype(np.float32)
    keepadd = np.stack([keep, add], axis=1).reshape(16, 128, 2, 32).transpose(1, 0, 2, 3)
    c_start = np.arange(127) * 16
    s_start = np.arange(32) * 64
    overlap = ((c_start[:, None] <= s_start[None] + 63) & (c_start[:, None] + 31 >= s_start[None])).astype(np.float32)
    bind = (np.arange(S)[None, :] // 64 == np.arange(32)[:, None]).astype(np.float32)
    inv = 1.0 / (10000.0 ** (np.arange(0, 64, 2, dtype=np.float32) / 64))
    ang = np.arange(S, dtype=np.float32)[:, None] * inv[None].astype(np.float32)
    cos = np.cos(ang.astype(np.float32)).astype(np.float32).T
    sin = np.sin(ang.astype(np.float32)).astype(np.float32).T
    cs = np.stack([np.concatenate([cos, cos], 0), np.concatenate([sin, sin], 0)], axis=1)
    mlamask = np.where(k <= q, 0.0, NEGM).astype(np.float32)
    return dict(biasA=np.ascontiguousarray(biasA.transpose(1, 0, 2, 3)),
                biasB=np.ascontiguousarray(biasB.transpose(1, 0, 2, 3)),
                cbiasU=np.ascontiguousarray(cbiasU), cfar=np.ascontiguousarray(cfar.transpose(1, 0, 2)),
                keepadd=np.ascontiguousarray(keepadd), overlap=overlap, bind=bind,
                cs=np.ascontiguousarray(cs), mlamask=mlamask, ident=np.eye(128, dtype=np.float32))


INPUT_SHAPES = dict(
    x=[S, D], norm_mix_e=[1, D], w_in_e=[D, 3120], sinks=[1, 16], cmp_pos_k=[32, 64], cmp_pos_v=[32, 64],
    cmp_k_w1=[2048, 256], cmp_k_w2=[256, 64], cmp_v_w1=[2048, 256], cmp_v_w2=[256, 64], w_out_e=[D, D],
    norm_mix_o=[1, D], w_in_o=[D, 1344], q_norm=[1, 768], w_q_up=[768, 3072], kv_norm=[1, 512],
    w_kv_up=[512, 4096], w_out_o=[D, D], norm_mlp=[2, D], w_up0=[D, DFF], w_up1=[D, DFF],
    w_down0=[DFF, D], w_down1=[DFF, D], norm_final=[1, D],
    biasA=[128, 2, 16, 128], biasB=[128, 4, 16, 128], cbiasU=[248, 16, 128], cfar=[32, 16, 16],
    keepadd=[128, 16, 2, 32], overlap=[127, 32], bind=[32, S], cs=[64, 2, S], mlamask=[128, 128], ident=[128, 128],
)


class Ctx:
    pass


class LazyInputs:
    def __init__(self, nc):
        self.nc = nc
        self.d = {}

    def __getitem__(self, k):
        if k not in self.d:
            self.d[k] = self.nc.dram_tensor(k, INPUT_SHAPES[k], F32, kind="ExternalInput").ap()
        return self.d[k]


def build_program(stages=("l0mix", "l0mlp", "l1mix", "l1mlp"), dbg=(), cut=None):
    nc = bass.Bass("TRN2", target_bir_lowering=False)
    P = Prog(nc)
    C = Ctx()
    C.nc, C.P = nc, P
    I = LazyInputs(nc)
    C.I = I
    C.out = nc.dram_tensor("out", [S, D], F32, kind="ExternalOutput").ap()
    C.xs = nc.dram_tensor("xs", [S, D], F32, kind="Internal").ap()
    C.xsB = [P.tok(f"xs{t}") for t in range(NT)]
    C.out_ops = []
    C.dbg = {}
    C.dbg_want = dbg
    C.cut = cut
    C.B = [P.ps(f"pb{i}", [128, 512], F32) for i in range(2)]
    C.O = [P.ps(f"po{i}", [128, 8, 128], F32) for i in range(2)]
    C.Of = [o.t[:, :, :].rearrange("p h c -> p (h c)") for o in C.O]
    C.TR = [P.ps(f"ptr{i}", [128, 1024], BF16) for i in range(2)]
    C.TRap = [t.t[:, 0:512] for t in C.TR]
    C.ident = P.sb("ident", [128, 128], BF16)
    P.dma("pool", C.ident[:], I["ident"], [], [C.ident], C.ident)
    C.ones = P.sb("ones", [128, 1], BF16)
    P.memset("dve", [C.ones], C.ones[:], 1.0)
    C.neghalf = P.sb("neghalf", [128, 2], F32)
    P.memset("pool", [C.neghalf], C.neghalf[:], -0.5)
    C.x_src = I["x"]
    C.x_srcB = None

    if "l0mix" in stages:
        layer0_mixer(C)
    if "l0mlp" in stages:
        mlp(C, 0, final=False)
    if "l1mix" in stages:
        layer1_mixer(C)
    if "l1mlp" in stages:
        mlp(C, 1, final=True)
    if "xs" in dbg:
        d = nc.dram_tensor("dbg_xs", [S, D], F32, kind="ExternalOutput").ap()
        db = P.tok("dbgxs")
        for t in range(NT):
            C.out_ops.append(P.dma("sp", d[t * 128:(t + 1) * 128, :], C.xs[t * 128:(t + 1) * 128, :], [C.xsB[t]], [], db))
    P.op("sp", lambda e: e.nop(), extra=C.out_ops)
    P.emit()
    P.used_inputs = list(I.d.keys())
    return nc, P


def bcast_row(ap_row, n=128):
    return ap_row.rearrange("o n -> (o n)").partition_broadcast(n)


def load_x_tile(C, dst, t):
    P = C.P
    reads = [] if C.x_srcB is None else [C.x_srcB[t]]
    P.dma("sp", dst[:], C.x_src[t * 128:(t + 1) * 128, :], reads, [dst], dst)


def norm_rows(C, src_ap, srcB, n, gbc, out_ap, outB, tmp):
    P = C.P
    junk, ssq, sd, rstd = tmp
    P.act([junk, ssq], junk[:, 0:n], [srcB], src_ap, AF.Square, accum_out=ssq[:, 0:1])
    P.ts("dve", [sd], sd[:, 0:1], [ssq], ssq[:, 0:1], 1.0 / n, EPS, ALU.mult, ALU.add)
    P.tt("pool", [rstd], rstd[:, 0:1], [sd, C.neghalf], sd[:, 0:1], C.neghalf[:, 0:1], ALU.pow)
    P.stt([outB], out_ap, [srcB, rstd, gbc], src_ap, rstd[:, 0:1], gbc[:, 0:n], ALU.mult, ALU.mult)


def norm_transpose(C, tiles, gbc, hT, col0, xst, hb, tmp):
    P = C.P
    for j, t in enumerate(tiles):
        xa = xst[j % 2]
        load_x_tile(C, xa, t)
        norm_rows(C, xa[:], xa, D, gbc, hb[:], hb, tmp)
        for q4 in range(4):
            tb = C.TR[q4 % 2]
            tap = C.TRap[q4 % 2]
            for k in range(4):
                kc = q4 * 4 + k
                P.tr([tb], tap[:, k * 128:(k + 1) * 128], [hb, C.ident], hb[:, kc * 128:(kc + 1) * 128], C.ident[:])
            dst = hT[:, q4 * 4:q4 * 4 + 4, col0 + j * 128:col0 + (j + 1) * 128]
            P.copy("act" if q4 % 2 == 0 else "dve", [hT], dst, [tb], tap.rearrange("p (a b) -> p a b", a=4))


def load_w_cast(C, dst, dst_ap, src_ap):
    C.P.dma("pool", dst_ap, src_ap, [], [dst], dst)


def layer0_mixer(C):
    P, I = C.P, C.I
    B = C.B
    m_persist = P.mark()
    kaT = P.sb("kaT", [128, 2, S], BF16)
    kwT = P.sb("kwT", [128, 2, S], BF16)
    ksA = P.sb("ksA", [128, 2, S], BF16)
    for kt_ in (kaT, kwT, ksA):
        P.memset("pool", [kt_], kt_[:], 0.0)
    va = P.sb("va", [128, NT, 2, 65], BF16)
    vs = P.sb("vs", [128, NT, 2, 65], BF16)
    vw = P.sb("vw", [128, NT, 2, 65], BF16)
    kcT = P.sb("kcT", [128, 2, 128], BF16)
    vcA = P.sb("vcA", [128, 2, 97], BF16)
    gates = P.sb("gates", [128, NT, 48], F32)
    sinkexp = P.sb("sinkexp", [128, 16], F32)
    qsa = C.nc.dram_tensor("qsa", [1024, S], BF16, kind="Internal").ap()
    qsb = C.nc.dram_tensor("qsb", [1024, S], BF16, kind="Internal").ap()
    qsB = {(w, c, tb): P.tok(f"qs{w}{c}_{tb}") for w in "ab" for c in range(8) for tb in range(4)}
    for vt in (va, vs, vw):
        P.memset("pool", [vt], vt[:, :, :, 64:65], 1.0)
    P.memset("pool", [vcA], vcA[:], 0.0)
    P.memset("pool", [kcT], kcT[:], 0.0)
    P.memset("pool", [vcA], vcA[:, :, 64:65], 1.0)
    for g in range(2):
        P.dma("pool", ksA[64:96, g, :], I["bind"], [], [ksA], ksA)
        P.dma("pool", vcA[0:127, g, 65:97], I["overlap"], [], [vcA], vcA)
    P.dma("sp", sinkexp[:], bcast_row(I["sinks"]), [], [sinkexp], sinkexp)
    P.act([sinkexp], sinkexp[:], [sinkexp], sinkexp[:], AF.Exp)

    m_raw = P.mark()
    rawk = P.sb("rawk", [64, 2, S], BF16)
    rawv = P.sb("rawv", [64, 2, S], BF16)
    m0 = P.mark()
    hTs = [P.sb(f"hT{tb}", [128, 16, 512], BF16) for tb in range(4)]
    gbc = P.sb("gbc", [128, D], F32)
    P.dma("sp", gbc[:], bcast_row(I["norm_mix_e"]), [], [gbc], gbc)
    xst = [P.sb(f"xst{i}", [128, D], F32) for i in range(2)]
    hb = P.sb("hb", [128, D], BF16)
    tmp = (P.sb("junk", [128, D], BF16), P.sb("ssq", [128, 1], F32), P.sb("sd", [128, 2], F32), P.sb("rstd", [128, 1], F32))
    WKV = P.sb("WKV", [128, 16, 1072], BF16)
    wv = I["w_in_e"].rearrange("(kc p) c -> p kc c", p=128)
    load_w_cast(C, WKV, WKV[:, :, 0:256], wv[:, :, 1024:1280])
    load_w_cast(C, WKV, WKV[:, :, 256:1072], wv[:, :, 2304:3120])
    WQ = [P.sb(f"WQ{i}", [128, 16, 256], BF16) for i in range(2)]
    qblocks = [(which, col0, blk) for (which, col0) in (("a", 0), ("b", 1280)) for blk in range(4)]

    def load_q(n):
        which, col0, blk = qblocks[n]
        load_w_cast(C, WQ[n % 2], WQ[n % 2][:], wv[:, :, col0 + blk * 256:col0 + (blk + 1) * 256])
    load_q(0)
    load_q(1)
    qstage = [P.sb(f"qst{i}", [128, 512], BF16) for i in range(3)]
    kdst = [(kaT, 0), (rawk, 256), (rawv, 384), (ksA, 512), (kwT, 768)]
    nb = 0
    for tb in range(4):
        hT = hTs[tb]
        norm_transpose(C, range(4 * tb, 4 * tb + 4), gbc, hT, 0, xst, hb, tmp)
        for (dst, c0) in kdst:
            pb = B[nb % 2]
            for kc in range(16):
                P.mm([pb], pb[:, :], [WKV, hT], WKV[:, kc, c0:c0 + 128], hT[:, kc, :], kc == 0, kc == 15)
            P.copy("dve", [dst], dst[0:64, 0, tb * 512:(tb + 1) * 512], [pb], pb[0:64, :])
            P.copy("act", [dst], dst[0:64, 1, tb * 512:(tb + 1) * 512], [pb], pb[64:128, :])
            nb += 1
        for t in range(4 * tb, 4 * tb + 4):
            tl = t % 4
            pbB, pb = C.O[t % 2], C.Of[t % 2]
            for (o0, c0, w) in ((0, 128, 128), (128, 640, 128), (256, 896, 176)):
                for kc in range(16):
                    P.mm([pbB], pb[:, o0:o0 + w], [WKV, hT], hT[:, kc, tl * 128:(tl + 1) * 128], WKV[:, kc, c0:c0 + w], kc == 0, kc == 15)
            for vi, vt in enumerate((va, vs, vw)):
                P.copy("dve" if vi != 1 else "act", [vt], vt[:, t, :, 0:64], [pbB],
                       pb[:, vi * 128:(vi + 1) * 128].rearrange("p (g d) -> p g d", g=2))
            P.act([gates], gates[:, t, :], [pbB], pb[:, 384:432], AF.Tanh, scale=0.5)
            P.ts("pool", [gates], gates[:, t, :], [gates], gates[:, t, :], 0.5, 0.5, ALU.mult, ALU.add)
    nb = 0
    for n, (which, col0, blk) in enumerate(qblocks):
        qs = qsa if which == "a" else qsb
        W = WQ[n % 2]
        for c4 in range(2):
            c = blk * 2 + c4
            for tb in range(4):
                pb = B[nb % 2]
                for kc in range(16):
                    P.mm([pb], pb[:], [W, hTs[tb]], W[:, kc, c4 * 128:(c4 + 1) * 128], hTs[tb][:, kc, :], kc == 0, kc == 15)
                st = qstage[nb % 3]
                if nb % 2 == 0:
                    P.op("act", lambda e, o=st[:], a=pb[:]: e.mul(o, a, 0.125), [pb], [st])
                else:
                    P.ts("dve", [st], st[:], [pb], pb[:], 0.125, None, ALU.mult)
                P.dma("sp", qs[c * 128:(c + 1) * 128, tb * 512:(tb + 1) * 512], st[:], [st], [qsB[(which, c, tb)]], st)
                nb += 1
        if n + 2 < len(qblocks):
            load_q(n + 2)
    if C.cut == "d":
        return
    P.release(m0)
    hid = P.sb("hid", [128, 2, 128], BF16)
    zt = [P.sb(f"cz{i}", [128, 128], F32) for i in range(4)]
    pbias = P.sb("pbias", [128, 1], F32)
    for kv, (w1n, w2n, posn, raw) in enumerate((("cmp_k_w1", "cmp_k_w2", "cmp_pos_k", rawk), ("cmp_v_w1", "cmp_v_w2", "cmp_pos_v", rawv))):
        w1 = P.sb(f"cw1{kv}", [64, 32, 256], BF16)
        w2 = P.sb(f"cw2{kv}", [128, 2, 64], BF16)
        posT = P.sb(f"cpos{kv}", [64, 32], BF16)
        P.dma("pool", w1[:], I[w1n].rearrange("(l d) h -> d l h", d=64), [], [w1], w1)
        P.dma("pool", w2[:], I[w2n].rearrange("(c p) d -> p c d", p=128), [], [w2], w2)
        P.dma("pool", posT[:], I[posn].rearrange("l d -> d l"), [], [posT], posT, allow_slow_non_contiguous=True)
        for g in range(2):
            for hc in range(2):
                pm, pp = B[0], B[1]
                for l in range(32):
                    P.mm([pm], pm[:, 0:127], [w1, raw], w1[:, l, hc * 128:(hc + 1) * 128], raw[:, g, l:l + 16 * 126 + 1:16], l == 0, l == 31)
                for l in range(32):
                    P.mm([pp], pp[:, 0:1], [w1, posT], w1[:, l, hc * 128:(hc + 1) * 128], posT[:, l:l + 1], l == 0, l == 31)
                z, z2, u, sg = zt
                P.copy("dve", [pbias], pbias[:], [pp], pp[:, 0:1])
                P.ts("dve", [z], z[:, 0:127], [pm, pbias], pm[:, 0:127], pbias[:, 0:1], None, ALU.add)
                P.tt("dve", [z2], z2[:, 0:127], [z], z[:, 0:127], z[:, 0:127], ALU.mult)
                P.ts("dve", [z2], z2[:, 0:127], [z2], z2[:, 0:127], 0.044715, 1.0, ALU.mult, ALU.add)
                P.tt("dve", [u], u[:, 0:127], [z2, z], z2[:, 0:127], z[:, 0:127], ALU.mult)
                P.act([sg], sg[:, 0:127], [u], u[:, 0:127], AF.Tanh, scale=0.7978845608028654)
                P.stt([sg], sg[:, 0:127], [sg, z], sg[:, 0:127], 1.0, z[:, 0:127], ALU.add, ALU.mult)
                P.ts("dve", [hid], hid[:, hc, 0:127], [sg], sg[:, 0:127], 0.5, None, ALU.mult)
            if kv == 0:
                poB, po = C.O[0], C.Of[0]
                for hc in range(2):
                    P.mm([poB], po[0:64, 0:127], [w2, hid], w2[:, hc, :], hid[:, hc, 0:127], hc == 0, hc == 1)
                P.copy("dve", [kcT], kcT[0:64, g, 0:127], [poB], po[0:64, 0:127])
            else:
                poB, po = C.O[1], C.Of[1]
                for hc in range(2):
                    P.mm([poB], po[0:127, 0:64], [hid, w2], hid[:, hc, 0:127], w2[:, hc, :], hc == 0, hc == 1)
                P.copy("dve", [vcA], vcA[0:127, g, 0:64], [poB], po[0:127, 0:64])
    if "l0proj" in C.dbg_want:
        dbg_dump(C, "kaT", kaT, kaT[0:64], [64, 2, S], BF16)
        dbg_dump(C, "ksA", ksA, ksA[0:96], [96, 2, S], BF16)
        dbg_dump(C, "va", va, va[:], [128, NT, 2, 65], BF16)
        dbg_dump(C, "kcT", kcT, kcT[0:64], [64, 2, 128], BF16)
        dbg_dump(C, "vcA", vcA, vcA[:], [128, 2, 97], BF16)
        dbg_dump(C, "gates", gates, gates[:], [128, NT, 48], F32)
    P.release(m_raw)

    if C.cut == "e":
        return
    WO = P.sb("WO", [128, 16, D], BF16)
    wov = I["w_out_e"].rearrange("(kc p) c -> p kc c", p=128)
    biasA = P.sb("biasA", [128, 2, 16, 128], BF16)
    biasB = P.sb("biasB", [128, 4, 16, 128], BF16)
    keepadd = P.sb("keepadd", [128, NT, 2, 32], F32)
    cfar = P.sb("cfar", [128, NT, 16], F32)
    cb = [P.sb(f"cb{i}", [128, 16, 128], BF16) for i in range(2)]
    qa = [P.sb(f"qa{i}", [128, 16, 128], BF16) for i in range(2)]
    qb = [P.sb(f"qb{i}", [128, 16, 128], BF16) for i in range(2)]
    for q_ in qa + qb:
        P.memset("pool", [q_], q_[:], 0.0)
    PT = [P.sb(f"PT{i}", [128, 512], BF16) for i in range(4)]
    TM = P.sb("TM", [128, 128], BF16)
    P.memset("dve", [TM], TM[:], 0.0)
    ocat = P.sb("ocat", [128, D], BF16)
    oT = P.sb("oT", [128, 16, 128], BF16)
    xst = [P.sb(f"xat{i}", [128, D], F32) for i in range(2)]
    acc = P.sb("oacc", [128, 16, 64], F32)
    tmpo = P.sb("otmp", [128, 8, 64], F32)
    sm = {k: P.sb(f"sm_{k}", shp, F32) for k, shp in dict(z=[128, 8], rz=[128, 8], w=[128, 8], ps3=[128, 8, 32],
                                                          pslc=[128, 32], sc=[128, 32], m8=[128, 8], sel=[128, 32]).items()}
    ST = [B[0], B[1]]
    Ot = C.O
    osl = [0]

    def attn_group(items, PVrhs, nheads_cols, g, Ob):
        pend = []
        for idx, it in enumerate(items):
            stb, sta = STB[st3[0] % 3]
            st3[0] += 1
            mml, (half, kt, nrows, first, last) = it
            for mi, (l, r, rb) in enumerate(mml):
                P.mm([stb], sta[0:nrows, :], rb, l, r, mi == 0, mi == len(mml) - 1)
            pend.append((stb, sta, half, kt, nrows, first, last, PVrhs, nheads_cols, g, Ob))
            if len(pend) > 2:
                finish(*pend.pop(0))
        while pend:
            finish(*pend.pop(0))

    def finish(stb, sta, half, kt, nrows, first, last, PVrhs, ncols, g, Ob):
        pt = PT[pt_ctr[0] % len(PT)]
        pt_ctr[0] += 1
        P.act([pt], pt[0:nrows, :], [stb], sta[0:nrows, :], AF.Exp)
        vt = PVrhs
        for hl in range(4):
            h8 = half * 4 + hl
            if vt is vcA:
                rhs = vcA[0:nrows, g, 0:ncols]
            else:
                rhs = vt[:, kt, g, 0:ncols]
            P.mm([Ob], Ob[:, h8, 0:ncols], [pt, vt], pt[0:nrows, hl * 128:(hl + 1) * 128], rhs, first and hl == 0, last and hl == 3, skip=True)

    st_ctr = [0]
    st3 = [0]
    pt_ctr = [0]
    STB = [(B[0], B[0].t), (B[1], B[1].t), (C.TR[0], C.TR[0].t[:, :].bitcast(F32))]
    def load_tile_inputs(i):
        qat, qbt, cbt = qa[i % 2], qb[i % 2], cb[i % 2]
        tb = i // 4
        P.dma("sp", qat[0:64, :, :], qsa.rearrange("(h d) t -> d h t", d=64)[:, :, i * 128:(i + 1) * 128],
              [qsB[("a", c, tb)] for c in range(8)], [qat], qat)
        P.dma("sp", qbt[0:64, :, :], qsb.rearrange("(h d) t -> d h t", d=64)[:, :, i * 128:(i + 1) * 128],
              [qsB[("b", c, tb)] for c in range(8)], [qbt], qbt)
        P.dma("pool", cbt[:, :, :], I["cbiasU"][120 - 8 * i:248 - 8 * i, :, :], [], [cbt], cbt)

    def out_proj(i):
        xa = xst[i % 2]
        oc = ocats[i % 2]
        for q4 in range(4):
            trb, trap = C.TR[1], C.TRap[1]
            for k in range(4):
                kc = q4 * 4 + k
                P.tr([trb], trap[:, k * 128:(k + 1) * 128], [oc, C.ident], oc[:, kc * 128:(kc + 1) * 128], C.ident[:])
            P.copy("act" if q4 % 2 == 0 else "dve", [oT], oT[:, q4 * 4:q4 * 4 + 4, :], [trb], trap.rearrange("p (a b) -> p a b", a=4))
        for db in range(4):
            pb = ST[st_ctr[0] % 2]
            st_ctr[0] += 1
            for kc in range(16):
                P.mm([pb], pb[:], [oT, WO], oT[:, kc, :], WO[:, kc, db * 512:(db + 1) * 512], kc == 0, kc == 15)
            P.tt("dve", [xa], xa[:, db * 512:(db + 1) * 512], [pb, xa], pb[:], xa[:, db * 512:(db + 1) * 512], ALU.add)
        P.dma("sp", C.xs[i * 128:(i + 1) * 128, :], xa[:], [xa], [C.xsB[i]], xa)

    ocats = [ocat, P.sb("ocat2", [128, D], BF16)]
    load_tile_inputs(0)
    P.dma("pool", biasB[:], I["biasB"], [], [biasB], biasB)
    P.dma("pool", biasA[:], I["biasA"], [], [biasA], biasA)
    P.dma("sp", keepadd[:], I["keepadd"], [], [keepadd], keepadd)
    P.dma("sp", cfar[64:96, :, :], I["cfar"], [], [cfar], cfar)
    for q4 in range(4):
        load_w_cast(C, WO, WO[:, q4 * 4:(q4 + 1) * 4, :], wov[:, q4 * 4:(q4 + 1) * 4, :])
    for i in range(NT):
        if C.cut is not None and C.cut.startswith("f") and i >= int(C.cut[1:]):
            return
        qat, qbt, cbt = qa[i % 2], qb[i % 2], cb[i % 2]
        ocat = ocats[i % 2]
        if i + 1 < NT:
            load_tile_inputs(i + 1)
        load_x_tile(C, xst[i % 2], i)
        xa = xst[i % 2]
        bo = 0
        for g in range(2):
            Ob = Ot[bo % 2]
            bo += 1
            items = []
            for half in range(2):
                hs = slice(g * 8 + half * 4, g * 8 + half * 4 + 4)
                mml = [(kcT[:, g, 0:128], qbt[:, hs, :], [kcT, qbt]),
                       (C.ident[:], cbt[:, hs, :], [C.ident, cbt])]
                items.append((mml, (half, 0, 128, True, True)))
            attn_group(items, vcA, 97, g, Ob)
            z, rz, w = sm["z"], sm["rz"], sm["w"]
            P.ts("dve", [z], z[:], [Ob], Ob[:, :, 64], 1e-30, None, ALU.max)
            P.recip([rz], rz[:], [z], z[:])
            P.tt("dve", [w], w[:], [rz, gates], rz[:], gates[:, i, g * 24:(g + 1) * 24].rearrange("p (h k) -> p h k", k=3)[:, :, 0], ALU.mult)
            P.tt("dve", [acc], acc[:, g * 8:(g + 1) * 8, :], [Ob, w], Ob[:, :, 0:64], w[:].unsqueeze(2).to_broadcast([128, 8, 64]), ALU.mult)
            ps3 = sm["ps3"]
            P.tt("dve", [ps3], ps3[:], [Ob, rz], Ob[:, :, 65:97], rz[:].unsqueeze(2).to_broadcast([128, 8, 32]), ALU.mult)
            pslc, sc, m8, sel = sm["pslc"], sm["sc"], sm["m8"], sm["sel"]
            P.op("dve", lambda e, o=pslc[:], a=ps3[:].rearrange("p h n -> p n h"): e.tensor_reduce(out=o, in_=a, axis=AX.X, op=ALU.add), [ps3], [pslc])
            P.tt("dve", [sc], sc[:], [pslc, keepadd], pslc[:], keepadd[:, i, 0, :], ALU.mult)
            P.tt("dve", [sc], sc[:], [sc, keepadd], sc[:], keepadd[:, i, 1, :], ALU.add)
            P.op("dve", lambda e, o=m8[:], a=sc[:]: e.max(out=o, in_=a), [sc], [m8])
            P.ts("dve", [sel], sel[:], [sc, m8], sc[:], m8[:, 7:8], None, ALU.is_ge)
            P.ts("dve", [TM], TM[:, 64:96], [sel], sel[:], 1.0, -NEGM, ALU.subtract, ALU.mult)
            trb, trap = C.TR[1], C.TRap[1]
            P.tr([trb], trap[:, 0:128], [TM, C.ident], TM[:], C.ident[:])
            P.tt("dve", [qbt], qbt[64:96, g * 8:(g + 1) * 8, :], [trb, cfar],
                 trap[64:96, 0:128].unsqueeze(1).to_broadcast([32, 8, 128]),
                 cfar[64:96, i, g * 8:(g + 1) * 8].unsqueeze(2).to_broadcast([32, 8, 128]), ALU.add)
        for g in range(2):
            Ob = Ot[bo % 2]
            bo += 1
            kts = [kt for kt in (i - 1, i) if kt >= 0]
            items = []
            for half in range(2):
                hs = slice(g * 8 + half * 4, g * 8 + half * 4 + 4)
                for kt in kts:
                    kind = 0 if kt == i else 1
                    mml = [(kaT[:, g, kt * 128:(kt + 1) * 128], qat[:, hs, :], [kaT, qat]),
                           (C.ident[:], biasA[:, kind, hs, :], [C.ident, biasA])]
                    items.append((mml, (half, kt, 128, kt == kts[0], kt == kts[-1])))
            attn_group(items, va, 65, g, Ob)
            z, rz = sm["z"], sm["rz"]
            P.tt("dve", [z], z[:], [Ob, sinkexp], Ob[:, :, 64], sinkexp[:, g * 8:(g + 1) * 8], ALU.add)
            P.recip([rz], rz[:], [z], z[:])
            P.tt("dve", [ocat], ocat[:, g * 512:(g + 1) * 512].rearrange("p (h d) -> p h d", d=64), [Ob, rz], Ob[:, :, 0:64],
                 rz[:].unsqueeze(2).to_broadcast([128, 8, 64]), ALU.mult)
        if i > 0:
            out_proj(i - 1)
        for br in ("win", "slc"):
            for g in range(2):
                Ob = Ot[bo % 2]
                bo += 1
                items = []
                kts = list(range(max(0, i - 4), i + 1)) if br == "win" else list(range(0, i + 1))
                for half in range(2):
                    hs = slice(g * 8 + half * 4, g * 8 + half * 4 + 4)
                    for kt in kts:
                        dk = i - kt
                        if br == "win":
                            kind = {0: 0, 1: 1, 2: 2, 3: 2, 4: 3}[dk]
                            mml = [(kwT[:, g, kt * 128:(kt + 1) * 128], qbt[:, hs, :], [kwT, qbt]),
                                   (C.ident[:], biasB[:, kind, hs, :], [C.ident, biasB])]
                        else:
                            mml = [(ksA[:, g, kt * 128:(kt + 1) * 128], qbt[:, hs, :], [ksA, qbt])]
                            if dk <= 1:
                                mml.append((C.ident[:], biasB[:, dk, hs, :], [C.ident, biasB]))
                        items.append((mml, (half, kt, 128, kt == kts[0], kt == kts[-1])))
                attn_group(items, vw if br == "win" else vs, 65, g, Ob)
                rz, w = sm["rz"], sm["w"]
                P.recip([rz], rz[:], [Ob], Ob[:, :, 64])
                gi = 2 if br == "win" else 1
                P.tt("dve", [w], w[:], [rz, gates], rz[:], gates[:, i, g * 24:(g + 1) * 24].rearrange("p (h k) -> p h k", k=3)[:, :, gi], ALU.mult)
                P.tt("dve", [tmpo], tmpo[:], [Ob, w], Ob[:, :, 0:64], w[:].unsqueeze(2).to_broadcast([128, 8, 64]), ALU.mult)
                if br == "win":
                    P.tt("pool", [acc], acc[:, g * 8:(g + 1) * 8, :], [acc, tmpo], acc[:, g * 8:(g + 1) * 8, :], tmpo[:], ALU.add)
                else:
                    P.tt("pool", [ocat], ocat[:, 1024 + g * 512:1024 + (g + 1) * 512].rearrange("p (h d) -> p h d", d=64), [acc, tmpo],
                         acc[:, g * 8:(g + 1) * 8, :], tmpo[:], ALU.add)
        if "ocat" in C.dbg_want:
            dbg_dump(C, f"ocat{i}", ocat, ocat[:], [128, D], BF16)
    out_proj(NT - 1)
    C.x_src = C.xs
    C.x_srcB = C.xsB
    P.release(m_persist)


def dbg_dump(C, name, buf, ap, shape, dtype):
    P = C.P
    d = C.nc.dram_tensor("dbg_" + name, list(shape), dtype, kind="ExternalOutput").ap()
    C.dbg[name] = (shape, dtype)
    oid = P.dma("sp", d, ap, [buf], [], buf)
    C.out_ops.append(oid)


def mlp(C, layer, final):
    P, I, B = C.P, C.I, C.B
    m = P.mark()
    gbc = P.sb("gbcm", [128, D], F32)
    gfin = gbc
    TB = 8
    hT = P.sb("hTm", [128, 16, TB * 128], BF16)
    yacc = P.sb("yacc", [128, TB, D], F32)
    xst = [P.sb(f"xsm{i}", [128, D], F32) for i in range(2)]
    hb = P.sb("hbm", [128, D], BF16)
    tmp = (hb, P.sb("ssqm", [128, 1], F32), P.sb("sdm", [128, 2], F32), P.sb("rstdm", [128, 1], F32))
    wu = [P.sb(f"wu{i}", [128, 16, 512], BF16) for i in range(2)]
    wd = [P.sb(f"wd{i}", [128, 4, D], BF16) for i in range(2)]
    uT = [P.sb(f"uT{i}", [128, 4, 512], BF16) for i in range(2)]
    rl = [P.sb(f"rl{i}", [128, 512], F32) for i in range(2)]
    wuv = I[f"w_up{layer}"].rearrange("(kc p) f -> p kc f", p=128)
    wdv = I[f"w_down{layer}"].rearrange("(fc p) d -> p fc d", p=128)
    nu = 0
    no = 0
    NG = 16

    def load_group(gi):
        wub, wdb = wu[gi % 2], wd[gi % 2]
        load_w_cast(C, wub, wub[:], wuv[:, :, gi * 512:(gi + 1) * 512])
        load_w_cast(C, wdb, wdb[:], wdv[:, gi * 4:(gi + 1) * 4, :])

    def up(gi, tb):
        nonlocal nu
        wub = wu[gi % 2]
        u = uT[(gi * 2 + tb) % 2]
        for fc in range(4):
            pb = B[nu % 2]
            r = rl[nu % 2]
            nu += 1
            for kc in range(16):
                P.mm([pb], pb[:], [wub, hT], wub[:, kc, fc * 128:(fc + 1) * 128], hT[:, kc, tb * 512:(tb + 1) * 512], kc == 0, kc == 15)
            P.act([r], r[:], [pb], pb[:], AF.Relu)
            P.act([u], u[:, fc, :], [r], r[:], AF.Square)

    def down(gi, tb):
        nonlocal no
        wdb = wd[gi % 2]
        u = uT[(gi * 2 + tb) % 2]
        for tt in range(4):
            j = tb * 4 + tt
            for dbp in range(2):
                ob, of = C.O[no % 2], C.Of[no % 2]
                no += 1
                for dbi in range(2):
                    db = dbp * 2 + dbi
                    for fc in range(4):
                        P.mm([ob], of[:, dbi * 512:(dbi + 1) * 512], [u, wdb], u[:, fc, tt * 128:(tt + 1) * 128],
                             wdb[:, fc, db * 512:(db + 1) * 512], fc == 0, fc == 3)
                ys = yacc[:, j, dbp * 1024:(dbp + 1) * 1024]
                if gi == 0:
                    P.copy("dve", [yacc], ys, [ob], of[:, :])
                else:
                    P.tt("dve", [yacc], ys, [ob, yacc], of[:, :], ys, ALU.add)

    for blk in range(NT // TB):
        P.dma("sp", gbc[:], bcast_row(I["norm_mlp"][layer:layer + 1, :]), [], [gbc], gbc)
        load_group(0)
        load_group(1)
        norm_transpose(C, range(blk * TB, (blk + 1) * TB), gbc, hT, 0, xst, hb, tmp)
        units = [(gi, tb) for gi in range(NG) for tb in range(TB // 4)]
        up(*units[0])
        for k, (gi, tb) in enumerate(units):
            if k + 1 < len(units):
                up(*units[k + 1])
            down(gi, tb)
            if tb == TB // 4 - 1 and 1 <= gi + 1 and gi + 2 < NG:
                load_group(gi + 2)
        if final:
            P.dma("sp", gfin[:], bcast_row(I["norm_final"]), [], [gfin], gfin)
        for j in range(TB):
            t = blk * TB + j
            if not final:
                P.dma("pool", C.xs[t * 128:(t + 1) * 128, :], yacc[:, j, :], [yacc, C.xsB[t]], [C.xsB[t]], yacc, accum_op=ALU.add)
                continue
            xa = xst[j % 2]
            load_x_tile(C, xa, t)
            P.tt("dve", [xa], xa[:], [xa, yacc], xa[:], yacc[:, j, :], ALU.add)
            if True:
                ot = hb
                yo = yacc[:, j, :]
                norm_rows(C, xa[:], xa, D, gfin, yo, yacc, tmp)
                oid = P.dma("sp", C.out[t * 128:(t + 1) * 128, :], yo, [yacc], [], yacc)
                C.out_ops.append(oid)
    C.x_src = C.xs
    C.x_srcB = C.xsB
    P.release(m)


def layer1_mixer(C):
    P, I, B = C.P, C.I, C.B
    m_all = P.mark()
    SCALE = 192 ** -0.5
    cqnT = P.sb("cqnT", [128, 6, S], BF16)
    ckvT = P.sb("ckvT", [128, 4, S], BF16)
    krT = P.sb("krT", [128, S], BF16)
    P.memset("pool", [krT], krT[:], 0.0)
    cs = P.sb("cs", [128, S], F32)
    P.dma("sp", cs[0:64, :], I["cs"][:, 0, :], [], [cs], cs)
    P.dma("sp", cs[64:128, :], I["cs"][:, 1, :], [], [cs], cs)
    mmask = P.sb("mmask", [128, 128], BF16)
    P.dma("pool", mmask[:], I["mlamask"], [], [mmask], mmask)
    m0 = P.mark()
    hTs = [P.sb(f"hT1_{tb}", [128, 16, 512], BF16) for tb in range(4)]
    gbc = P.sb("gbc1", [128, D], F32)
    P.dma("sp", gbc[:], bcast_row(I["norm_mix_o"]), [], [gbc], gbc)
    qg = P.sb("qg", [128, 768], F32)
    kg = P.sb("kg", [128, 512], F32)
    P.dma("sp", qg[:], bcast_row(I["q_norm"]), [], [qg], qg)
    P.dma("sp", kg[:], bcast_row(I["kv_norm"]), [], [kg], kg)
    hb = P.sb("hb1", [128, D], BF16)
    tmp = (P.sb("junk1", [128, D], BF16), P.sb("ssq1", [128, 1], F32), P.sb("sd1", [128, 2], F32), P.sb("rstd1", [128, 1], F32))
    WI = P.sb("WI", [128, 16, 1344], BF16)
    wiv = I["w_in_o"].rearrange("(kc p) c -> p kc c", p=128)
    load_w_cast(C, WI, WI[:, 0:8, :], wiv[:, 0:8, :])
    load_w_cast(C, WI, WI[:, 8:16, :], wiv[:, 8:16, :])
    WIr = P.sb("WIr", [128, 16, 128], BF16)
    P.copy("pool", [WIr], WIr[:, :, 0:64], [WI], WI[:, :, 1280:1344])
    P.ts("pool", [WIr], WIr[:, :, 64:96], [WI], WI[:, :, 1312:1344], -1.0, None, ALU.mult)
    P.copy("pool", [WIr], WIr[:, :, 96:128], [WI], WI[:, :, 1280:1312])
    m_x = P.mark()
    xst = [P.sb(f"xs1{i}", [128, D], F32) for i in range(2)]
    cn = P.sb("cn", [128, 1280], BF16)
    ssb = P.sb("ssb", [128, 4], F32)
    for t in range(NT):
        if t % 4 == 0:
            norm_transpose(C, range(t, t + 4), gbc, hTs[t // 4], 0, xst, hb, tmp)
        hT = hTs[t // 4]
        tl = t % 4
        pa, pbk, pc = B[0], B[1], C.O[t % 2]
        paa, pba, pca = B[0].t, B[1].t, C.Of[t % 2]
        for (pb, pap, c0, w) in ((pa, paa, 0, 384), (pbk, pba, 384, 384), (pc, pca, 768, 512)):
            for kc in range(16):
                P.mm([pb], pap[:, 0:w], [hT, WI], hT[:, kc, tl * 128:(tl + 1) * 128], WI[:, kc, c0:c0 + w], kc == 0, kc == 15)
        junk = tmp[0]
        P.act([junk, ssb], junk[:, 0:384], [pa], pa[:, 0:384], AF.Square, accum_out=ssb[:, 0:1])
        P.act([junk, ssb], junk[:, 384:768], [pbk], pbk[:, 0:384], AF.Square, accum_out=ssb[:, 1:2])
        P.act([junk, ssb], junk[:, 768:1280], [pc], pca[:, 0:512], AF.Square, accum_out=ssb[:, 2:3])
        sd = tmp[2]
        rq = tmp[3]
        rk = tmp[1]
        P.tt("dve", [ssb], ssb[:, 3:4], [ssb], ssb[:, 0:1], ssb[:, 1:2], ALU.add)
        P.ts("dve", [sd], sd[:, 0:1], [ssb], ssb[:, 3:4], 1.0 / 768, EPS, ALU.mult, ALU.add)
        P.ts("dve", [sd], sd[:, 1:2], [ssb], ssb[:, 2:3], 1.0 / 512, EPS, ALU.mult, ALU.add)
        P.tt("pool", [rq], rq[:, 0:1], [sd, C.neghalf], sd[:, 0:1], C.neghalf[:, 0:1], ALU.pow)
        P.tt("pool", [rk], rk[:, 0:1], [sd, C.neghalf], sd[:, 1:2], C.neghalf[:, 0:1], ALU.pow)
        P.stt([cn], cn[:, 0:384], [pa, rq, qg], pa[:, 0:384], rq[:, 0:1], qg[:, 0:384], ALU.mult, ALU.mult)
        P.stt([cn], cn[:, 384:768], [pbk, rq, qg], pbk[:, 0:384], rq[:, 0:1], qg[:, 384:768], ALU.mult, ALU.mult)
        P.stt([cn], cn[:, 768:1280], [pc, rk, kg], pca[:, 0:512], rk[:, 0:1], kg[:, 0:512], ALU.mult, ALU.mult)
        for grp, (dstT, nck, cbase) in enumerate(((cqnT, 4, 0), (cqnT, 2, 512), (ckvT, 4, 768))):
            trb, trap = C.TR[grp % 2], C.TRap[grp % 2]
            for k in range(nck):
                P.tr([trb], trap[:, k * 128:(k + 1) * 128], [cn, C.ident], cn[:, cbase + k * 128:cbase + (k + 1) * 128], C.ident[:])
            k0 = 0 if grp != 1 else 4
            P.copy("act" if grp % 2 == 0 else "dve", [dstT], dstT[:, k0:k0 + nck, t * 128:(t + 1) * 128], [trb],
                   trap[:, 0:nck * 128].rearrange("p (a b) -> p a b", a=nck))
    t1 = P.sb("rt1", [128, 512], F32)
    t2 = P.sb("rt2", [64, 512], F32)
    for tb in range(4):
        p1 = B[tb % 2]
        for kc in range(16):
            P.mm([p1], p1[:, :], [WIr, hTs[tb]], WIr[:, kc, :], hTs[tb][:, kc, :], kc == 0, kc == 15)
        P.tt("dve", [t1], t1[:], [p1, cs], p1[:, :], cs[:, tb * 512:(tb + 1) * 512], ALU.mult)
        P.copy("act", [t2], t2[:], [t1], t1[64:128, :])
        P.tt("pool", [krT], krT[0:64, tb * 512:(tb + 1) * 512], [t1, t2], t1[0:64, :], t2[:], ALU.add)
    if "l1lat" in C.dbg_want:
        dbg_dump(C, "cqnT", cqnT, cqnT[:], [128, 6, S], BF16)
        dbg_dump(C, "ckvT", ckvT, ckvT[:], [128, 4, S], BF16)
        dbg_dump(C, "krT", krT, krT[0:64], [64, S], BF16)
    P.release(m0)
    oT = P.sb("oT1", [128, 16, S], BF16)
    m_after_oT = P.mark()
    wq = [P.sb(f"wq{i}", [128, 6, 192], BF16) for i in range(2)]
    wqr = [P.sb(f"wqr{i}", [128, 6, 128], BF16) for i in range(2)]
    wkv = [P.sb(f"wkv{i}", [128, 4, 256], BF16) for i in range(2)]
    kT = [P.sb(f"kTh{i}", [128, S], BF16) for i in range(2)]
    vh = [P.sb(f"vh{i}", [128, NT, 129], BF16) for i in range(2)]
    for v_ in vh:
        P.memset("pool", [v_], v_[:, :, 128:129], 1.0)
    qn = [P.sb(f"qnh{i}", [128, S], BF16) for i in range(2)]
    qr = [P.sb(f"qrh{i}", [128, S], BF16) for i in range(2)]
    for q_ in qr:
        P.memset("pool", [q_], q_[:], 0.0)
    PT = [P.sb(f"PT1{i}", [128, 512], BF16) for i in range(4)]
    STB = [(B[0], B[0].t), (B[1], B[1].t), (C.TR[0], C.TR[0].t[:, :].bitcast(F32))]
    t1s = [t1, P.sb("rt1b", [128, 512], F32)]
    t2s = [t2, P.sb("rt2b", [64, 512], F32)]
    ob = [P.sb(f"ob{i}", [128, 4, 128], BF16) for i in range(2)]
    rz = [P.sb(f"rz1{i}", [128, 4], F32) for i in range(2)]
    wqv = I["w_q_up"].rearrange("(kc p) c -> p kc c", p=128)
    wkvv = I["w_kv_up"].rearrange("(kc p) c -> p kc c", p=128)
    stc = 0
    ptc = 0
    def load_head(h):
        s2 = h % 2
        load_w_cast(C, wq[s2], wq[s2][:], wqv[:, :, h * 192:(h + 1) * 192])
        load_w_cast(C, wkv[s2], wkv[s2][:], wkvv[:, :, h * 256:(h + 1) * 256])
    load_head(0)
    pend_qb = []
    for h in range(16):
        s2 = h % 2
        if h + 1 < 16:
            load_head(h + 1)
        P.copy("pool", [wqr[s2]], wqr[s2][:, :, 0:64], [wq[s2]], wq[s2][:, :, 128:192])
        P.ts("pool", [wqr[s2]], wqr[s2][:, :, 64:96], [wq[s2]], wq[s2][:, :, 160:192], -1.0, None, ALU.mult)
        P.copy("pool", [wqr[s2]], wqr[s2][:, :, 96:128], [wq[s2]], wq[s2][:, :, 128:160])
        pjc = [0]

        def nextpj():
            b_, a_ = STB[pjc[0] % 3]
            pjc[0] += 1
            return b_, a_
        for tb in range(4):
            pjB, pj = nextpj()
            for kc in range(4):
                P.mm([pjB], pj[:], [wkv[s2], ckvT], wkv[s2][:, kc, 0:128], ckvT[:, kc, tb * 512:(tb + 1) * 512], kc == 0, kc == 3)
            P.copy("dve", [kT[s2]], kT[s2][:, tb * 512:(tb + 1) * 512], [pjB], pj[:])
        for t4 in range(4):
            pjB, pj = nextpj()
            for tt in range(4):
                t = t4 * 4 + tt
                for kc in range(4):
                    P.mm([pjB], pj[:, tt * 128:(tt + 1) * 128], [ckvT, wkv[s2]], ckvT[:, kc, t * 128:(t + 1) * 128], wkv[s2][:, kc, 128:256], kc == 0, kc == 3)
            P.copy("act", [vh[s2]], vh[s2][:, t4 * 4:(t4 + 1) * 4, 0:128], [pjB], pj[:].rearrange("p (a b) -> p a b", a=4))
        for tb in range(4):
            pjB, pj = nextpj()
            for kc in range(6):
                P.mm([pjB], pj[:], [wq[s2], cqnT], wq[s2][:, kc, 0:128], cqnT[:, kc, tb * 512:(tb + 1) * 512], kc == 0, kc == 5)
            P.copy("act", [qn[s2]], qn[s2][:, tb * 512:(tb + 1) * 512], [pjB], pj[:])
            pjB, pj = nextpj()
            for kc in range(6):
                P.mm([pjB], pj[:, :], [wqr[s2], cqnT], wqr[s2][:, kc, :], cqnT[:, kc, tb * 512:(tb + 1) * 512], kc == 0, kc == 5)
            t1 = t1s[tb % 2]
            t2 = t2s[tb % 2]
            P.tt("dve", [t1], t1[:], [pjB, cs], pj[:, :], cs[:, tb * 512:(tb + 1) * 512], ALU.mult)
            P.copy("act", [t2], t2[:], [t1], t1[64:128, :])
            P.tt("pool", [qr[s2]], qr[s2][0:64, tb * 512:(tb + 1) * 512], [t1, t2], t1[0:64, :], t2[:], ALU.add)
        for Qb in range(4):
            Ob, Of = C.O[Qb % 2], C.Of[Qb % 2]
            nkt = 4 * Qb + 4
            pend = []

            def fin(stb, sta, kt, c0, Qb=Qb, Ob=Ob, Of=Of, s2=s2):
                nonlocal ptc
                pt = PT[ptc % len(PT)]
                ptc += 1
                P.act([pt], pt[:, c0:512], [stb], sta[:, c0:512], AF.Exp, scale=SCALE)
                for jj in range(c0 // 128, 4):
                    last = (kt == 4 * Qb + jj)
                    P.mm([Ob], Of[:, jj * 256:jj * 256 + 129], [pt, vh[s2]], pt[:, jj * 128:(jj + 1) * 128], vh[s2][:, kt, :], kt == 0 and jj % 2 == 0, last, skip=True)

            for kt in range(nkt):
                stb, sta = STB[stc % 3]
                stc += 1
                c0 = max(0, kt - 4 * Qb) * 128
                q0 = Qb * 512
                kl = kT[s2][:, kt * 128:(kt + 1) * 128]
                krl = krT[:, kt * 128:(kt + 1) * 128]
                if kt >= 4 * Qb:
                    P.mm([stb], sta[:, c0:c0 + 128], [C.ident, mmask], C.ident[:], mmask[:], True, False)
                    P.mm([stb], sta[:, c0:c0 + 128], [kT[s2], qn[s2]], kl, qn[s2][:, q0 + c0:q0 + c0 + 128], False, False)
                    P.mm([stb], sta[:, c0:c0 + 128], [krT, qr[s2]], krl, qr[s2][:, q0 + c0:q0 + c0 + 128], False, True)
                    c1 = c0 + 128
                else:
                    c1 = c0
                if c1 < 512:
                    P.mm([stb], sta[:, c1:512], [kT[s2], qn[s2]], kl, qn[s2][:, q0 + c1:q0 + 512], True, False)
                    P.mm([stb], sta[:, c1:512], [krT, qr[s2]], krl, qr[s2][:, q0 + c1:q0 + 512], False, True)
                pend.append((stb, sta, kt, c0))
                if len(pend) > 2:
                    fin(*pend.pop(0))
                if kt == 2 and pend_qb:
                    pend_qb.pop(0)()
            while pend:
                fin(*pend.pop(0))
            def finish_qb(Qb=Qb, Ob=Ob, Of=Of, h=h):
                rzb = rz[Qb % 2]
                obb = ob[Qb % 2]
                O4 = Of.rearrange("p (a b) -> p a b", a=4)
                P.recip([rzb], rzb[:], [Ob], O4[:, :, 128])
                P.tt("dve", [obb], obb[:], [Ob, rzb], O4[:, :, 0:128], rzb[:].unsqueeze(2).to_broadcast([128, 4, 128]), ALU.mult)
                trb, trap = C.TR[1], C.TRap[1]
                for jj in range(4):
                    P.tr([trb], trap[:, jj * 128:(jj + 1) * 128], [obb, C.ident], obb[:, jj, :], C.ident[:])
                P.copy("act", [oT], oT[:, h, Qb * 512:(Qb + 1) * 512], [trb], trap[:, :])
            pend_qb.append(finish_qb)
    while pend_qb:
        pend_qb.pop(0)()
    if "l1o" in C.dbg_want:
        dbg_dump(C, "oT1", oT, oT[:], [128, 16, S], BF16)
    P.release(m_after_oT)
    WO = P.sb("WO1", [128, 16, D], BF16)
    wov = I["w_out_o"].rearrange("(kc p) c -> p kc c", p=128)
    for q4 in range(4):
        load_w_cast(C, WO, WO[:, q4 * 4:(q4 + 1) * 4, :], wov[:, q4 * 4:(q4 + 1) * 4, :])
    xst = [P.sb(f"xo1{i}", [128, D], F32) for i in range(2)]
    for t in range(NT):
        xa = xst[t % 2]
        load_x_tile(C, xa, t)
        for db in range(4):
            pb = B[db % 2]
            for kc in range(16):
                P.mm([pb], pb[:], [oT, WO], oT[:, kc, t * 128:(t + 1) * 128], WO[:, kc, db * 512:(db + 1) * 512], kc == 0, kc == 15)
            P.tt("dve", [xa], xa[:, db * 512:(db + 1) * 512], [pb, xa], pb[:], xa[:, db * 512:(db + 1) * 512], ALU.add)
        P.dma("sp", C.xs[t * 128:(t + 1) * 128, :], xa[:], [xa], [C.xsB[t]], xa)
    C.x_src = C.xs
    C.x_srcB = C.xsB
    P.release(m_all)


_CACHE = {}


def _prep_inputs(inputs, used=None):
    tabs = _host_tables(np.asarray(inputs["rel_bias"], np.float32))
    shared = dict(tabs)
    sq = lambda k: np.ascontiguousarray(np.asarray(inputs[k], np.float32)[0])
    for k in ("w_in_e", "cmp_pos_k", "cmp_pos_v", "cmp_k_w1", "cmp_k_w2", "cmp_v_w1", "cmp_v_w2", "w_out_e",
              "w_in_o", "w_q_up", "w_kv_up", "w_out_o"):
        shared[k] = sq(k)
    for k in ("norm_mix_e", "sinks", "norm_mix_o", "q_norm", "kv_norm"):
        shared[k] = np.ascontiguousarray(np.asarray(inputs[k], np.float32).reshape(1, -1))
    shared["norm_mlp"] = np.ascontiguousarray(np.asarray(inputs["norm_mlp"], np.float32))
    shared["norm_final"] = np.ascontiguousarray(np.asarray(inputs["norm_final"], np.float32).reshape(1, -1))
    for l in range(2):
        shared[f"w_up{l}"] = np.ascontiguousarray(np.asarray(inputs["w_up"], np.float32)[l])
        shared[f"w_down{l}"] = np.ascontiguousarray(np.asarray(inputs["w_down"], np.float32)[l])
    x = np.asarray(inputs["x"], np.float32)
    in_maps = []
    for c in range(NCORES):
        m = dict(shared)
        m["x"] = np.ascontiguousarray(x[c])
        if used is not None:
            m = {k: v for k, v in m.items() if k in used}
        in_maps.append(m)
    return in_maps


def kernel(**inputs):
    if "nc" not in _CACHE:
        _CACHE["nc"], _CACHE["P"] = build_program()
    nc = _CACHE["nc"]
    in_maps = _prep_inputs(inputs, _CACHE["P"].used_inputs)
    res = run_bass_kernel_spmd(nc, in_maps, core_ids=list(range(NCORES)))
    return np.stack([np.asarray(r["out"], np.float32) for r in res.results], axis=0)
```

```python
import math
import numpy as np
import concourse.bass as bass
import concourse.mybir as mybir
from concourse.bass_utils import run_bass_kernel_spmd

F32 = mybir.dt.float32
BF16 = mybir.dt.bfloat16
AF = mybir.ActivationFunctionType
ALU = mybir.AluOpType
AX = mybir.AxisListType

S = 2048
D = 2048
NT = 16
DFF = 8192
NEGM = -30000.0
EPS = 1e-6
NCORES = 8


class Buf:
    __slots__ = ("name", "t", "last_w", "readers", "off", "size", "psum")

    def __init__(self, name, t=None):
        self.psum = False
        self.name = name
        self.t = t
        self.last_w = None
        self.readers = []
        self.off = None
        self.size = 0

    def __getitem__(self, k):
        return self.t[k]


class Prog:
    ENGS = ("pe", "act", "dve", "pool", "sp")
    SB_LO = 16512
    SB_HI = 229344

    def __init__(self, nc):
        self.nc = nc
        self.eng = {"pe": nc.tensor, "act": nc.scalar, "dve": nc.vector,
                    "pool": nc.gpsimd, "sp": nc.sync}
        self.ops = []
        self.top = self.SB_LO
        self.allocs = []
        self.uid = 0

    def sb(self, name, shape, dtype):
        esz = 2 if dtype == BF16 else 4
        n = 1
        for s_ in shape[1:]:
            n *= s_
        size = (n * esz + 63) // 64 * 64
        off = self.top
        assert off + size <= self.SB_HI, f"SBUF overflow allocating {name}: {off}+{size}"
        self.top = off + size
        self.uid += 1
        t = self.nc.alloc_sbuf_tensor_at(f"{name}_{self.uid}", list(shape), dtype, offset=off)
        b = Buf(f"{name}_{self.uid}", t)
        b.off, b.size = off, size
        inh = set()
        for (o2, s2, b2) in self.allocs:
            if o2 < off + size and off < o2 + s2:
                if b2.last_w is not None:
                    inh.add(b2.last_w)
                inh.update(b2.readers)
        b.readers = list(inh)
        self.allocs.append((off, size, b))
        return b

    def mark(self):
        return self.top

    def release(self, m):
        self.top = m

    def ps(self, name, shape, dtype=F32):
        b = Buf(name, self.nc.alloc_psum_tensor(name, list(shape), dtype))
        b.psum = True
        return b

    def tok(self, name):
        return Buf(name, None)

    def op(self, engine, fn, reads=(), writes=(), dma=None, extra=()):
        oid = len(self.ops)
        deps = set(extra)
        for b in reads:
            if b.last_w is not None:
                deps.add(b.last_w)
            if b.psum:
                deps.update(r for r in b.readers if self.ops[r][0] != engine)
        for b in writes:
            if b.last_w is not None:
                deps.add(b.last_w)
            deps.update(b.readers)
        for b in writes:
            b.last_w = oid
            b.readers = []
        for b in reads:
            if b not in writes:
                if dma is None:
                    b.readers = [r for r in b.readers if not (self.ops[r][0] == engine and self.ops[r][3] is None)]
                b.readers.append(oid)
        self.ops.append((engine, fn, deps, dma))
        return oid

    def dma(self, engine, out_ap, in_ap, reads, writes, chan, **kw):
        return self.op(engine, lambda e: e.dma_start(out=out_ap, in_=in_ap, **kw), reads, writes, dma=chan)

    def mm(self, W, out, R, lhsT, rhs, start, stop, skip=False):
        return self.op("pe", lambda e: e.matmul(out, lhsT=lhsT, rhs=rhs, start=start, stop=stop, skip_group_check=skip), R, W)

    def tr(self, W, out, R, in_, ident):
        return self.op("pe", lambda e: e.transpose(out=out, in_=in_, identity=ident), R, W)

    def act(self, W, out, R, in_, func, **kw):
        return self.op("act", lambda e: e.activation(out=out, in_=in_, func=func, **kw), R, W)

    def tt(self, eng, W, out, R, in0, in1, op):
        return self.op(eng, lambda e: e.tensor_tensor(out=out, in0=in0, in1=in1, op=op), R, W)

    def ts(self, eng, W, out, R, in0, s1, s2, op0, op1=None):
        if op1 is None:
            return self.op(eng, lambda e: e.tensor_scalar(out=out, in0=in0, scalar1=s1, scalar2=None, op0=op0), R, W)
        return self.op(eng, lambda e: e.tensor_scalar(out=out, in0=in0, scalar1=s1, scalar2=s2, op0=op0, op1=op1), R, W)

    def stt(self, W, out, R, in0, scalar, in1, op0, op1):
        return self.op("dve", lambda e: e.scalar_tensor_tensor(out=out, in0=in0, scalar=scalar, in1=in1, op0=op0, op1=op1), R, W)

    def copy(self, eng, W, out, R, in_):
        if eng == "act":
            return self.op("act", lambda e: e.activation(out=out, in_=in_, func=AF.Copy), R, W)
        return self.op(eng, lambda e: e.tensor_copy(out=out, in_=in_), R, W)

    def recip(self, W, out, R, in_):
        return self.op("dve", lambda e: e.reciprocal(out=out, in_=in_), R, W)

    def memset(self, eng, W, out, val):
        return self.op(eng, lambda e: e.memset(out, val), (), W)

    def emit(self):
        nc = self.nc
        ops = self.ops
        n = len(ops)

        def skip(e, dma, d):
            return e == "pe" and dma is None and ops[d][0] == "pe" and ops[d][3] is None

        needed = [False] * n
        for (e, fn, deps, dma) in ops:
            for d in deps:
                if not skip(e, dma, d):
                    needed[d] = True
        sems = {"e_" + e: nc.alloc_semaphore(name=f"sem_{e}") for e in self.ENGS}
        ecount = {e: 0 for e in self.ENGS}
        chan_count = {}
        event = [None] * n
        waited = {e: {} for e in self.ENGS}
        nwaits = 0
        for i, (e, fn, deps, dma) in enumerate(ops):
            eng = self.eng[e]
            req = {}
            for d in deps:
                if skip(e, dma, d):
                    continue
                k, v = event[d]
                if k in chan_count:
                    v = chan_count[k]
                if req.get(k, 0) < v:
                    req[k] = v
            for k, v in req.items():
                if waited[e].get(k, 0) >= v:
                    continue
                eng.wait_ge(sems[k], v)
                waited[e][k] = v
                nwaits += 1
            ins = fn(eng)
            if dma is not None:
                key = "c_" + dma.name + "_" + e
                if key not in sems:
                    sems[key] = nc.alloc_semaphore(name="sem_" + key)
                    chan_count[key] = 0
                chan_count[key] += 16
                ins.then_inc(sems[key], 16)
                event[i] = (key, chan_count[key])
            elif needed[i]:
                ecount[e] += 1
                ins.then_inc(sems["e_" + e], 1)
                event[i] = ("e_" + e, ecount[e])
            else:
                event[i] = ("e_" + e, ecount[e] + 1)
        self.stats = dict(n_ops=n, n_waits=nwaits, counts=dict(ecount), n_sems=len(sems))


def _t5_bucket(dist):
    dist = np.maximum(dist, 0)
    d = np.maximum(dist, 1).astype(np.float32)
    large = 16 + (np.log(d / np.float32(16)) / np.float32(math.log(128 / 16)) * np.float32(16)).astype(np.int32)
    large = np.minimum(large, 31)
    return np.where(dist < 16, dist, large).astype(np.int64)


def _host_tables(rel_bias):
    rb = np.concatenate([rel_bias.astype(np.float32), np.full((1, 32), NEGM, np.float32)], axis=0)
    k = np.arange(128)[:, None]
    q = np.arange(128)[None, :]

    def tile(dist, valid, heads):
        idx = np.where(valid, _t5_bucket(dist), 32)
        return rb[idx][:, :, heads].transpose(0, 2, 1)

    hA = np.arange(0, 16)
    hB = np.arange(16, 32)
    d0 = q - k
    d1 = q - k + 128
    d4 = q - k + 512
    ones = np.ones((128, 128), bool)
    biasA = np.stack([tile(d0, (d0 >= 0) & (d0 < 128), hA), tile(d1, (d1 >= 0) & (d1 < 128), hA)])
    biasB = np.stack([tile(d0, d0 >= 0, hB), tile(d1, ones, hB), tile(np.full((128, 128), 1000), ones, hB),
                      tile(d4, d4 < 512, hB)])
    cc = (np.arange(248) - 120)[:, None]
    tq = np.arange(128)[None, :]
    dc = tq - 16 * cc - 31
    idx = np.where(dc >= 0, _t5_bucket(dc), 32)
    cbiasU = rb[idx][:, :, hB].transpose(0, 2, 1)
    cfar = np.zeros((16, 32, 16), np.float32)
    for i in range(16):
        for n in range(32):
            if n < 2 * i - 2:
                cfar[i, n, :] = rel_bias[31, 16:32]
    t = np.arange(S)[:, None]
    n = np.arange(32)[None, :]
    cur = t // 64
    future = n * 64 > t
    forced = (n == 0) | (n == cur) | (n == cur - 1)
    keep = np.where(future | forced, 0.0, 1.0).astype(np.float32)
    add = np.where(future, -1e30, np.where(forced, 1e4, 0.0)).astype(np.float32)
    keepadd = np.stack([keep, add], axis=1).reshape(16, 128, 2, 32).transpose(1, 0, 2, 3)
    c_start = np.arange(127) * 16
    s_start = np.arange(32) * 64
    overlap = ((c_start[:, None] <= s_start[None] + 63) & (c_start[:, None] + 31 >= s_start[None])).astype(np.float32)
    bind = (np.arange(S)[None, :] // 64 == np.arange(32)[:, None]).astype(np.float32)
    inv = 1.0 / (10000.0 ** (np.arange(0, 64, 2, dtype=np.float32) / 64))
    ang = np.arange(S, dtype=np.float32)[:, None] * inv[None].astype(np.float32)
    cos = np.cos(ang.astype(np.float32)).astype(np.float32).T
    sin = np.sin(ang.astype(np.float32)).astype(np.float32).T
    cs = np.stack([np.concatenate([cos, cos], 0), np.concatenate([sin, sin], 0)], axis=1)
    mlamask = np.where(k <= q, 0.0, NEGM).astype(np.float32)
    return dict(biasA=np.ascontiguousarray(biasA.transpose(1, 0, 2, 3)),
                biasB=np.ascontiguousarray(biasB.transpose(1, 0, 2, 3)),
                cbiasU=np.ascontiguousarray(cbiasU), cfar=np.ascontiguousarray(cfar.transpose(1, 0, 2)),
                keepadd=np.ascontiguousarray(keepadd), overlap=overlap, bind=bind,
                cs=np.ascontiguousarray(cs), mlamask=mlamask, ident=np.eye(128, dtype=np.float32))


INPUT_SHAPES = dict(
    x=[S, D], norm_mix_e=[1, D], w_in_e=[D, 3120], sinks=[1, 16], cmp_pos_k=[32, 64], cmp_pos_v=[32, 64],
    cmp_k_w1=[2048, 256], cmp_k_w2=[256, 64], cmp_v_w1=[2048, 256], cmp_v_w2=[256, 64], w_out_e=[D, D],
    norm_mix_o=[1, D], w_in_o=[D, 1344], q_norm=[1, 768], w_q_up=[768, 3072], kv_norm=[1, 512],
    w_kv_up=[512, 4096], w_out_o=[D, D], norm_mlp=[2, D], w_up0=[D, DFF], w_up1=[D, DFF],
    w_down0=[DFF, D], w_down1=[DFF, D], norm_final=[1, D],
    biasA=[128, 2, 16, 128], biasB=[128, 4, 16, 128], cbiasU=[248, 16, 128], cfar=[32, 16, 16],
    keepadd=[128, 16, 2, 32], overlap=[127, 32], bind=[32, S], cs=[64, 2, S], mlamask=[128, 128], ident=[128, 128],
)


class Ctx:
    pass


class LazyInputs:
    def __init__(self, nc):
        self.nc = nc
        self.d = {}

    def __getitem__(self, k):
        if k not in self.d:
            self.d[k] = self.nc.dram_tensor(k, INPUT_SHAPES[k], F32, kind="ExternalInput").ap()
        return self.d[k]


def build_program(stages=("l0mix", "l0mlp", "l1mix", "l1mlp"), dbg=(), cut=None):
    nc = bass.Bass("TRN2", target_bir_lowering=False)
    P = Prog(nc)
    C = Ctx()
    C.nc, C.P = nc, P
    I = LazyInputs(nc)
    C.I = I
    C.out = nc.dram_tensor("out", [S, D], F32, kind="ExternalOutput").ap()
    C.xs = nc.dram_tensor("xs", [S, D], F32, kind="Internal").ap()
    C.xsB = [P.tok(f"xs{t}") for t in range(NT)]
    C.out_ops = []
    C.dbg = {}
    C.dbg_want = dbg
    C.cut = cut
    C.B = [P.ps(f"pb{i}", [128, 512], F32) for i in range(2)]
    C.O = [P.ps(f"po{i}", [128, 8, 128], F32) for i in range(2)]
    C.Of = [o.t[:, :, :].rearrange("p h c -> p (h c)") for o in C.O]
    C.TR = [P.ps(f"ptr{i}", [128, 1024], BF16) for i in range(2)]
    C.TRap = [t.t[:, 0:512] for t in C.TR]
    C.ident = P.sb("ident", [128, 128], BF16)
    P.dma("pool", C.ident[:], I["ident"], [], [C.ident], C.ident)
    C.ones = P.sb("ones", [128, 1], BF16)
    P.memset("dve", [C.ones], C.ones[:], 1.0)
    C.neghalf = P.sb("neghalf", [128, 2], F32)
    P.memset("pool", [C.neghalf], C.neghalf[:], -0.5)
    C.x_src = I["x"]
    C.x_srcB = None

    if "l0mix" in stages:
        layer0_mixer(C)
    if "l0mlp" in stages:
        mlp(C, 0, final=False)
    if "l1mix" in stages:
        layer1_mixer(C)
    if "l1mlp" in stages:
        mlp(C, 1, final=True)
    if "xs" in dbg:
        d = nc.dram_tensor("dbg_xs", [S, D], F32, kind="ExternalOutput").ap()
        db = P.tok("dbgxs")
        for t in range(NT):
            C.out_ops.append(P.dma("sp", d[t * 128:(t + 1) * 128, :], C.xs[t * 128:(t + 1) * 128, :], [C.xsB[t]], [], db))
    P.op("sp", lambda e: e.nop(), extra=C.out_ops)
    P.emit()
    P.used_inputs = list(I.d.keys())
    return nc, P


def bcast_row(ap_row, n=128):
    return ap_row.rearrange("o n -> (o n)").partition_broadcast(n)


def load_x_tile(C, dst, t):
    P = C.P
    reads = [] if C.x_srcB is None else [C.x_srcB[t]]
    P.dma("sp", dst[:], C.x_src[t * 128:(t + 1) * 128, :], reads, [dst], dst)


def norm_rows(C, src_ap, srcB, n, gbc, out_ap, outB, tmp):
    P = C.P
    junk, ssq, sd, rstd = tmp
    P.act([junk, ssq], junk[:, 0:n], [srcB], src_ap, AF.Square, accum_out=ssq[:, 0:1])
    P.ts("dve", [sd], sd[:, 0:1], [ssq], ssq[:, 0:1], 1.0 / n, EPS, ALU.mult, ALU.add)
    P.tt("pool", [rstd], rstd[:, 0:1], [sd, C.neghalf], sd[:, 0:1], C.neghalf[:, 0:1], ALU.pow)
    P.stt([outB], out_ap, [srcB, rstd, gbc], src_ap, rstd[:, 0:1], gbc[:, 0:n], ALU.mult, ALU.mult)


def norm_transpose(C, tiles, gbc, hT, col0, xst, hb, tmp):
    P = C.P
    for j, t in enumerate(tiles):
        xa = xst[j % 2]
        load_x_tile(C, xa, t)
        norm_rows(C, xa[:], xa, D, gbc, hb[:], hb, tmp)
        for q4 in range(4):
            tb = C.TR[q4 % 2]
            tap = C.TRap[q4 % 2]
            for k in range(4):
                kc = q4 * 4 + k
                P.tr([tb], tap[:, k * 128:(k + 1) * 128], [hb, C.ident], hb[:, kc * 128:(kc + 1) * 128], C.ident[:])
            dst = hT[:, q4 * 4:q4 * 4 + 4, col0 + j * 128:col0 + (j + 1) * 128]
            P.copy("act" if q4 % 2 == 0 else "dve", [hT], dst, [tb], tap.rearrange("p (a b) -> p a b", a=4))


def load_w_cast(C, dst, dst_ap, src_ap):
    C.P.dma("pool", dst_ap, src_ap, [], [dst], dst)


def layer0_mixer(C):
    P, I = C.P, C.I
    B = C.B
    m_persist = P.mark()
    kaT = P.sb("kaT", [128, 2, S], BF16)
    kwT = P.sb("kwT", [128, 2, S], BF16)
    ksA = P.sb("ksA", [128, 2, S], BF16)
    for kt_ in (kaT, kwT, ksA):
        P.memset("pool", [kt_], kt_[:], 0.0)
    va = P.sb("va", [128, NT, 2, 65], BF16)
    vs = P.sb("vs", [128, NT, 2, 65], BF16)
    vw = P.sb("vw", [128, NT, 2, 65], BF16)
    kcT = P.sb("kcT", [128, 2, 128], BF16)
    vcA = P.sb("vcA", [128, 2, 97], BF16)
    gates = P.sb("gates", [128, NT, 48], F32)
    sinkexp = P.sb("sinkexp", [128, 16], F32)
    qsa = C.nc.dram_tensor("qsa", [1024, S], BF16, kind="Internal").ap()
    qsb = C.nc.dram_tensor("qsb", [1024, S], BF16, kind="Internal").ap()
    qsB = {(w, c, tb): P.tok(f"qs{w}{c}_{tb}") for w in "ab" for c in range(8) for tb in range(4)}
    for vt in (va, vs, vw):
        P.memset("pool", [vt], vt[:, :, :, 64:65], 1.0)
    P.memset("pool", [vcA], vcA[:], 0.0)
    P.memset("pool", [kcT], kcT[:], 0.0)
    P.memset("pool", [vcA], vcA[:, :, 64:65], 1.0)
    for g in range(2):
        P.dma("pool", ksA[64:96, g, :], I["bind"], [], [ksA], ksA)
        P.dma("pool", vcA[0:127, g, 65:97], I["overlap"], [], [vcA], vcA)
    P.dma("sp", sinkexp[:], bcast_row(I["sinks"]), [], [sinkexp], sinkexp)
    P.act([sinkexp], sinkexp[:], [sinkexp], sinkexp[:], AF.Exp)

    m_raw = P.mark()
    rawk = P.sb("rawk", [64, 2, S], BF16)
    rawv = P.sb("rawv", [64, 2, S], BF16)
    m0 = P.mark()
    hTs = [P.sb(f"hT{tb}", [128, 16, 512], BF16) for tb in range(4)]
    gbc = P.sb("gbc", [128, D], F32)
    P.dma("sp", gbc[:], bcast_row(I["norm_mix_e"]), [], [gbc], gbc)
    xst = [P.sb(f"xst{i}", [128, D], F32) for i in range(2)]
    hb = P.sb("hb", [128, D], BF16)
    tmp = (P.sb("junk", [128, D], BF16), P.sb("ssq", [128, 1], F32), P.sb("sd", [128, 2], F32), P.sb("rstd", [128, 1], F32))
    WKV = P.sb("WKV", [128, 16, 1072], BF16)
    wv = I["w_in_e"].rearrange("(kc p) c -> p kc c", p=128)
    load_w_cast(C, WKV, WKV[:, :, 0:256], wv[:, :, 1024:1280])
    load_w_cast(C, WKV, WKV[:, :, 256:1072], wv[:, :, 2304:3120])
    WQ = [P.sb(f"WQ{i}", [128, 16, 256], BF16) for i in range(2)]
    qblocks = [(which, col0, blk) for (which, col0) in (("a", 0), ("b", 1280)) for blk in range(4)]

    def load_q(n):
        which, col0, blk = qblocks[n]
        load_w_cast(C, WQ[n % 2], WQ[n % 2][:], wv[:, :, col0 + blk * 256:col0 + (blk + 1) * 256])
    load_q(0)
    load_q(1)
    qstage = [P.sb(f"qst{i}", [128, 512], BF16) for i in range(3)]
    kdst = [(kaT, 0), (rawk, 256), (rawv, 384), (ksA, 512), (kwT, 768)]
    nb = 0
    for tb in range(4):
        hT = hTs[tb]
        norm_transpose(C, range(4 * tb, 4 * tb + 4), gbc, hT, 0, xst, hb, tmp)
        for (dst, c0) in kdst:
            pb = B[nb % 2]
            for kc in range(16):
                P.mm([pb], pb[:, :], [WKV, hT], WKV[:, kc, c0:c0 + 128], hT[:, kc, :], kc == 0, kc == 15)
            P.copy("dve", [dst], dst[0:64, 0, tb * 512:(tb + 1) * 512], [pb], pb[0:64, :])
            P.copy("act", [dst], dst[0:64, 1, tb * 512:(tb + 1) * 512], [pb], pb[64:128, :])
            nb += 1
        for t in range(4 * tb, 4 * tb + 4):
            tl = t % 4
            pbB, pb = C.O[t % 2], C.Of[t % 2]
            for (o0, c0, w) in ((0, 128, 128), (128, 640, 128), (256, 896, 176)):
                for kc in range(16):
                    P.mm([pbB], pb[:, o0:o0 + w], [WKV, hT], hT[:, kc, tl * 128:(tl + 1) * 128], WKV[:, kc, c0:c0 + w], kc == 0, kc == 15)
            for vi, vt in enumerate((va, vs, vw)):
                P.copy("dve" if vi != 1 else "act", [vt], vt[:, t, :, 0:64], [pbB],
                       pb[:, vi * 128:(vi + 1) * 128].rearrange("p (g d) -> p g d", g=2))
            P.act([gates], gates[:, t, :], [pbB], pb[:, 384:432], AF.Tanh, scale=0.5)
            P.ts("pool", [gates], gates[:, t, :], [gates], gates[:, t, :], 0.5, 0.5, ALU.mult, ALU.add)
    nb = 0
    for n, (which, col0, blk) in enumerate(qblocks):
        qs = qsa if which == "a" else qsb
        W = WQ[n % 2]
        for c4 in range(2):
            c = blk * 2 + c4
            for tb in range(4):
                pb = B[nb % 2]
                for kc in range(16):
                    P.mm([pb], pb[:], [W, hTs[tb]], W[:, kc, c4 * 128:(c4 + 1) * 128], hTs[tb][:, kc, :], kc == 0, kc == 15)
                st = qstage[nb % 3]
                if nb % 2 == 0:
                    P.op("act", lambda e, o=st[:], a=pb[:]: e.mul(o, a, 0.125), [pb], [st])
                else:
                    P.ts("dve", [st], st[:], [pb], pb[:], 0.125, None, ALU.mult)
                P.dma("sp", qs[c * 128:(c + 1) * 128, tb * 512:(tb + 1) * 512], st[:], [st], [qsB[(which, c, tb)]], st)
                nb += 1
        if n + 2 < len(qblocks):
            load_q(n + 2)
    if C.cut == "d":
        return
    P.release(m0)
    hid = P.sb("hid", [128, 2, 128], BF16)
    zt = [P.sb(f"cz{i}", [128, 128], F32) for i in range(4)]
    pbias = P.sb("pbias", [128, 1], F32)
    for kv, (w1n, w2n, posn, raw) in enumerate((("cmp_k_w1", "cmp_k_w2", "cmp_pos_k", rawk), ("cmp_v_w1", "cmp_v_w2", "cmp_pos_v", rawv))):
        w1 = P.sb(f"cw1{kv}", [64, 32, 256], BF16)
        w2 = P.sb(f"cw2{kv}", [128, 2, 64], BF16)
        posT = P.sb(f"cpos{kv}", [64, 32], BF16)
        P.dma("pool", w1[:], I[w1n].rearrange("(l d) h -> d l h", d=64), [], [w1], w1)
        P.dma("pool", w2[:], I[w2n].rearrange("(c p) d -> p c d", p=128), [], [w2], w2)
        P.dma("pool", posT[:], I[posn].rearrange("l d -> d l"), [], [posT], posT, allow_slow_non_contiguous=True)
        for g in range(2):
            for hc in range(2):
                pm, pp = B[0], B[1]
                for l in range(32):
                    P.mm([pm], pm[:, 0:127], [w1, raw], w1[:, l, hc * 128:(hc + 1) * 128], raw[:, g, l:l + 16 * 126 + 1:16], l == 0, l == 31)
                for l in range(32):
                    P.mm([pp], pp[:, 0:1], [w1, posT], w1[:, l, hc * 128:(hc + 1) * 128], posT[:, l:l + 1], l == 0, l == 31)
                z, z2, u, sg = zt
                P.copy("dve", [pbias], pbias[:], [pp], pp[:, 0:1])
                P.ts("dve", [z], z[:, 0:127], [pm, pbias], pm[:, 0:127], pbias[:, 0:1], None, ALU.add)
                P.tt("dve", [z2], z2[:, 0:127], [z], z[:, 0:127], z[:, 0:127], ALU.mult)
                P.ts("dve", [z2], z2[:, 0:127], [z2], z2[:, 0:127], 0.044715, 1.0, ALU.mult, ALU.add)
                P.tt("dve", [u], u[:, 0:127], [z2, z], z2[:, 0:127], z[:, 0:127], ALU.mult)
                P.act([sg], sg[:, 0:127], [u], u[:, 0:127], AF.Tanh, scale=0.7978845608028654)
                P.stt([sg], sg[:, 0:127], [sg, z], sg[:, 0:127], 1.0, z[:, 0:127], ALU.add, ALU.mult)
                P.ts("dve", [hid], hid[:, hc, 0:127], [sg], sg[:, 0:127], 0.5, None, ALU.mult)
            if kv == 0:
                poB, po = C.O[0], C.Of[0]
                for hc in range(2):
                    P.mm([poB], po[0:64, 0:127], [w2, hid], w2[:, hc, :], hid[:, hc, 0:127], hc == 0, hc == 1)
                P.copy("dve", [kcT], kcT[0:64, g, 0:127], [poB], po[0:64, 0:127])
            else:
                poB, po = C.O[1], C.Of[1]
                for hc in range(2):
                    P.mm([poB], po[0:127, 0:64], [hid, w2], hid[:, hc, 0:127], w2[:, hc, :], hc == 0, hc == 1)
                P.copy("dve", [vcA], vcA[0:127, g, 0:64], [poB], po[0:127, 0:64])
    if "l0proj" in C.dbg_want:
        dbg_dump(C, "kaT", kaT, kaT[0:64], [64, 2, S], BF16)
        dbg_dump(C, "ksA", ksA, ksA[0:96], [96, 2, S], BF16)
        dbg_dump(C, "va", va, va[:], [128, NT, 2, 65], BF16)
        dbg_dump(C, "kcT", kcT, kcT[0:64], [64, 2, 128], BF16)
        dbg_dump(C, "vcA", vcA, vcA[:], [128, 2, 97], BF16)
        dbg_dump(C, "gates", gates, gates[:], [128, NT, 48], F32)
    P.release(m_raw)

    if C.cut == "e":
        return
    WO = P.sb("WO", [128, 16, D], BF16)
    wov = I["w_out_e"].rearrange("(kc p) c -> p kc c", p=128)
    biasA = P.sb("biasA", [128, 2, 16, 128], BF16)
    biasB = P.sb("biasB", [128, 4, 16, 128], BF16)
    keepadd = P.sb("keepadd", [128, NT, 2, 32], F32)
    cfar = P.sb("cfar", [128, NT, 16], F32)
    cb = [P.sb(f"cb{i}", [128, 16, 128], BF16) for i in range(2)]
    qa = [P.sb(f"qa{i}", [128, 16, 128], BF16) for i in range(2)]
    qb = [P.sb(f"qb{i}", [128, 16, 128], BF16) for i in range(2)]
    for q_ in qa + qb:
        P.memset("pool", [q_], q_[:], 0.0)
    PT = [P.sb(f"PT{i}", [128, 512], BF16) for i in range(4)]
    TM = P.sb("TM", [128, 128], BF16)
    P.memset("dve", [TM], TM[:], 0.0)
    ocat = P.sb("ocat", [128, D], BF16)
    oT = P.sb("oT", [128, 16, 128], BF16)
    xst = [P.sb(f"xat{i}", [128, D], F32) for i in range(2)]
    acc = P.sb("oacc", [128, 16, 64], F32)
    tmpo = P.sb("otmp", [128, 8, 64], F32)
    sm = {k: P.sb(f"sm_{k}", shp, F32) for k, shp in dict(z=[128, 8], rz=[128, 8], w=[128, 8], ps3=[128, 8, 32],
                                                          pslc=[128, 32], sc=[128, 32], m8=[128, 8], sel=[128, 32]).items()}
    ST = [B[0], B[1]]
    Ot = C.O
    osl = [0]

    def attn_group(items, PVrhs, nheads_cols, g, Ob):
        pend = []
        for idx, it in enumerate(items):
            stb, sta = STB[st3[0] % 3]
            st3[0] += 1
            mml, (half, kt, nrows, first, last) = it
            for mi, (l, r, rb) in enumerate(mml):
                P.mm([stb], sta[0:nrows, :], rb, l, r, mi == 0, mi == len(mml) - 1)
            pend.append((stb, sta, half, kt, nrows, first, last, PVrhs, nheads_cols, g, Ob))
            if len(pend) > 2:
                finish(*pend.pop(0))
        while pend:
            finish(*pend.pop(0))

    def finish(stb, sta, half, kt, nrows, first, last, PVrhs, ncols, g, Ob):
        pt = PT[pt_ctr[0] % len(PT)]
        pt_ctr[0] += 1
        P.act([pt], pt[0:nrows, :], [stb], sta[0:nrows, :], AF.Exp)
        vt = PVrhs
        for hl in range(4):
            h8 = half * 4 + hl
            if vt is vcA:
                rhs = vcA[0:nrows, g, 0:ncols]
            else:
                rhs = vt[:, kt, g, 0:ncols]
            P.mm([Ob], Ob[:, h8, 0:ncols], [pt, vt], pt[0:nrows, hl * 128:(hl + 1) * 128], rhs, first and hl == 0, last and hl == 3, skip=True)

    st_ctr = [0]
    st3 = [0]
    pt_ctr = [0]
    STB = [(B[0], B[0].t), (B[1], B[1].t), (C.TR[0], C.TR[0].t[:, :].bitcast(F32))]
    def load_tile_inputs(i):
        qat, qbt, cbt = qa[i % 2], qb[i % 2], cb[i % 2]
        tb = i // 4
        P.dma("sp", qat[0:64, :, :], qsa.rearrange("(h d) t -> d h t", d=64)[:, :, i * 128:(i + 1) * 128],
              [qsB[("a", c, tb)] for c in range(8)], [qat], qat)
        P.dma("sp", qbt[0:64, :, :], qsb.rearrange("(h d) t -> d h t", d=64)[:, :, i * 128:(i + 1) * 128],
              [qsB[("b", c, tb)] for c in range(8)], [qbt], qbt)
        P.dma("pool", cbt[:, :, :], I["cbiasU"][120 - 8 * i:248 - 8 * i, :, :], [], [cbt], cbt)

    def out_proj(i):
        xa = xst[i % 2]
        oc = ocats[i % 2]
        for q4 in range(4):
            trb, trap = C.TR[1], C.TRap[1]
            for k in range(4):
                kc = q4 * 4 + k
                P.tr([trb], trap[:, k * 128:(k + 1) * 128], [oc, C.ident], oc[:, kc * 128:(kc + 1) * 128], C.ident[:])
            P.copy("act" if q4 % 2 == 0 else "dve", [oT], oT[:, q4 * 4:q4 * 4 + 4, :], [trb], trap.rearrange("p (a b) -> p a b", a=4))
        for db in range(4):
            pb = ST[st_ctr[0] % 2]
            st_ctr[0] += 1
            for kc in range(16):
                P.mm([pb], pb[:], [oT, WO], oT[:, kc, :], WO[:, kc, db * 512:(db + 1) * 512], kc == 0, kc == 15)
            P.tt("dve", [xa], xa[:, db * 512:(db + 1) * 512], [pb, xa], pb[:], xa[:, db * 512:(db + 1) * 512], ALU.add)
        P.dma("sp", C.xs[i * 128:(i + 1) * 128, :], xa[:], [xa], [C.xsB[i]], xa)

    ocats = [ocat, P.sb("ocat2", [128, D], BF16)]
    load_tile_inputs(0)
    P.dma("pool", biasB[:], I["biasB"], [], [biasB], biasB)
    P.dma("pool", biasA[:], I["biasA"], [], [biasA], biasA)
    P.dma("sp", keepadd[:], I["keepadd"], [], [keepadd], keepadd)
    P.dma("sp", cfar[64:96, :, :], I["cfar"], [], [cfar], cfar)
    for q4 in range(4):
        load_w_cast(C, WO, WO[:, q4 * 4:(q4 + 1) * 4, :], wov[:, q4 * 4:(q4 + 1) * 4, :])
    for i in range(NT):
        if C.cut is not None and C.cut.startswith("f") and i >= int(C.cut[1:]):
            return
        qat, qbt, cbt = qa[i % 2], qb[i % 2], cb[i % 2]
        ocat = ocats[i % 2]
        if i + 1 < NT:
            load_tile_inputs(i + 1)
        load_x_tile(C, xst[i % 2], i)
        xa = xst[i % 2]
        bo = 0
        for g in range(2):
            Ob = Ot[bo % 2]
            bo += 1
            items = []
            for half in range(2):
                hs = slice(g * 8 + half * 4, g * 8 + half * 4 + 4)
                mml = [(kcT[:, g, 0:128], qbt[:, hs, :], [kcT, qbt]),
                       (C.ident[:], cbt[:, hs, :], [C.ident, cbt])]
                items.append((mml, (half, 0, 128, True, True)))
            attn_group(items, vcA, 97, g, Ob)
            z, rz, w = sm["z"], sm["rz"], sm["w"]
            P.ts("dve", [z], z[:], [Ob], Ob[:, :, 64], 1e-30, None, ALU.max)
            P.recip([rz], rz[:], [z], z[:])
            P.tt("dve", [w], w[:], [rz, gates], rz[:], gates[:, i, g * 24:(g + 1) * 24].rearrange("p (h k) -> p h k", k=3)[:, :, 0], ALU.mult)
            P.tt("dve", [acc], acc[:, g * 8:(g + 1) * 8, :], [Ob, w], Ob[:, :, 0:64], w[:].unsqueeze(2).to_broadcast([128, 8, 64]), ALU.mult)
            ps3 = sm["ps3"]
            P.tt("dve", [ps3], ps3[:], [Ob, rz], Ob[:, :, 65:97], rz[:].unsqueeze(2).to_broadcast([128, 8, 32]), ALU.mult)
            pslc, sc, m8, sel = sm["pslc"], sm["sc"], sm["m8"], sm["sel"]
            P.op("dve", lambda e, o=pslc[:], a=ps3[:].rearrange("p h n -> p n h"): e.tensor_reduce(out=o, in_=a, axis=AX.X, op=ALU.add), [ps3], [pslc])
            P.tt("dve", [sc], sc[:], [pslc, keepadd], pslc[:], keepadd[:, i, 0, :], ALU.mult)
            P.tt("dve", [sc], sc[:], [sc, keepadd], sc[:], keepadd[:, i, 1, :], ALU.add)
            P.op("dve", lambda e, o=m8[:], a=sc[:]: e.max(out=o, in_=a), [sc], [m8])
            P.ts("dve", [sel], sel[:], [sc, m8], sc[:], m8[:, 7:8], None, ALU.is_ge)
            P.ts("dve", [TM], TM[:, 64:96], [sel], sel[:], 1.0, -NEGM, ALU.subtract, ALU.mult)
            trb, trap = C.TR[1], C.TRap[1]
            P.tr([trb], trap[:, 0:128], [TM, C.ident], TM[:], C.ident[:])
            P.tt("dve", [qbt], qbt[64:96, g * 8:(g + 1) * 8, :], [trb, cfar],
                 trap[64:96, 0:128].unsqueeze(1).to_broadcast([32, 8, 128]),
                 cfar[64:96, i, g * 8:(g + 1) * 8].unsqueeze(2).to_broadcast([32, 8, 128]), ALU.add)
        for g in range(2):
            Ob = Ot[bo % 2]
            bo += 1
            kts = [kt for kt in (i - 1, i) if kt >= 0]
            items = []
            for half in range(2):
                hs = slice(g * 8 + half * 4, g * 8 + half * 4 + 4)
                for kt in kts:
                    kind = 0 if kt == i else 1
                    mml = [(kaT[:, g, kt * 128:(kt + 1) * 128], qat[:, hs, :], [kaT, qat]),
                           (C.ident[:], biasA[:, kind, hs, :], [C.ident, biasA])]
                    items.append((mml, (half, kt, 128, kt == kts[0], kt == kts[-1])))
            attn_group(items, va, 65, g, Ob)
            z, rz = sm["z"], sm["rz"]
            P.tt("dve", [z], z[:], [Ob, sinkexp], Ob[:, :, 64], sinkexp[:, g * 8:(g + 1) * 8], ALU.add)
            P.recip([rz], rz[:], [z], z[:])
            P.tt("dve", [ocat], ocat[:, g * 512:(g + 1) * 512].rearrange("p (h d) -> p h d", d=64), [Ob, rz], Ob[:, :, 0:64],
                 rz[:].unsqueeze(2).to_broadcast([128, 8, 64]), ALU.mult)
        if i > 0:
            out_proj(i - 1)
        for br in ("win", "slc"):
            for g in range(2):
                Ob = Ot[bo % 2]
                bo += 1
                items = []
                kts = list(range(max(0, i - 4), i + 1)) if br == "win" else list(range(0, i + 1))
                for half in range(2):
                    hs = slice(g * 8 + half * 4, g * 8 + half * 4 + 4)
                    for kt in kts:
                        dk = i - kt
                        if br == "win":
                            kind = {0: 0, 1: 1, 2: 2, 3: 2, 4: 3}[dk]
                            mml = [(kwT[:, g, kt * 128:(kt + 1) * 128], qbt[:, hs, :], [kwT, qbt]),
                                   (C.ident[:], biasB[:, kind, hs, :], [C.ident, biasB])]
                        else:
                            mml = [(ksA[:, g, kt * 128:(kt + 1) * 128], qbt[:, hs, :], [ksA, qbt])]
                            if dk <= 1:
                                mml.append((C.ident[:], biasB[:, dk, hs, :], [C.ident, biasB]))
                        items.append((mml, (half, kt, 128, kt == kts[0], kt == kts[-1])))
                attn_group(items, vw if br == "win" else vs, 65, g, Ob)
                rz, w = sm["rz"], sm["w"]
                P.recip([rz], rz[:], [Ob], Ob[:, :, 64])
                gi = 2 if br == "win" else 1
                P.tt("dve", [w], w[:], [rz, gates], rz[:], gates[:, i, g * 24:(g + 1) * 24].rearrange("p (h k) -> p h k", k=3)[:, :, gi], ALU.mult)
                P.tt("dve", [tmpo], tmpo[:], [Ob, w], Ob[:, :, 0:64], w[:].unsqueeze(2).to_broadcast([128, 8, 64]), ALU.mult)
                if br == "win":
                    P.tt("pool", [acc], acc[:, g * 8:(g + 1) * 8, :], [acc, tmpo], acc[:, g * 8:(g + 1) * 8, :], tmpo[:], ALU.add)
                else:
                    P.tt("pool", [ocat], ocat[:, 1024 + g * 512:1024 + (g + 1) * 512].rearrange("p (h d) -> p h d", d=64), [acc, tmpo],
                         acc[:, g * 8:(g + 1) * 8, :], tmpo[:], ALU.add)
        if "ocat" in C.dbg_want:
            dbg_dump(C, f"ocat{i}", ocat, ocat[:], [128, D], BF16)
    out_proj(NT - 1)
    C.x_src = C.xs
    C.x_srcB = C.xsB
    P.release(m_persist)


def dbg_dump(C, name, buf, ap, shape, dtype):
    P = C.P
    d = C.nc.dram_tensor("dbg_" + name, list(shape), dtype, kind="ExternalOutput").ap()
    C.dbg[name] = (shape, dtype)
    oid = P.dma("sp", d, ap, [buf], [], buf)
    C.out_ops.append(oid)


def mlp(C, layer, final):
    P, I, B = C.P, C.I, C.B
    m = P.mark()
    gbc = P.sb("gbcm", [128, D], F32)
    gfin = gbc
    TB = 8
    hTs = [P.sb(f"hTm{i}", [128, 16, 512], BF16) for i in range(TB // 4)]
    yacc = P.sb("yacc", [128, TB, D], F32)
    xst = [P.sb(f"xsm{i}", [128, D], F32) for i in range(2)]
    hb = P.sb("hbm", [128, D], BF16)
    tmp = (hb, P.sb("ssqm", [128, 1], F32), P.sb("sdm", [128, 2], F32), P.sb("rstdm", [128, 1], F32))
    wu = [P.sb(f"wu{i}", [128, 16, 512], BF16) for i in range(2)]
    wd = [P.sb(f"wd{i}", [128, 4, D], BF16) for i in range(2)]
    uT = [P.sb(f"uT{i}", [128, 4, 512], BF16) for i in range(2)]
    rl = [P.sb(f"rl{i}", [128, 512], F32) for i in range(2)]
    wuv = I[f"w_up{layer}"].rearrange("(kc p) f -> p kc f", p=128)
    wdv = I[f"w_down{layer}"].rearrange("(fc p) d -> p fc d", p=128)
    nu = 0
    no = 0
    NG = 16

    def load_group(gi):
        wub, wdb = wu[gi % 2], wd[gi % 2]
        load_w_cast(C, wub, wub[:], wuv[:, :, gi * 512:(gi + 1) * 512])
        load_w_cast(C, wdb, wdb[:], wdv[:, gi * 4:(gi + 1) * 4, :])

    def up(gi, tb):
        nonlocal nu
        wub = wu[gi % 2]
        u = uT[(gi * 2 + tb) % 2]
        for fc in range(4):
            pb = B[nu % 2]
            r = rl[nu % 2]
            nu += 1
            for kc in range(16):
                P.mm([pb], pb[:], [wub, hTs[tb]], wub[:, kc, fc * 128:(fc + 1) * 128], hTs[tb][:, kc, :], kc == 0, kc == 15)
            P.act([r], r[:], [pb], pb[:], AF.Relu)
            P.act([u], u[:, fc, :], [r], r[:], AF.Square)

    def down(gi, tb):
        nonlocal no
        wdb = wd[gi % 2]
        u = uT[(gi * 2 + tb) % 2]
        for tt in range(4):
            j = tb * 4 + tt
            for dbp in range(2):
                ob, of = C.O[no % 2], C.Of[no % 2]
                no += 1
                for dbi in range(2):
                    db = dbp * 2 + dbi
                    for fc in range(4):
                        P.mm([ob], of[:, dbi * 512:(dbi + 1) * 512], [u, wdb], u[:, fc, tt * 128:(tt + 1) * 128],
                             wdb[:, fc, db * 512:(db + 1) * 512], fc == 0, fc == 3)
                ys = yacc[:, j, dbp * 1024:(dbp + 1) * 1024]
                if gi == 0:
                    P.copy("dve", [yacc], ys, [ob], of[:, :])
                else:
                    P.tt("dve", [yacc], ys, [ob, yacc], of[:, :], ys, ALU.add)

    for blk in range(NT // TB):
        P.dma("sp", gbc[:], bcast_row(I["norm_mlp"][layer:layer + 1, :]), [], [gbc], gbc)
        load_group(0)
        load_group(1)
        units = [(gi, tb) for gi in range(NG) for tb in range(TB // 4)]
        norm_transpose(C, range(blk * TB, blk * TB + 4), gbc, hTs[0], 0, xst, hb, tmp)
        up(*units[0])
        for q_ in range(1, TB // 4):
            norm_transpose(C, range(blk * TB + 4 * q_, blk * TB + 4 * q_ + 4), gbc, hTs[q_], 0, xst, hb, tmp)
        for k, (gi, tb) in enumerate(units):
            if k + 1 < len(units):
                up(*units[k + 1])
            down(gi, tb)
            if tb == TB // 4 - 1 and 1 <= gi + 1 and gi + 2 < NG:
                load_group(gi + 2)
        if final:
            P.dma("sp", gfin[:], bcast_row(I["norm_final"]), [], [gfin], gfin)
        for j in range(TB):
            t = blk * TB + j
            if not final:
                P.dma("pool", C.xs[t * 128:(t + 1) * 128, :], yacc[:, j, :], [yacc, C.xsB[t]], [C.xsB[t]], yacc, accum_op=ALU.add)
                continue
            xa = xst[j % 2]
            load_x_tile(C, xa, t)
            P.tt("dve", [xa], xa[:], [xa, yacc], xa[:], yacc[:, j, :], ALU.add)
            if True:
                ot = hb
                yo = yacc[:, j, :]
                norm_rows(C, xa[:], xa, D, gfin, yo, yacc, tmp)
                oid = P.dma("sp", C.out[t * 128:(t + 1) * 128, :], yo, [yacc], [], yacc)
                C.out_ops.append(oid)
    C.x_src = C.xs
    C.x_srcB = C.xsB
    P.release(m)


def layer1_mixer(C):
    P, I, B = C.P, C.I, C.B
    m_all = P.mark()
    SCALE = 192 ** -0.5
    cqnT = P.sb("cqnT", [128, 6, S], BF16)
    ckvT = P.sb("ckvT", [128, 4, S], BF16)
    krT = P.sb("krT", [128, S], BF16)
    P.memset("pool", [krT], krT[:], 0.0)
    cs = P.sb("cs", [128, S], F32)
    P.dma("sp", cs[0:64, :], I["cs"][:, 0, :], [], [cs], cs)
    P.dma("sp", cs[64:128, :], I["cs"][:, 1, :], [], [cs], cs)
    mmask = P.sb("mmask", [128, 128], BF16)
    P.dma("pool", mmask[:], I["mlamask"], [], [mmask], mmask)
    m0 = P.mark()
    hTs = [P.sb(f"hT1_{tb}", [128, 16, 512], BF16) for tb in range(4)]
    gbc = P.sb("gbc1", [128, D], F32)
    P.dma("sp", gbc[:], bcast_row(I["norm_mix_o"]), [], [gbc], gbc)
    qg = P.sb("qg", [128, 768], F32)
    kg = P.sb("kg", [128, 512], F32)
    P.dma("sp", qg[:], bcast_row(I["q_norm"]), [], [qg], qg)
    P.dma("sp", kg[:], bcast_row(I["kv_norm"]), [], [kg], kg)
    hb = P.sb("hb1", [128, D], BF16)
    tmp = (P.sb("junk1", [128, D], BF16), P.sb("ssq1", [128, 1], F32), P.sb("sd1", [128, 2], F32), P.sb("rstd1", [128, 1], F32))
    WI = P.sb("WI", [128, 16, 1344], BF16)
    wiv = I["w_in_o"].rearrange("(kc p) c -> p kc c", p=128)
    load_w_cast(C, WI, WI[:, 0:8, :], wiv[:, 0:8, :])
    load_w_cast(C, WI, WI[:, 8:16, :], wiv[:, 8:16, :])
    WIr = P.sb("WIr", [128, 16, 128], BF16)
    P.copy("pool", [WIr], WIr[:, :, 0:64], [WI], WI[:, :, 1280:1344])
    P.ts("pool", [WIr], WIr[:, :, 64:96], [WI], WI[:, :, 1312:1344], -1.0, None, ALU.mult)
    P.copy("pool", [WIr], WIr[:, :, 96:128], [WI], WI[:, :, 1280:1312])
    m_x = P.mark()
    xst = [P.sb(f"xs1{i}", [128, D], F32) for i in range(2)]
    cn = P.sb("cn", [128, 1280], BF16)
    ssb = P.sb("ssb", [128, 4], F32)
    for t in range(NT):
        if t % 4 == 0:
            norm_transpose(C, range(t, t + 4), gbc, hTs[t // 4], 0, xst, hb, tmp)
        hT = hTs[t // 4]
        tl = t % 4
        pa, pbk, pc = B[0], B[1], C.O[t % 2]
        paa, pba, pca = B[0].t, B[1].t, C.Of[t % 2]
        for (pb, pap, c0, w) in ((pa, paa, 0, 384), (pbk, pba, 384, 384), (pc, pca, 768, 512)):
            for kc in range(16):
                P.mm([pb], pap[:, 0:w], [hT, WI], hT[:, kc, tl * 128:(tl + 1) * 128], WI[:, kc, c0:c0 + w], kc == 0, kc == 15)
        junk = tmp[0]
        P.act([junk, ssb], junk[:, 0:384], [pa], pa[:, 0:384], AF.Square, accum_out=ssb[:, 0:1])
        P.act([junk, ssb], junk[:, 384:768], [pbk], pbk[:, 0:384], AF.Square, accum_out=ssb[:, 1:2])
        P.act([junk, ssb], junk[:, 768:1280], [pc], pca[:, 0:512], AF.Square, accum_out=ssb[:, 2:3])
        sd = tmp[2]
        rq = tmp[3]
        rk = tmp[1]
        P.tt("dve", [ssb], ssb[:, 3:4], [ssb], ssb[:, 0:1], ssb[:, 1:2], ALU.add)
        P.ts("dve", [sd], sd[:, 0:1], [ssb], ssb[:, 3:4], 1.0 / 768, EPS, ALU.mult, ALU.add)
        P.ts("dve", [sd], sd[:, 1:2], [ssb], ssb[:, 2:3], 1.0 / 512, EPS, ALU.mult, ALU.add)
        P.tt("pool", [rq], rq[:, 0:1], [sd, C.neghalf], sd[:, 0:1], C.neghalf[:, 0:1], ALU.pow)
        P.tt("pool", [rk], rk[:, 0:1], [sd, C.neghalf], sd[:, 1:2], C.neghalf[:, 0:1], ALU.pow)
        P.stt([cn], cn[:, 0:384], [pa, rq, qg], pa[:, 0:384], rq[:, 0:1], qg[:, 0:384], ALU.mult, ALU.mult)
        P.stt([cn], cn[:, 384:768], [pbk, rq, qg], pbk[:, 0:384], rq[:, 0:1], qg[:, 384:768], ALU.mult, ALU.mult)
        P.stt([cn], cn[:, 768:1280], [pc, rk, kg], pca[:, 0:512], rk[:, 0:1], kg[:, 0:512], ALU.mult, ALU.mult)
        for grp, (dstT, nck, cbase) in enumerate(((cqnT, 4, 0), (cqnT, 2, 512), (ckvT, 4, 768))):
            trb, trap = C.TR[grp % 2], C.TRap[grp % 2]
            for k in range(nck):
                P.tr([trb], trap[:, k * 128:(k + 1) * 128], [cn, C.ident], cn[:, cbase + k * 128:cbase + (k + 1) * 128], C.ident[:])
            k0 = 0 if grp != 1 else 4
            P.copy("act" if grp % 2 == 0 else "dve", [dstT], dstT[:, k0:k0 + nck, t * 128:(t + 1) * 128], [trb],
                   trap[:, 0:nck * 128].rearrange("p (a b) -> p a b", a=nck))
    t1 = P.sb("rt1", [128, 512], F32)
    t2 = P.sb("rt2", [64, 512], F32)
    for tb in range(4):
        p1 = B[tb % 2]
        for kc in range(16):
            P.mm([p1], p1[:, :], [WIr, hTs[tb]], WIr[:, kc, :], hTs[tb][:, kc, :], kc == 0, kc == 15)
        P.tt("dve", [t1], t1[:], [p1, cs], p1[:, :], cs[:, tb * 512:(tb + 1) * 512], ALU.mult)
        P.copy("act", [t2], t2[:], [t1], t1[64:128, :])
        P.tt("pool", [krT], krT[0:64, tb * 512:(tb + 1) * 512], [t1, t2], t1[0:64, :], t2[:], ALU.add)
    if "l1lat" in C.dbg_want:
        dbg_dump(C, "cqnT", cqnT, cqnT[:], [128, 6, S], BF16)
        dbg_dump(C, "ckvT", ckvT, ckvT[:], [128, 4, S], BF16)
        dbg_dump(C, "krT", krT, krT[0:64], [64, S], BF16)
    P.release(m0)
    oT = P.sb("oT1", [128, 16, S], BF16)
    m_after_oT = P.mark()
    wq = [P.sb(f"wq{i}", [128, 6, 192], BF16) for i in range(2)]
    wqr = [P.sb(f"wqr{i}", [128, 6, 128], BF16) for i in range(2)]
    wkv = [P.sb(f"wkv{i}", [128, 4, 256], BF16) for i in range(2)]
    kT = [P.sb(f"kTh{i}", [128, S], BF16) for i in range(2)]
    vh = [P.sb(f"vh{i}", [128, NT, 129], BF16) for i in range(2)]
    for v_ in vh:
        P.memset("pool", [v_], v_[:, :, 128:129], 1.0)
    qn = [P.sb(f"qnh{i}", [128, S], BF16) for i in range(2)]
    qr = [P.sb(f"qrh{i}", [128, S], BF16) for i in range(2)]
    for q_ in qr:
        P.memset("pool", [q_], q_[:], 0.0)
    PT = [P.sb(f"PT1{i}", [128, 512], BF16) for i in range(4)]
    STB = [(B[0], B[0].t), (B[1], B[1].t), (C.TR[0], C.TR[0].t[:, :].bitcast(F32))]
    t1s = [t1, P.sb("rt1b", [128, 512], F32)]
    t2s = [t2, P.sb("rt2b", [64, 512], F32)]
    ob = [P.sb(f"ob{i}", [128, 4, 128], BF16) for i in range(2)]
    rz = [P.sb(f"rz1{i}", [128, 4], F32) for i in range(2)]
    wqv = I["w_q_up"].rearrange("(kc p) c -> p kc c", p=128)
    wkvv = I["w_kv_up"].rearrange("(kc p) c -> p kc c", p=128)
    stc = 0
    ptc = 0
    def load_head(h):
        s2 = h % 2
        load_w_cast(C, wq[s2], wq[s2][:], wqv[:, :, h * 192:(h + 1) * 192])
        load_w_cast(C, wkv[s2], wkv[s2][:], wkvv[:, :, h * 256:(h + 1) * 256])
    load_head(0)
    pend_qb = []
    for h in range(16):
        s2 = h % 2
        if h + 1 < 16:
            load_head(h + 1)
        P.copy("pool", [wqr[s2]], wqr[s2][:, :, 0:64], [wq[s2]], wq[s2][:, :, 128:192])
        P.ts("pool", [wqr[s2]], wqr[s2][:, :, 64:96], [wq[s2]], wq[s2][:, :, 160:192], -1.0, None, ALU.mult)
        P.copy("pool", [wqr[s2]], wqr[s2][:, :, 96:128], [wq[s2]], wq[s2][:, :, 128:160])
        pjc = [0]

        def nextpj():
            b_, a_ = STB[pjc[0] % 3]
            pjc[0] += 1
            return b_, a_
        for tb in range(4):
            pjB, pj = nextpj()
            for kc in range(4):
                P.mm([pjB], pj[:], [wkv[s2], ckvT], wkv[s2][:, kc, 0:128], ckvT[:, kc, tb * 512:(tb + 1) * 512], kc == 0, kc == 3)
            P.copy("dve", [kT[s2]], kT[s2][:, tb * 512:(tb + 1) * 512], [pjB], pj[:])
        for t4 in range(4):
            pjB, pj = nextpj()
            for tt in range(4):
                t = t4 * 4 + tt
                for kc in range(4):
                    P.mm([pjB], pj[:, tt * 128:(tt + 1) * 128], [ckvT, wkv[s2]], ckvT[:, kc, t * 128:(t + 1) * 128], wkv[s2][:, kc, 128:256], kc == 0, kc == 3)
            P.copy("act", [vh[s2]], vh[s2][:, t4 * 4:(t4 + 1) * 4, 0:128], [pjB], pj[:].rearrange("p (a b) -> p a b", a=4))
        for tb in range(4):
            pjB, pj = nextpj()
            for kc in range(6):
                P.mm([pjB], pj[:], [wq[s2], cqnT], wq[s2][:, kc, 0:128], cqnT[:, kc, tb * 512:(tb + 1) * 512], kc == 0, kc == 5)
            P.copy("act", [qn[s2]], qn[s2][:, tb * 512:(tb + 1) * 512], [pjB], pj[:])
            pjB, pj = nextpj()
            for kc in range(6):
                P.mm([pjB], pj[:, :], [wqr[s2], cqnT], wqr[s2][:, kc, :], cqnT[:, kc, tb * 512:(tb + 1) * 512], kc == 0, kc == 5)
            t1 = t1s[tb % 2]
            t2 = t2s[tb % 2]
            P.tt("dve", [t1], t1[:], [pjB, cs], pj[:, :], cs[:, tb * 512:(tb + 1) * 512], ALU.mult)
            P.copy("act", [t2], t2[:], [t1], t1[64:128, :])
            P.tt("pool", [qr[s2]], qr[s2][0:64, tb * 512:(tb + 1) * 512], [t1, t2], t1[0:64, :], t2[:], ALU.add)
        for Qb in range(4):
            Ob, Of = C.O[Qb % 2], C.Of[Qb % 2]
            nkt = 4 * Qb + 4
            pend = []

            def fin(stb, sta, kt, c0, Qb=Qb, Ob=Ob, Of=Of, s2=s2):
                nonlocal ptc
                pt = PT[ptc % len(PT)]
                ptc += 1
                P.act([pt], pt[:, c0:512], [stb], sta[:, c0:512], AF.Exp, scale=SCALE)
                for jj in range(c0 // 128, 4):
                    last = (kt == 4 * Qb + jj)
                    P.mm([Ob], Of[:, jj * 256:jj * 256 + 129], [pt, vh[s2]], pt[:, jj * 128:(jj + 1) * 128], vh[s2][:, kt, :], kt == 0 and jj % 2 == 0, last, skip=True)

            for kt in range(nkt):
                stb, sta = STB[stc % 3]
                stc += 1
                c0 = max(0, kt - 4 * Qb) * 128
                q0 = Qb * 512
                kl = kT[s2][:, kt * 128:(kt + 1) * 128]
                krl = krT[:, kt * 128:(kt + 1) * 128]
                if kt >= 4 * Qb:
                    P.mm([stb], sta[:, c0:c0 + 128], [C.ident, mmask], C.ident[:], mmask[:], True, False)
                    P.mm([stb], sta[:, c0:c0 + 128], [kT[s2], qn[s2]], kl, qn[s2][:, q0 + c0:q0 + c0 + 128], False, False)
                    P.mm([stb], sta[:, c0:c0 + 128], [krT, qr[s2]], krl, qr[s2][:, q0 + c0:q0 + c0 + 128], False, True)
                    c1 = c0 + 128
                else:
                    c1 = c0
                if c1 < 512:
                    P.mm([stb], sta[:, c1:512], [kT[s2], qn[s2]], kl, qn[s2][:, q0 + c1:q0 + 512], True, False)
                    P.mm([stb], sta[:, c1:512], [krT, qr[s2]], krl, qr[s2][:, q0 + c1:q0 + 512], False, True)
                pend.append((stb, sta, kt, c0))
                if len(pend) > 2:
                    fin(*pend.pop(0))
                if kt == 2 and pend_qb:
                    pend_qb.pop(0)()
            while pend:
                fin(*pend.pop(0))
            def finish_qb(Qb=Qb, Ob=Ob, Of=Of, h=h):
                rzb = rz[Qb % 2]
                obb = ob[Qb % 2]
                O4 = Of.rearrange("p (a b) -> p a b", a=4)
                P.recip([rzb], rzb[:], [Ob], O4[:, :, 128])
                P.tt("dve", [obb], obb[:], [Ob, rzb], O4[:, :, 0:128], rzb[:].unsqueeze(2).to_broadcast([128, 4, 128]), ALU.mult)
                trb, trap = C.TR[1], C.TRap[1]
                for jj in range(4):
                    P.tr([trb], trap[:, jj * 128:(jj + 1) * 128], [obb, C.ident], obb[:, jj, :], C.ident[:])
                P.copy("act", [oT], oT[:, h, Qb * 512:(Qb + 1) * 512], [trb], trap[:, :])
            pend_qb.append(finish_qb)
    while pend_qb:
        pend_qb.pop(0)()
    if "l1o" in C.dbg_want:
        dbg_dump(C, "oT1", oT, oT[:], [128, 16, S], BF16)
    P.release(m_after_oT)
    WO = P.sb("WO1", [128, 16, D], BF16)
    wov = I["w_out_o"].rearrange("(kc p) c -> p kc c", p=128)
    for q4 in range(4):
        load_w_cast(C, WO, WO[:, q4 * 4:(q4 + 1) * 4, :], wov[:, q4 * 4:(q4 + 1) * 4, :])
    xst = [P.sb(f"xo1{i}", [128, D], F32) for i in range(2)]
    for t in range(NT):
        xa = xst[t % 2]
        load_x_tile(C, xa, t)
        for db in range(4):
            pb = B[db % 2]
            for kc in range(16):
                P.mm([pb], pb[:], [oT, WO], oT[:, kc, t * 128:(t + 1) * 128], WO[:, kc, db * 512:(db + 1) * 512], kc == 0, kc == 15)
            P.tt("dve", [xa], xa[:, db * 512:(db + 1) * 512], [pb, xa], pb[:], xa[:, db * 512:(db + 1) * 512], ALU.add)
        P.dma("sp", C.xs[t * 128:(t + 1) * 128, :], xa[:], [xa], [C.xsB[t]], xa)
    C.x_src = C.xs
    C.x_srcB = C.xsB
    P.release(m_all)


_CACHE = {}


def _prep_inputs(inputs, used=None):
    tabs = _host_tables(np.asarray(inputs["rel_bias"], np.float32))
    shared = dict(tabs)
    sq = lambda k: np.ascontiguousarray(np.asarray(inputs[k], np.float32)[0])
    for k in ("w_in_e", "cmp_pos_k", "cmp_pos_v", "cmp_k_w1", "cmp_k_w2", "cmp_v_w1", "cmp_v_w2", "w_out_e",
              "w_in_o", "w_q_up", "w_kv_up", "w_out_o"):
        shared[k] = sq(k)
    for k in ("norm_mix_e", "sinks", "norm_mix_o", "q_norm", "kv_norm"):
        shared[k] = np.ascontiguousarray(np.asarray(inputs[k], np.float32).reshape(1, -1))
    shared["norm_mlp"] = np.ascontiguousarray(np.asarray(inputs["norm_mlp"], np.float32))
    shared["norm_final"] = np.ascontiguousarray(np.asarray(inputs["norm_final"], np.float32).reshape(1, -1))
    for l in range(2):
        shared[f"w_up{l}"] = np.ascontiguousarray(np.asarray(inputs["w_up"], np.float32)[l])
        shared[f"w_down{l}"] = np.ascontiguousarray(np.asarray(inputs["w_down"], np.float32)[l])
    x = np.asarray(inputs["x"], np.float32)
    in_maps = []
    for c in range(NCORES):
        m = dict(shared)
        m["x"] = np.ascontiguousarray(x[c])
        if used is not None:
            m = {k: v for k, v in m.items() if k in used}
        in_maps.append(m)
    return in_maps


def kernel(**inputs):
    if "nc" not in _CACHE:
        _CACHE["nc"], _CACHE["P"] = build_program()
    nc = _CACHE["nc"]
    in_maps = _prep_inputs(inputs, _CACHE["P"].used_inputs)
    res = run_bass_kernel_spmd(nc, in_maps, core_ids=list(range(NCORES)))
    return np.stack([np.asarray(r["out"], np.float32) for r in res.results], axis=0)
```
